# Optimizing a Trainium2 kernel written in Bass

```python
import jax, jax.numpy as jnp
from jax import lax
import numpy as np

D_MODEL = 1024
BATCH = 4
SEQ = 8192
DEPTH = 1

NSA_HEADS = 8
NSA_KV_GROUPS = 2
NSA_HEAD_DIM = 64
CMP_LEN = 32
CMP_STRIDE = 16
CMP_HIDDEN = 256
SEL_BLOCK = 64
SEL_TOPK = 16
WINDOW = 512
MLA_HEADS = 8
MLA_NOPE = 64
MLA_ROPE = 32
MLA_V = 64
MLA_Q_RANK = 384
MLA_KV_RANK = 256
ROPE_THETA = 10000.0
D_FF = 4 * D_MODEL
Q_BLOCK = 128
EPS = 1e-6
FORCE_BONUS = 1000.0

NSA_HG = NSA_HEADS // NSA_KV_GROUPS
NSA_Q_DIM = NSA_HEADS * NSA_HEAD_DIM
NSA_KV_DIM = NSA_KV_GROUPS * NSA_HEAD_DIM
IN_SPLITS = (NSA_Q_DIM, NSA_KV_DIM, NSA_KV_DIM, NSA_KV_DIM, NSA_KV_DIM, NSA_KV_DIM, NSA_KV_DIM,
             3 * NSA_HEADS, MLA_Q_RANK, MLA_KV_RANK, MLA_ROPE, D_MODEL, D_MODEL)
IN_DIM = sum(IN_SPLITS)

kernel_name = 'hybrid_nsa_mla_gated_block'


def rmsnorm(x, g):
    xf = x.astype(jnp.float32)
    y = xf * lax.rsqrt(jnp.mean(xf * xf, axis=-1, keepdims=True) + EPS)
    return (y * g.astype(jnp.float32)).astype(x.dtype)


def modulate(h, shift, scale):
    return h * (1.0 + scale[:, None, :]) + shift[:, None, :]


def masked_softmax(s, mask):
    s = jnp.where(mask, s, -jnp.inf)
    m = jnp.max(s, axis=-1, keepdims=True)
    m = jnp.where(jnp.isfinite(m), m, 0.0)
    e = jnp.exp(s - m)
    return e / jnp.maximum(jnp.sum(e, axis=-1, keepdims=True), 1e-30)


def alibi_slopes(n):
    return 2.0 ** (-8.0 * jnp.arange(1, n + 1, dtype=jnp.float32) / n)


def rope(x, cos, sin):
    half = x.shape[-1] // 2
    x1, x2 = x[..., :half], x[..., half:]
    return jnp.concatenate([x1 * cos - x2 * sin, x2 * cos + x1 * sin], axis=-1).astype(x.dtype)


def compress(kv, pe, w1, w2):
    B, S, G, dh = kv.shape
    r = CMP_LEN // CMP_STRIDE
    n_chunks = S // CMP_STRIDE
    nc = n_chunks - r + 1
    chunks = kv.reshape(B, n_chunks, CMP_STRIDE, G, dh)
    blocks = jnp.concatenate([chunks[:, i:i + nc] for i in range(r)], axis=2)
    blocks = blocks + pe[None, None, :, None, :]
    flat = blocks.transpose(0, 3, 1, 2, 4).reshape(B, G, nc, CMP_LEN * dh)
    return jax.nn.silu(flat @ w1) @ w2


def nsa_attention(q, kc, vc, ks, vs, kw, vw, gates):
    B, G, Hg, S, dh = q.shape
    nc = kc.shape[2]
    ns = S // SEL_BLOCK
    n_sel = min(SEL_TOPK, ns)
    n_key_sel = n_sel * SEL_BLOCK
    scale = dh ** -0.5
    slopes = alibi_slopes(G * Hg).reshape(1, G, Hg, 1, 1)
    c_start = jnp.arange(nc) * CMP_STRIDE
    c_end = c_start + CMP_LEN - 1
    s_start = jnp.arange(ns) * SEL_BLOCK
    overlap = ((c_start[:, None] < s_start[None, :] + SEL_BLOCK)
               & (c_end[:, None] >= s_start[None, :])).astype(jnp.float32)
    ks_blocks = ks.reshape(B, G, ns, SEL_BLOCK, dh)
    vs_blocks = vs.reshape(B, G, ns, SEL_BLOCK, dh)
    kw_pad = jnp.pad(kw, ((0, 0), (0, 0), (WINDOW, 0), (0, 0)))
    vw_pad = jnp.pad(vw, ((0, 0), (0, 0), (WINDOW, 0), (0, 0)))
    blk = jnp.arange(ns)
    in_blk = jnp.arange(SEL_BLOCK)
    win_off = jnp.arange(Q_BLOCK + WINDOW) - WINDOW
    gather = jax.vmap(jax.vmap(lambda kb, ix: kb[ix]))

    def block(qb):
        t0 = qb * Q_BLOCK
        t = t0 + jnp.arange(Q_BLOCK)
        qq = lax.dynamic_slice_in_dim(q, t0, Q_BLOCK, axis=3)
        g = lax.dynamic_slice_in_dim(gates, t0, Q_BLOCK, axis=4)
        dist_c = (t[:, None] - c_end[None, :]).astype(jnp.float32)
        s = jnp.einsum('bghqd,bgkd->bghqk', qq, kc).astype(jnp.float32) * scale - slopes * dist_c
        p_cmp = masked_softmax(s, dist_c >= 0)
        o_cmp = jnp.einsum('bghqk,bgkd->bghqd', p_cmp.astype(vc.dtype), vc)
        p_slc = jnp.einsum('bghqk,ks->bgqs', p_cmp, overlap)
        cur = t // SEL_BLOCK
        valid = blk[None, :] <= cur[:, None]
        forced = (blk[None, :] == 0) | (blk[None, :] >= cur[:, None] - 1)
        score = jnp.where(valid, p_slc + FORCE_BONUS * forced.astype(jnp.float32), -jnp.inf)
        _, idx = lax.top_k(score, n_sel)
        spos = idx[..., None] * SEL_BLOCK + in_blk
        smask = (spos <= t[:, None, None]).reshape(B, G, Q_BLOCK, n_key_sel)
        dist_s = (t[:, None, None] - spos).reshape(B, G, Q_BLOCK, n_key_sel).astype(jnp.float32)
        k_sel = gather(ks_blocks, idx).reshape(B, G, Q_BLOCK, n_key_sel, dh)
        v_sel = gather(vs_blocks, idx).reshape(B, G, Q_BLOCK, n_key_sel, dh)
        s = jnp.einsum('bghqd,bgqkd->bghqk', qq, k_sel).astype(jnp.float32) * scale - slopes * dist_s[:, :, None]
        p = masked_softmax(s, smask[:, :, None])
        o_sel = jnp.einsum('bghqk,bgqkd->bghqd', p.astype(vs.dtype), v_sel)
        k_win = lax.dynamic_slice_in_dim(kw_pad, t0, Q_BLOCK + WINDOW, axis=2)
        v_win = lax.dynamic_slice_in_dim(vw_pad, t0, Q_BLOCK + WINDOW, axis=2)
        wpos = t0 + win_off
        dist_w = t[:, None] - wpos[None, :]
        wmask = (dist_w >= 0) & (dist_w < WINDOW) & (wpos[None, :] >= 0)
        s = jnp.einsum('bghqd,bgkd->bghqk', qq, k_win).astype(jnp.float32) * scale - slopes * dist_w.astype(jnp.float32)
        p = masked_softmax(s, wmask)
        o_win = jnp.einsum('bghqk,bgkd->bghqd', p.astype(vw.dtype), v_win)
        o = g[0] * o_cmp + g[1] * o_sel + g[2] * o_win
        return o.transpose(0, 3, 1, 2, 4).reshape(B, Q_BLOCK, G * Hg * dh)

    out = lax.map(block, jnp.arange(S // Q_BLOCK))
    return out.transpose(1, 0, 2, 3).reshape(B, S, G * Hg * dh)


def mla_attention(q, k, v):
    B, H, S, dqk = q.shape
    dv = v.shape[-1]
    scale = dqk ** -0.5
    kpos = jnp.arange(S)

    def block(qb):
        t0 = qb * Q_BLOCK
        t = t0 + jnp.arange(Q_BLOCK)
        qq = lax.dynamic_slice_in_dim(q, t0, Q_BLOCK, axis=2)
        s = jnp.einsum('bhqd,bhkd->bhqk', qq, k).astype(jnp.float32) * scale
        p = masked_softmax(s, kpos[None, :] <= t[:, None])
        o = jnp.einsum('bhqk,bhkd->bhqd', p.astype(v.dtype), v)
        return o.transpose(0, 2, 1, 3).reshape(B, Q_BLOCK, H * dv)

    out = lax.map(block, jnp.arange(S // Q_BLOCK))
    return out.transpose(1, 0, 2, 3).reshape(B, S, H * dv)


def setup_inputs(seed: int = 0) -> dict:
    key = jax.random.key(seed)
    ks = jax.random.split(key, 24)
    f32 = jnp.float32

    def nrm(k, shape, scale):
        return jax.random.normal(k, (DEPTH,) + shape, f32) * scale

    def gain(k, n):
        return 1.0 + 0.05 * jax.random.normal(k, (DEPTH, n), f32)

    dqk = MLA_NOPE + MLA_ROPE
    return {
        'x': jax.random.normal(ks[0], (BATCH, SEQ, D_MODEL), f32),
        'c': jax.random.normal(ks[1], (BATCH, D_MODEL), f32),
        'w_ada': nrm(ks[2], (D_MODEL, 6 * D_MODEL), 0.5 * D_MODEL ** -0.5),
        'b_ada': nrm(ks[3], (6 * D_MODEL,), 0.01),
        'g_mix': gain(ks[4], D_MODEL),
        'w_in': nrm(ks[5], (D_MODEL, IN_DIM), D_MODEL ** -0.5),
        'pe_ck': nrm(ks[6], (CMP_LEN, NSA_HEAD_DIM), 0.1),
        'w_ck1': nrm(ks[7], (CMP_LEN * NSA_HEAD_DIM, CMP_HIDDEN), (CMP_LEN * NSA_HEAD_DIM) ** -0.5),
        'w_ck2': nrm(ks[8], (CMP_HIDDEN, NSA_HEAD_DIM), CMP_HIDDEN ** -0.5),
        'pe_cv': nrm(ks[9], (CMP_LEN, NSA_HEAD_DIM), 0.1),
        'w_cv1': nrm(ks[10], (CMP_LEN * NSA_HEAD_DIM, CMP_HIDDEN), (CMP_LEN * NSA_HEAD_DIM) ** -0.5),
        'w_cv2': nrm(ks[11], (CMP_HIDDEN, NSA_HEAD_DIM), CMP_HIDDEN ** -0.5),
        'g_cq': gain(ks[12], MLA_Q_RANK),
        'w_uq': nrm(ks[13], (MLA_Q_RANK, MLA_HEADS * dqk), MLA_Q_RANK ** -0.5),
        'g_ckv': gain(ks[14], MLA_KV_RANK),
        'w_uk': nrm(ks[15], (MLA_KV_RANK, MLA_HEADS * MLA_NOPE), MLA_KV_RANK ** -0.5),
        'w_uv': nrm(ks[16], (MLA_KV_RANK, MLA_HEADS * MLA_V), MLA_KV_RANK ** -0.5),
        'w_o_nsa': nrm(ks[17], (NSA_Q_DIM, D_MODEL), NSA_Q_DIM ** -0.5),
        'w_o_mla': nrm(ks[18], (MLA_HEADS * MLA_V, D_MODEL), (MLA_HEADS * MLA_V) ** -0.5),
        'w_out': nrm(ks[19], (D_MODEL, D_MODEL), D_MODEL ** -0.5),
        'g_mlp': gain(ks[20], D_MODEL),
        'w_fc1': nrm(ks[21], (D_MODEL, D_FF), D_MODEL ** -0.5),
        'w_fc2': nrm(ks[22], (D_FF, D_MODEL), D_FF ** -0.5),
        'g_final': 1.0 + 0.05 * jax.random.normal(ks[23], (D_MODEL,), f32),
    }


def reference(x, c, w_ada, b_ada, g_mix, w_in, pe_ck, w_ck1, w_ck2, pe_cv, w_cv1, w_cv2,
              g_cq, w_uq, g_ckv, w_uk, w_uv, w_o_nsa, w_o_mla, w_out, g_mlp, w_fc1, w_fc2, g_final):
    B, S, _ = x.shape
    G, Hg, dh = NSA_KV_GROUPS, NSA_HG, NSA_HEAD_DIM
    split_points = np.cumsum(IN_SPLITS)[:-1].tolist()
    half = MLA_ROPE // 2
    pos = jnp.arange(S, dtype=jnp.float32)
    inv_freq = ROPE_THETA ** (-jnp.arange(half, dtype=jnp.float32) / half)
    ang = pos[:, None] * inv_freq[None, :]
    cos, sin = jnp.cos(ang), jnp.sin(ang)

    def kv_groups(t):
        return t.reshape(B, S, G, dh)

    for layer in range(DEPTH):
        mod = c @ w_ada[layer] + b_ada[layer]
        sh_a, sc_a, gt_a, sh_m, sc_m, gt_m = jnp.split(mod, 6, axis=-1)

        h = modulate(rmsnorm(x, g_mix[layer]), sh_a, sc_a)
        z = h @ w_in[layer]
        (zq, zkc, zvc, zks, zvs, zkw, zvw, zg, zqd, zkvd, zkr, zga, zgb) = jnp.split(z, split_points, axis=-1)

        q_nsa = zq.reshape(B, S, G, Hg, dh).transpose(0, 2, 3, 1, 4)
        kc = compress(kv_groups(zkc), pe_ck[layer], w_ck1[layer], w_ck2[layer])
        vc = compress(kv_groups(zvc), pe_cv[layer], w_cv1[layer], w_cv2[layer])
        k_s = kv_groups(zks).transpose(0, 2, 1, 3)
        v_s = kv_groups(zvs).transpose(0, 2, 1, 3)
        k_w = kv_groups(zkw).transpose(0, 2, 1, 3)
        v_w = kv_groups(zvw).transpose(0, 2, 1, 3)
        nsa_gates = jax.nn.sigmoid(zg).reshape(B, S, 3, G, Hg).transpose(2, 0, 3, 4, 1)[..., None]
        o_nsa = nsa_attention(q_nsa, kc, vc, k_s, v_s, k_w, v_w, nsa_gates)

        cq = rmsnorm(zqd, g_cq[layer])
        qm = (cq @ w_uq[layer]).reshape(B, S, MLA_HEADS, MLA_NOPE + MLA_ROPE)
        q_rot = rope(qm[..., MLA_NOPE:], cos[:, None, :], sin[:, None, :])
        q_mla = jnp.concatenate([qm[..., :MLA_NOPE], q_rot], axis=-1)
        ckv = rmsnorm(zkvd, g_ckv[layer])
        k_nope = (ckv @ w_uk[layer]).reshape(B, S, MLA_HEADS, MLA_NOPE)
        v_mla = (ckv @ w_uv[layer]).reshape(B, S, MLA_HEADS, MLA_V)
        k_rot = rope(zkr, cos, sin)
        k_mla = jnp.concatenate(
            [k_nope, jnp.broadcast_to(k_rot[:, :, None, :], (B, S, MLA_HEADS, MLA_ROPE))], axis=-1)
        o_mla = mla_attention(q_mla.transpose(0, 2, 1, 3), k_mla.transpose(0, 2, 1, 3),
                              v_mla.transpose(0, 2, 1, 3))

        y = jax.nn.sigmoid(zga) * (o_nsa @ w_o_nsa[layer]) + jax.nn.sigmoid(zgb) * (o_mla @ w_o_mla[layer])
        x = x + gt_a[:, None, :] * (y @ w_out[layer])

        h = modulate(rmsnorm(x, g_mlp[layer]), sh_m, sc_m)
        x = x + gt_m[:, None, :] * (jnp.square(jax.nn.relu(h @ w_fc1[layer])) @ w_fc2[layer])

    return rmsnorm(x, g_final)
```

```python
import numpy as np
import ml_dtypes
from contextlib import ExitStack
import concourse.bass as bass
import concourse.mybir as mybir
from concourse.bass_utils import run_bass_kernel_spmd

F32 = mybir.dt.float32
BF16 = mybir.dt.bfloat16
AF = mybir.ActivationFunctionType
ALU = mybir.AluOpType
NPBF = ml_dtypes.bfloat16

D = 1024
S = 8192
NT = 64
NO = 32
NEG = -30000.0
EPS = 1e-6
GEN = 8192
EVAC_ACT_ONLY = True
NDSEM = 12
SC_NSA = 0.125
SC_MLA = 96.0 ** -0.5
C_Q, C_KC, C_VC, C_KS, C_VS, C_KW, C_VW, C_G, C_QD, C_KVD, C_KR, C_GA, C_GB = (
    0, 512, 640, 768, 896, 1024, 1152, 1280, 1304, 1688, 1944, 1976, 3000)


class Prog:
    ENGS = ('pe', 'act', 'dve', 'pool', 'sp')

    def __init__(self, nc):
        self.nc = nc
        self.ops = {e: [] for e in self.ENGS}
        self.cnt = {e: 0 for e in self.ENGS}
        self.lastw = {}
        self.readers = {}
        self.dcnt = {}
        self.dnext = {e: 0 for e in self.ENGS}
        self.floor = {}

    def _deps(self, r, w):
        deps = dict(self.floor)

        def add(tok):
            if tok is None:
                return
            k, v = tok
            if deps.get(k, 0) < v:
                deps[k] = v
        for x in r:
            add(self.lastw.get(x))
        for x in w:
            add(self.lastw.get(x))
            for t in self.readers.get(x, ()):
                add(t)
        return deps

    def _commit(self, tok, r, w):
        for x in r:
            self.readers.setdefault(x, []).append(tok)
        for x in w:
            self.lastw[x] = tok
            self.readers[x] = []

    def barrier(self):
        fl = {}
        for e in self.ENGS:
            c = self.cnt[e]
            if c > 0:
                fl[(e, (c - 1) // GEN)] = (c - 1) % GEN + 1
        for key, n in self.dcnt.items():
            fl[key] = 16 * n
        self.floor = fl
        self.lastw = {}
        self.readers = {}

    def op(self, eng, fn, r=(), w=()):
        deps = self._deps(r, w)
        idx = self.cnt[eng]
        self.cnt[eng] += 1
        tok = ((eng, idx // GEN), idx % GEN + 1)
        if eng == 'pe':
            deps = {k: v for k, v in deps.items() if k[0] != 'pe'}
        self.ops[eng].append(('c', fn, deps, tok))
        self._commit(tok, r, w)
        return tok

    def dma(self, q, fn, r=(), w=()):
        deps = self._deps(r, w)
        slot = self.dnext[q] % NDSEM
        self.dnext[q] += 1
        key = ('dma_' + q, slot)
        n = self.dcnt.get(key, 0)
        if n > 0 and deps.get(key, 0) < 16 * n:
            deps[key] = 16 * n
        self.dcnt[key] = n + 1
        tok = (key, 16 * (n + 1))
        self.ops[q].append(('d', fn, deps, tok))
        self._commit(tok, r, w)
        return tok

    def emit(self):
        nc = self.nc
        with ExitStack() as es:
            sems = {}
            for e in self.ENGS:
                for g in range((self.cnt[e] + GEN - 1) // GEN):
                    sems[(e, g)] = es.enter_context(nc.semaphore(f"s_{e}_{g}"))
            for key in self.dcnt:
                sems[key] = es.enter_context(nc.semaphore(f"s_{key[0]}_{key[1]}"))
            block = es.enter_context(nc.Block())
            engobj = {'pe': 'tensor', 'act': 'scalar', 'dve': 'vector', 'pool': 'gpsimd', 'sp': 'sync'}

            def make(ename):
                def body(e):
                    waited = {}
                    for kind, fn, deps, tok in self.ops[ename]:
                        for k, v in deps.items():
                            if waited.get(k, 0) < v:
                                e.wait_ge(sems[k], v)
                                waited[k] = v
                        ins = fn(e)
                        ins.then_inc(sems[tok[0]], 1 if kind == 'c' else 16)
                    if ename == 'sp':
                        fin = {}
                        for e2 in self.ENGS:
                            c = self.cnt[e2]
                            if c > 0:
                                fin[(e2, (c - 1) // GEN)] = (c - 1) % GEN + 1
                        for key, n in self.dcnt.items():
                            fin[key] = 16 * n
                        for k, v in fin.items():
                            if waited.get(k, 0) < v:
                                e.wait_ge(sems[k], v)
                return body
            for ename in self.ENGS:
                getattr(block, engobj[ename])(make(ename))


def host_consts(p):
    shift = 128 * (1 - p)
    c = {}
    c['ident'] = np.eye(128, dtype=np.float32)
    c['identb'] = np.eye(128, dtype=np.float32).astype(NPBF)
    k = np.arange(128)[:, None]
    q = np.arange(128)[None, :]
    c['tri'] = np.where(k > q, NEG, 0.0).astype(NPBF)
    c['band'] = np.where(k <= q, NEG, 0.0).astype(NPBF)
    half = 16
    inv_freq = (10000.0 ** (-np.arange(half, dtype=np.float32) / half)).astype(np.float32)
    L = np.arange(S)
    gpos = (L - shift).astype(np.float32)
    ang = (gpos[:, None] * inv_freq[None, :]).astype(np.float32)
    cos2 = np.concatenate([np.cos(ang), np.cos(ang)], axis=1).T.astype(np.float32)
    sin2 = np.concatenate([np.sin(ang), np.sin(ang)], axis=1).T.astype(np.float32)
    c['cosk'] = np.ascontiguousarray(cos2)
    c['sink'] = np.ascontiguousarray(sin2)
    own = (np.arange(NO)[:, None] * 256 + 128 + np.arange(128)[None, :]).reshape(-1)
    c['cosq'] = np.ascontiguousarray(cos2[:, own])
    c['sinq'] = np.ascontiguousarray(sin2[:, own])
    ka = np.zeros((5, S), np.float32)
    ka[0] = 1.0
    ka[1] = 1.0
    ka[2] = 128.0 * (L // 128)
    ka[3] = L % 128
    ka[4] = (L < shift).astype(np.float32)
    c['KA'] = ka.astype(NPBF)
    li = np.arange(512)
    cend = 16 * li + 31
    kca = np.zeros((5, 512), np.float32)
    kca[0] = 1.0
    kca[1] = 1.0
    kca[2] = 128.0 * (cend // 128)
    kca[3] = cend % 128
    kca[4] = (16 * li < shift).astype(np.float32)
    c['KCA'] = kca.astype(NPBF)
    slopes = 2.0 ** (-8.0 * np.arange(1, 9) / 8.0)
    qa = np.zeros((8, 5, NO * 128), np.float32)
    for h in range(8):
        cc = slopes[h] / SC_NSA
        qa[h, 0] = -cc * 128.0 * (own // 128)
        qa[h, 1] = -cc * (own % 128)
        qa[h, 2] = cc
        qa[h, 3] = cc
        qa[h, 4] = NEG
    c['QAh'] = qa.astype(NPBF)
    e32 = np.zeros((32, 16, 128), np.float32)
    for v in range(16):
        e32[2 * v, v, 0:64] = 1.0
        e32[2 * v + 1, v, 64:128] = 1.0

    ef = np.zeros((59, S), np.float32)
    lbt = (L // 64)
    for j in range(58):
        ef[j] = (lbt % 58 == j)
    c['EF'] = ef.astype(NPBF)
    cm = np.zeros((128, 8, 128), np.float32)
    for mi in range(8):
        m = 2 * mi + 1
        delta = 128 * m
        valid = (16 * k + 31) <= (delta + q)
        cm[:, mi, :] = np.where(valid, 0.0, NEG)
    c['cmask'] = cm.astype(NPBF)
    lia = np.arange(512)[:, None]
    lb = np.arange(128)[None, :]
    ov = ((16 * lia < 64 * lb + 64) & (16 * lia + 31 >= 64 * lb)).astype(np.float32)
    c['OVL'] = np.ascontiguousarray(ov.reshape(4, 128, 128).transpose(1, 0, 2)).astype(NPBF)
    bon = np.zeros((NO, 128, 128), np.float32)
    blk0 = 2 * (1 - p)
    for i in range(NO):
        t = 128 * (2 * i + 1) + np.arange(128)[:, None]
        cur = t // 64
        lbb = np.arange(128)[None, :]
        valid = (lbb <= cur) & (lbb >= blk0)
        forced = (lbb == blk0) | (lbb >= cur - 1)
        bon[i] = np.where(valid, np.where(forced, 1000.0, 0.0), np.where(lbb > cur, -1e9, -2e9))
    c['bonus'] = bon
    return c


CONST_SHAPES = {
    'ident': ([128, 128], F32), 'identb': ([128, 128], BF16), 'tri': ([128, 128], BF16), 'band': ([128, 128], BF16),
    'cosk': ([32, S], F32), 'sink': ([32, S], F32), 'cosq': ([32, NO * 128], F32), 'sinq': ([32, NO * 128], F32),
    'KA': ([5, S], BF16), 'KCA': ([5, 512], BF16), 'QAh': ([8, 5, NO * 128], BF16), 'EF': ([59, S], BF16),
    'cmask': ([128, 8, 128], BF16), 'OVL': ([128, 4, 128], BF16), 'bonus': ([NO, 128, 128], F32),
}
IN_SHAPES = {
    'xl': [S, D], 'xo': [NO * 128, D], 'c_l': [128, 8], 'w_ada': [D, 6 * D], 'bada_l': [128, 48],
    'gmix_l': [128, 8], 'gmlp_l': [128, 8], 'gcq_l': [128, 3], 'gckv_l': [128, 2], 'gfin_l': [128, 8],
    'w_in': [D, 4024], 'peck_t': [64, 32], 'w_ck1': [2048, 256], 'w_ck2': [256, 64],
    'pecv_t': [64, 32], 'w_cv1': [2048, 256], 'w_cv2': [256, 64],
    'w_uq': [384, 768], 'w_uk': [256, 512], 'w_uv': [256, 512], 'w_o_nsa': [512, D], 'w_o_mla': [512, D],
    'w_out': [D, D], 'w_fc1': [D, 4 * D], 'w_fc2': [4 * D, D],
}


class StopBuild(Exception):
    pass


def build_program(debug=None, stop=None):
    try:
        return _build_program(debug, stop)
    except StopBuild as ex:
        return ex.args[0]


def _build_program(debug=None, stop=None):
    nc = bass.Bass("TRN2", target_bir_lowering=False)
    I = {}
    for name, shp in IN_SHAPES.items():
        I[name] = nc.dram_tensor(name, shp, F32, kind="ExternalInput").ap()
    for name, (shp, dt) in CONST_SHAPES.items():
        I[name] = nc.dram_tensor(name, shp, dt, kind="ExternalInput").ap()
    out = nc.dram_tensor("out", [NO * 128, D], F32, kind="ExternalOutput").ap()
    x1s = nc.dram_tensor("x1s", [NO * 128, D], F32, kind="Internal").ap()
    dbg = {}
    if debug:
        for name, shp in debug.items():
            dbg[name] = nc.dram_tensor("dbg_" + name, shp, F32, kind="ExternalOutput").ap()

    P = Prog(nc)
    rr = {'ev': 0}

    def E(meth, **kw):
        return lambda e: getattr(e, meth)(**kw)

    def SB(name, shape, dt=F32):
        return nc.sbuf_tensor("sb_" + name, shape, dt)

    def kparts(ap, p=128):
        return ap.rearrange("(k p) n -> p k n", p=p)

    def checkpoint(name):
        if stop == name:
            raise StopBuild(nc)

    with ExitStack() as G:
        G.callback(P.emit)

        def sbg(name, shape, dt=F32):
            return G.enter_context(SB(name, shape, dt))
        psT = G.enter_context(nc.psum_tensor("psT", [128, 1024], F32))
        psZ = [G.enter_context(nc.psum_tensor(f"psZ{i}", [128, 512], F32)) for i in range(4)]
        psO = G.enter_context(nc.psum_tensor("psO", [128, 512], F32))
        psB = G.enter_context(nc.psum_tensor("psB", [128, 1024], BF16))
        ident = sbg("ident", [128, 128]); identb = sbg("identb", [128, 128], BF16)
        tri = sbg("tri", [128, 128], BF16); band = sbg("band", [128, 128], BF16)
        onesf = sbg("onesf", [128, 128])
        epsb = sbg("epsb", [128, 1])
        modT = sbg("modT", [128, 48])
        gsA = sbg("gsA", [128, 8]); gsM = sbg("gsM", [128, 8])
        gl = sbg("gl", [128, 8 + 8 + 3 + 2 + 8])
        ssr = sbg("ssr", [128, 8])
        junk = sbg("junk", [128, 1024], BF16)
        gbc = sbg("gbc", [128, 1024])
        dg = sbg("dg", [128, 128])
        OS = ExitStack()
        oTn = OS.enter_context(SB("oTn", [128, 4, NO * 128], BF16))

        def dma(out_ap, in_ap, r=(), w=(), q='sp'):
            return P.dma(q, lambda e: e.dma_start(out=out_ap, in_=in_ap), r=r, w=w)

        def mm(out_ap, lhsT, rhs, start, stop, r=(), w=(), skip=False):
            if skip:
                return P.op('pe', lambda e: e.matmul(out_ap, lhsT=lhsT, rhs=rhs, start=start, stop=stop,
                                                     skip_group_check=True), r=r, w=w)
            return P.op('pe', lambda e: e.matmul(out_ap, lhsT=lhsT, rhs=rhs, start=start, stop=stop), r=r, w=w)

        def act(out_ap, in_ap, func, r=(), w=(), **kw):
            return P.op('act', lambda e: e.activation(out=out_ap, in_=in_ap, func=func, **kw), r=r, w=w)

        def evac(out_ap, in_ap, r=(), w=()):
            rr['ev'] += 1
            if EVAC_ACT_ONLY or rr['ev'] % 2:
                return P.op('act', lambda e: e.activation(out=out_ap, in_=in_ap, func=AF.Copy), r=r, w=w)
            return P.op('dve', lambda e: e.tensor_scalar(out=out_ap, in0=in_ap, scalar1=1.0, scalar2=None, op0=ALU.mult), r=r, w=w)

        def ts(eng, out_ap, in0, s1, s2, op0, op1=None, r=(), w=()):
            if op1 is None:
                return P.op(eng, lambda e: e.tensor_scalar(out=out_ap, in0=in0, scalar1=s1, scalar2=None, op0=op0), r=r, w=w)
            return P.op(eng, lambda e: e.tensor_scalar(out=out_ap, in0=in0, scalar1=s1, scalar2=s2, op0=op0, op1=op1), r=r, w=w)

        def tt(eng, out_ap, in0, in1, op, r=(), w=()):
            return P.op(eng, lambda e: e.tensor_tensor(out=out_ap, in0=in0, in1=in1, op=op), r=r, w=w)

        def stt(eng, out_ap, in0, scalar, in1, op0, op1, r=(), w=()):
            return P.op(eng, lambda e: e.scalar_tensor_tensor(out=out_ap, in0=in0, scalar=scalar, in1=in1, op0=op0, op1=op1), r=r, w=w)

        def memset(eng, ap, val, w=()):
            return P.op(eng, lambda e: e.memset(ap, val), w=w)

        def dump(name, ap, r):
            if name in dbg:
                dma(dbg[name], ap, r=r, w=['dbg_' + name], q='pool')

        dma(ident[:], I['ident'][:, :], w=['ident'])
        dma(identb[:], I['identb'][:, :], w=['identb'])
        dma(tri[:], I['tri'][:, :], w=['tri'])
        dma(band[:], I['band'][:, :], w=['band'])
        dma(gl[:, 0:8], I['gmix_l'][:, :], w=['gl'])
        dma(gl[:, 8:16], I['gmlp_l'][:, :], w=['gl'])
        dma(gl[:, 16:19], I['gcq_l'][:, :], w=['gl'])
        dma(gl[:, 19:21], I['gckv_l'][:, :], w=['gl'])
        dma(gl[:, 21:29], I['gfin_l'][:, :], w=['gl'])
        memset('pool', onesf[:], 1.0, w=['onesf'])
        memset('pool', epsb[:], EPS, w=['epsb'])

        with ExitStack() as es:
            wad = [es.enter_context(SB(f"wad{i}", [128, 8, 512], F32)) for i in range(2)]
            cT = es.enter_context(SB("cT", [128, 8], F32))
            bl = es.enter_context(SB("bl", [128, 48], F32))
            dma(cT[:], I['c_l'][:, :], w=['cT'])
            dma(bl[:], I['bada_l'][:, :], w=['bl'])
            wv = kparts(I['w_ada'])
            for piece in range(12):
                buf = wad[piece % 2]
                bn = f"wad{piece % 2}"
                dma(buf[:], wv[:, :, piece * 512:(piece + 1) * 512], w=[bn])
                for jj in range(4):
                    j = piece * 4 + jj
                    for k in range(8):
                        mm(psZ[0][:, j:j + 1], buf[:, k, jj * 128:(jj + 1) * 128], cT[:, k:k + 1],
                           k == 0, k == 7, r=[bn, 'cT'], w=['psZ0'])
            tt('dve', modT[:], psZ[0][:, 0:48], bl[:], ALU.add, r=['psZ0', 'bl'], w=['modT'])
            stt('dve', gsA[:], modT[:, 8:16], 1.0, gl[:, 0:8], ALU.add, ALU.mult, r=['modT', 'gl'], w=['gsA'])
            stt('dve', gsM[:], modT[:, 32:40], 1.0, gl[:, 8:16], ALU.add, ALU.mult, r=['modT', 'gl'], w=['gsM'])
            dump('modT', modT[:], ['modT'])
        P.barrier()
        checkpoint('p0')
        shA = modT[:, 0:8]
        shM = modT[:, 24:32]

        def make_hT(xin, xin_res, xn, xn_res, hT, hT_res, gs, sh, slot):
            hT_pre(xin, xin_res, xn, xn_res, slot)
            hT_post(hT, hT_res, gs, sh)

        def hT_pre(xin, xin_res, xn, xn_res, slot):
            act(junk[:], xin[:], AF.Square, r=[xin_res], w=['junk', f'ss{slot}'], accum_out=ssr[:, slot:slot + 1])
            act(ssr[:, slot + 2:slot + 3], ssr[:, slot:slot + 1], AF.Sqrt, r=[f'ss{slot}', 'epsb'], w=[f'sq{slot}'],
                scale=1.0 / D, bias=epsb[:, 0:1])
            P.op('dve', E('reciprocal', out=ssr[:, slot + 4:slot + 5], in_=ssr[:, slot + 2:slot + 3]),
                 r=[f'sq{slot}'], w=[f'rs{slot}'])
            rstd = ssr[:, slot + 4:slot + 5]
            act(xn[:, 0:512], xin[:, 0:512], AF.Identity, r=[xin_res, f'rs{slot}'], w=[xn_res + 'a'], scale=rstd)
            ts('dve', xn[:, 512:1024], xin[:, 512:1024], rstd, None, ALU.mult, r=[xin_res, f'rs{slot}'], w=[xn_res + 'b'])
            for k in range(8):
                hf = 'a' if k < 4 else 'b'
                P.op('pe', E('transpose', out=psT[:, k * 128:(k + 1) * 128], in_=xn[:, k * 128:(k + 1) * 128],
                                                      identity=ident[:]),
                     r=[xn_res + hf, 'ident'], w=['psT' + hf])

        def hT_post(hT, hT_res, gs, sh):
            for k in range(8):
                hf = 'a' if k < 4 else 'b'
                if k % 2 == 0:
                    ts('dve', hT[:, k, :], psT[:, k * 128:(k + 1) * 128], gs[:, k:k + 1], sh[:, k:k + 1], ALU.mult, ALU.add,
                       r=['psT' + hf, 'gs', 'modT'], w=[hT_res])
                else:
                    act(hT[:, k, :], psT[:, k * 128:(k + 1) * 128], AF.Identity, r=['psT' + hf, 'gs', 'modT'], w=[hT_res],
                        scale=gs[:, k:k + 1], bias=sh[:, k:k + 1])

        dq = []
        LA = 2

        def defer(fn):
            dq.append(fn)
            while len(dq) > LA:
                dq.pop(0)()

        def flush():
            while dq:
                dq.pop(0)()

        zring = {'i': 0, 'n': 4}

        def nextZ():
            zring['i'] = (zring['i'] + 1) % zring['n']
            return psZ[zring['i']], f"psZ{zring['i']}"

        for g in range(2):
            with ExitStack() as NS:
                def sbn(name, shape, dt=F32):
                    return NS.enter_context(SB(f"{name}_g{g}", shape, dt))
                zqT = sbn("zqT", [128, 2, NO * 128], BF16)
                ksA = sbn("ksA", [128, S], BF16)
                kwA = sbn("kwA", [128, S], BF16)
                vsA = sbn("vsA", [128, NT, 66], BF16)
                vwA = sbn("vwA", [128, NT, 66], BF16)
                sg = sbn("sg", [128, NO, 24])
                kcA = sbn("kcA", [128, 512], BF16)
                VCA = sbn("VCA", [128, 4, 194], BF16)
                memset('pool', vsA[:, :, 64:66], 1.0, w=['vsA'])
                memset('pool', vwA[:, :, 64:66], 1.0, w=['vwA'])
                memset('pool', kcA[:], 0.0, w=['kcA'])
                memset('pool', VCA[:, :, 192:194], 1.0, w=['VCA'])
                checkpoint('n_a')
                dma(VCA[:, :, 64:192], I['OVL'][:, :, :], w=['VCA'])
                checkpoint('n_b')
                with ExitStack() as PS:
                    def sbp(name, shape, dt=F32):
                        return PS.enter_context(SB(f"{name}_g{g}", shape, dt))
                    wn = sbp("wn", [128, 8, 664], BF16)
                    xr = [sbp(f"xr{i}", [128, 1024]) for i in range(2)]
                    hTr = [sbp(f"hTr{i}", [128, 8, 128], BF16) for i in range(2)]
                    kcT = sbp("kcT", [128, S], BF16)
                    vcT = sbp("vcT", [128, S], BF16)
                    w1 = sbp("w1", [64, 32, 256], BF16)
                    w2 = sbp("w2", [128, 2, 64], BF16)
                    peT = sbp("peT", [64, 32], BF16)
                    hb = sbp("hb", [128, 2])
                    hid = sbp("hid", [128, 2, 512], BF16)
                    wiv = kparts(I['w_in'])
                    colmap = [(0, C_Q + 256 * g, 256), (256, C_KC + 64 * g, 64), (320, C_VC + 64 * g, 64),
                              (384, C_KS + 64 * g, 64), (448, C_VS + 64 * g, 64), (512, C_KW + 64 * g, 64),
                              (576, C_VW + 64 * g, 64), (640, C_G, 24)]
                    for (d0, s0, n) in colmap:
                        dma(wn[:, :, d0:d0 + n], wiv[:, :, s0:s0 + n], w=['wn'], q='pool')
                    print("sbuf remaining after proj alloc", nc.sbuf_bytes_remaining, flush=True)
                    checkpoint('n_w')
                    for lt in range(NT):
                        if lt == 1:
                            checkpoint('n_t0')
                        if lt == 2:
                            checkpoint('n_t1')
                        sl = lt % 2
                        hT = hTr[sl]
                        hres = f"hT{sl}"

                        def nsa_pre(t2):
                            s2 = t2 % 2
                            xt2 = xr[s2]
                            xres2 = f"xr{s2}"
                            dma(xt2[:], I['xl'][t2 * 128:(t2 + 1) * 128, :], w=[xres2 + 'a', xres2 + 'b', xres2])
                            hT_pre(xt2, xres2, xt2, xres2, s2)
                        if lt == 0:
                            nsa_pre(0)
                        hT_post(hT, hres, gsA, shA)
                        tsl = slice(lt * 128, (lt + 1) * 128)
                        pz, pzn = nextZ()
                        for j, c0 in enumerate((256, 320, 384, 512)):
                            for k in range(8):
                                mm(pz[0:64, j * 128:(j + 1) * 128], wn[:, k, c0:c0 + 64], hT[:, k, :], k == 0, k == 7,
                                   r=['wn', hres], w=[pzn])
                        pv, pvn = nextZ()
                        for j, c0 in enumerate((448, 576)):
                            for k in range(8):
                                mm(pv[:, j * 64:(j + 1) * 64], hT[:, k, :], wn[:, k, c0:c0 + 64], k == 0, k == 7,
                                   r=['wn', hres], w=[pvn])
                        own = (lt % 2 == 1)
                        i = lt // 2
                        if own:
                            for n2 in range(2):
                                for k in range(8):
                                    mm(pv[:, 128 + n2 * 128:256 + n2 * 128], wn[:, k, n2 * 128:(n2 + 1) * 128], hT[:, k, :],
                                       k == 0, k == 7, r=['wn', hres], w=[pvn])
                            for k in range(8):
                                mm(pv[:, 384:408], hT[:, k, :], wn[:, k, 640:664], k == 0, k == 7, r=['wn', hres], w=[pvn])
                        if lt + 1 < NT:
                            nsa_pre(lt + 1)
                        evac(kcT[0:64, tsl], pz[0:64, 0:128], r=[pzn], w=['kcT'])
                        evac(vcT[0:64, tsl], pz[0:64, 128:256], r=[pzn], w=['vcT'])
                        evac(ksA[0:64, tsl], pz[0:64, 256:384], r=[pzn], w=['ksA'])
                        evac(kwA[0:64, tsl], pz[0:64, 384:512], r=[pzn], w=['kwA'])
                        evac(vsA[:, lt, 0:64], pv[:, 0:64], r=[pvn], w=['vsA'])
                        evac(vwA[:, lt, 0:64], pv[:, 64:128], r=[pvn], w=['vwA'])
                        if own:
                            evac(zqT[:, 0, i * 128:(i + 1) * 128], pv[:, 128:256], r=[pvn], w=['zqT'])
                            evac(zqT[:, 1, i * 128:(i + 1) * 128], pv[:, 256:384], r=[pvn], w=['zqT'])
                            act(sg[:, i, :], pv[:, 384:408], AF.Sigmoid, r=[pvn], w=['sg'])
                    checkpoint('n_tiles')
                    dma(ksA[64:69, :], I['KA'][:, :], w=['ksA'])
                    dma(ksA[69:128, :], I['EF'][:, :], w=['ksA'])
                    dma(kwA[64:69, :], I['KA'][:, :], w=['kwA'])
                    dma(kcA[64:69, :], I['KCA'][:, :], w=['kcA'])
                    checkpoint('n_aug')
                    for which in range(2):
                        src = kcT if which == 0 else vcT
                        srcn = 'kcT' if which == 0 else 'vcT'
                        w1d = I['w_ck1'] if which == 0 else I['w_cv1']
                        w2d = I['w_ck2'] if which == 0 else I['w_cv2']
                        ped = I['peck_t'] if which == 0 else I['pecv_t']
                        dma(w1[:], w1d.rearrange("(l d) h -> d l h", d=64), w=['w1'], q='pool')
                        dma(w2[:], kparts(w2d), w=['w2'], q='pool')
                        dma(peT[:], ped[:, :], w=['peT'], q='pool')
                        pz, pzn = nextZ()
                        for hc in range(2):
                            for l in range(32):
                                mm(pz[:, hc:hc + 1], w1[:, l, hc * 128:(hc + 1) * 128], peT[:, l:l + 1], l == 0, l == 31,
                                   r=['w1', 'peT'], w=[pzn])
                        evac(hb[:], pz[:, 0:2], r=[pzn], w=['hb'])
                        memset('pool', hid[:], 0.0, w=['hid'])
                        for hc in range(2):
                            pz, pzn = nextZ()
                            for l in range(32):
                                mm(pz[:, 0:511], w1[:, l, hc * 128:(hc + 1) * 128], src[0:64, l:l + 16 * 510 + 1:16],
                                   l == 0, l == 31, r=['w1', srcn], w=[pzn])
                            act(hid[:, hc, 0:511], pz[:, 0:511], AF.Silu, r=[pzn, 'hb'], w=['hid'], bias=hb[:, hc:hc + 1])
                        if which == 0:
                            pz, pzn = nextZ()
                            for hc in range(2):
                                mm(pz[0:64, 0:511], w2[:, hc, :], hid[:, hc, 0:511], hc == 0, hc == 1, r=['w2', 'hid'], w=[pzn])
                            evac(kcA[0:64, 0:511], pz[0:64, 0:511], r=[pzn], w=['kcA'])
                        else:
                            pz, pzn = nextZ()
                            for ct in range(4):
                                for hc in range(2):
                                    mm(pz[:, ct * 64:(ct + 1) * 64], hid[:, hc, ct * 128:(ct + 1) * 128], w2[:, hc, :],
                                       hc == 0, hc == 1, r=['w2', 'hid'], w=[pzn])
                            evac(VCA[:, :, 0:64], pz[:, 0:256].rearrange("p (a b) -> p a b", b=64), r=[pzn], w=['VCA'])
                    if g == 0:
                        dump('kcA', kcA[0:64, :], ['kcA'])
                        dump('ksA', ksA[0:64, 0:1024], ['ksA'])
                        dump('zqT', zqT[:, 0, 0:512], ['zqT'])
                P.barrier()
                checkpoint(f'nproj{g}')
                with ExitStack() as AS:
                    def sba(name, shape, dt=F32):
                        return AS.enter_context(SB(f"{name}_g{g}", shape, dt))
                    QA = [sba(f"QA{h}", [128, NO * 128], BF16) for h in range(4)]
                    PT = [sba(f"PT{i}", [128, 512], BF16) for i in range(4)]
                    PTw = [sba(f"PTw{i}", [128, 640], BF16) for i in range(3)]
                    Zm = [sba(f"Zm{i}", [128, 128]) for i in range(3)]
                    RqAll = [[sba(f"Rq{p_}_{i}", [128, 512], BF16) for i in range(3)] for p_ in range(2)]
                    ocmpAll = [sba(f"ocmp{p_}", [128, 4, 4, 64]) for p_ in range(2)]
                    ostage = sba("ostage", [128, NO, 256], BF16)
                    cmask = sba("cmask", [128, 8, 128], BF16)
                    bon = [sba(f"bon{i}", [128, 128]) for i in range(2)]
                    pslc = sba("pslc", [128, 128])
                    scb = sba("scb", [128, 128])
                    mrb = sba("mrb", [128, 128])
                    mx = sba("mx", [128, 16])
                    negm = sba("negm", [128, 128])
                    sm = sba("sm", [128, 64])
                    t1b = sba("t1b", [128, 4, 64])
                    zring['n'] = 3
                    for r_ in range(3):
                        memset('pool', Zm[r_][:], 0.0, w=[f'Zm{r_}'])
                    dma(cmask[:], I['cmask'][:, :, :], w=['cmask'])
                    for hg in range(4):
                        h = 4 * g + hg
                        r0 = (hg % 2) * 64
                        P.op('pool', E('tensor_copy', out=QA[hg][0:64, :], in_=zqT[r0:r0 + 64, hg // 2, :]),
                             r=['zqT'], w=[f'QA{hg}'])
                        dma(QA[hg][64:69, :], I['QAh'][h, :, :], w=[f'QA{hg}'])
                    pti = {'i': 0, 'w': 0}
                    def cmp_topk(c):
                        for qi in range(4):
                            i = 4 * c + qi
                            qt = 2 * i + 1
                            qsl = slice(i * 128, (i + 1) * 128)
                            ctm = (qt - 1) // 16
                            bsl = i % 2
                            dma(bon[bsl][:], I['bonus'][i, :, :], w=[f'bon{bsl}'])
                            for hg in range(4):
                                h = 4 * g + hg
                                pz, pzn = nextZ()
                                for ct in range(ctm + 1):
                                    m = qt - 16 * ct
                                    partial = m < 17
                                    mm(pz[:, ct * 128:(ct + 1) * 128], kcA[0:69, ct * 128:(ct + 1) * 128], QA[hg][0:69, qsl],
                                       True, not partial, r=['kcA', f'QA{hg}'], w=[pzn])
                                    if partial:
                                        mm(pz[:, ct * 128:(ct + 1) * 128], identb[:], cmask[:, (m - 1) // 2, :], False, True,
                                           r=['identb', 'cmask'], w=[pzn])
                                ptc = PT[pti['i'] % 4]
                                ptn = f"PT{pti['i'] % 4}"
                                pti['i'] += 1
                                ncol = (ctm + 1) * 128
                                act(ptc[:, 0:ncol], pz[:, 0:ncol], AF.Exp, r=[pzn], w=[ptn], scale=SC_NSA)
                                def cmpB(hg=hg, h=h, i=i, qi=qi, ctm=ctm, ptc=ptc, ptn=ptn):
                                    ub = 'psTa' if hg < 2 else 'psTb'
                                    uo = hg * 256
                                    for ct in range(ctm + 1):
                                        mm(psT[:, uo:uo + 193], ptc[:, ct * 128:(ct + 1) * 128], VCA[:, ct, 0:193],
                                           ct == 0, ct == ctm, r=[ptn, 'VCA'], w=[ub])
                                    ts('dve', sm[:, hg:hg + 1], psT[:, uo + 192:uo + 193], 1e-30, None, ALU.max, r=[ub], w=[f'sm{hg}'])
                                    P.op('dve', E('reciprocal', out=sm[:, 4 + hg:5 + hg], in_=sm[:, hg:hg + 1]),
                                         r=[f'sm{hg}'], w=[f'smr{hg}'])
                                    if hg == 0:
                                        ts('dve', pslc[:], psT[:, uo + 64:uo + 192], sm[:, 4 + hg:5 + hg], None, ALU.mult,
                                           r=[ub, f'smr{hg}'], w=['pslc'])
                                    else:
                                        stt('dve', pslc[:], psT[:, uo + 64:uo + 192], sm[:, 4 + hg:5 + hg], pslc[:], ALU.mult, ALU.add,
                                            r=[ub, f'smr{hg}', 'pslc'], w=['pslc'])
                                    tt('dve', sm[:, 8 + hg:9 + hg], sm[:, 4 + hg:5 + hg], sg[:, i, h:h + 1], ALU.mult,
                                       r=[f'smr{hg}', 'sg'], w=[f'smg{hg}'])
                                    ts('dve', ocmpAll[c % 2][:, qi, hg, :], psT[:, uo:uo + 64], sm[:, 8 + hg:9 + hg], None, ALU.mult,
                                       r=[ub, f'smg{hg}'], w=[f'ocmp{c % 2}'])
                                defer(cmpB)
                            flush()
                            tt('dve', scb[:], pslc[:], bon[bsl][:], ALU.add, r=['pslc', f'bon{bsl}'], w=['scb'])
                            P.op('dve', E('max', out=mx[:, 0:8], in_=scb[:]), r=['scb'], w=['mx0'])
                            P.op('dve', E('match_replace', out=mrb[:], in_to_replace=mx[:, 0:8], in_values=scb[:],
                                                                  imm_value=-3e9), r=['scb', 'mx0'], w=['mrb'])
                            P.op('dve', E('max', out=mx[:, 8:16], in_=mrb[:]), r=['mrb'], w=['mx1'])
                            for r_ in range((2 * qt + 1) // 58 + 1):
                                nb = min(58, 128 - 58 * r_)
                                ts('dve', Zm[r_][:, 69:69 + nb], scb[:, 58 * r_:58 * r_ + nb], mx[:, 15:16], NEG, ALU.is_lt, ALU.mult,
                                   r=['scb', 'mx1'], w=[f'Zm{r_}'])
                                pz, pzn = nextZ()
                                P.op('pe', E('transpose', out=pz[:, 0:128], in_=Zm[r_][:], identity=ident[:]),
                                     r=[f'Zm{r_}', 'ident'], w=[pzn])
                                act(RqAll[c % 2][r_][64:128, qi * 128:(qi + 1) * 128], pz[64:128, 0:128], AF.Copy, r=[pzn], w=[f'Rq{c % 2}_{r_}'])
                            if g == 0 and c == 1 and qi == 0:
                                dump('pslc', pslc[:], ['pslc'])
                                dump('negm', Zm[0][:], ['Zm0'])
                    cmp_topk(0)
                    for c in range(8):
                        for hg in range(4):
                            h = 4 * g + hg
                            if hg == 2 and c + 1 < 8:
                                cmp_topk(c + 1)
                            for qi in range(4):
                                i = 4 * c + qi
                                qt = 2 * i + 1
                                qsl = slice(i * 128, (i + 1) * 128)
                                kts = [kt for kt in range(qt - 4, qt + 1) if kt >= 0]
                                pw = PTw[pti['w'] % 3]
                                pwn = f"PTw{pti['w'] % 3}"
                                pti['w'] += 1
                                pzA, pzAn = nextZ()
                                pzB, pzBn = nextZ()
                                for j, kt in enumerate(kts):
                                    ksl = slice(kt * 128, (kt + 1) * 128)
                                    last = (kt == qt)
                                    dst = pzB[:, 0:128] if last else pzA[:, j * 128:(j + 1) * 128]
                                    dn = pzBn if last else pzAn
                                    masked = last or (kt == qt - 4)
                                    mm(dst, kwA[0:69, ksl], QA[hg][0:69, qsl], True, not masked, r=['kwA', f'QA{hg}'], w=[dn])
                                    if masked:
                                        mm(dst, identb[:], tri[:] if last else band[:], False, True, r=['identb', 'tri', 'band'], w=[dn])
                                na = len(kts) - 1
                                if na > 0:
                                    act(pw[:, 0:na * 128], pzA[:, 0:na * 128], AF.Exp, r=[pzAn], w=[pwn + 'a'], scale=SC_NSA)
                                act(pw[:, 512:640], pzB[:, 0:128], AF.Exp, r=[pzBn], w=[pwn + 'b'], scale=SC_NSA)
                                def winB(kts=kts, qt=qt, qi=qi, pw=pw, pwn=pwn):
                                    for j, kt in enumerate(kts):
                                        last = (kt == qt)
                                        src = pw[:, 512:640] if last else pw[:, j * 128:(j + 1) * 128]
                                        mm(psO[:, qi * 128:qi * 128 + 65], src, vwA[:, kt, 0:65], (j == 0 and qi == 0), last,
                                           r=[pwn + 'a', pwn + 'b', 'vwA'], w=['psO'], skip=True)
                                defer(winB)
                            for r_ in range((16 * c + 15) // 58 + 1):
                                P.op('pool', E('tensor_copy', out=RqAll[c % 2][r_][0:69, :], in_=QA[hg][0:69, c * 512:(c + 1) * 512]),
                                     r=[f'QA{hg}'], w=[f'Rq{c % 2}_{r_}'])
                            ktmax = 8 * c + 7
                            for kt in range(ktmax + 1):
                                qmin = max(0, (kt - (8 * c + 1) + 1) // 2)
                                cs = slice(qmin * 128, 512)
                                qs = slice(c * 512 + qmin * 128, (c + 1) * 512)
                                ksl = slice(kt * 128, (kt + 1) * 128)
                                diag = (kt % 2 == 1) and (kt >= 8 * c + 1)
                                pz, pzn = nextZ()
                                if pzn == 'psZ3':
                                    pz, pzn = nextZ()
                                rr_ = (2 * kt) // 58
                                mm(pz[:, cs], ksA[:, ksl], RqAll[c % 2][rr_][:, cs], True, not diag, r=['ksA', f'Rq{c % 2}_{rr_}'], w=[pzn])
                                if diag:
                                    qd = (kt - (8 * c + 1)) // 2
                                    mm(pz[:, qd * 128:(qd + 1) * 128], identb[:], tri[:], False, True, r=['identb', 'tri'], w=[pzn])
                                ptc = PT[pti['i'] % 4]
                                ptn = f"PT{pti['i'] % 4}"
                                pti['i'] += 1
                                act(ptc[:, cs], pz[:, cs], AF.Exp, r=[pzn], w=[ptn], scale=SC_NSA)
                                def selB(kt=kt, qmin=qmin, ptc=ptc, ptn=ptn, c=c):
                                    for qi in range(qmin, 4):
                                        mm(psZ[3][:, qi * 128:qi * 128 + 65], ptc[:, qi * 128:(qi + 1) * 128], vsA[:, kt, 0:65],
                                           (kt == 0 and qi == 0), kt == 8 * c + 1 + 2 * qi, r=[ptn, 'vsA'], w=['psZ3'], skip=True)
                                defer(selB)
                            def finB(c=c, hg=hg, h=h):
                                i0 = 4 * c
                                ts('dve', sm[:, 16:20], psO[:, 64:512:128], 1e-30, None, ALU.max, r=['psO'], w=['smw'])
                                P.op('dve', E('reciprocal', out=sm[:, 20:24], in_=sm[:, 16:20]), r=['smw'], w=['smwr'])
                                tt('dve', sm[:, 24:28], sm[:, 20:24], sg[:, i0:i0 + 4, 16 + h], ALU.mult, r=['smwr', 'sg'], w=['smwm'])
                                ts('dve', sm[:, 28:32], psZ[3][:, 64:512:128], 1e-30, None, ALU.max, r=['psZ3'], w=['sms'])
                                P.op('dve', E('reciprocal', out=sm[:, 32:36], in_=sm[:, 28:32]), r=['sms'], w=['smsr'])
                                tt('dve', sm[:, 36:40], sm[:, 32:36], sg[:, i0:i0 + 4, 8 + h], ALU.mult, r=['smsr', 'sg'], w=['smsm'])
                                for qi in range(4):
                                    stt('dve', t1b[:, qi, :], psO[:, qi * 128:qi * 128 + 64], sm[:, 24 + qi:25 + qi], ocmpAll[c % 2][:, qi, hg, :],
                                        ALU.mult, ALU.add, r=['psO', 'smwm', f'ocmp{c % 2}'], w=['t1b'])
                                    stt('dve', ostage[:, i0 + qi, hg * 64:(hg + 1) * 64], psZ[3][:, qi * 128:qi * 128 + 64],
                                        sm[:, 36 + qi:37 + qi], t1b[:, qi, :], ALU.mult, ALU.add, r=['psZ3', 'smsm', 't1b'], w=['ostage'])
                            defer(finB)
                    flush()
                    if g == 0:
                        dump('ostage', ostage[:, 0:4, :], ['ostage'])
                    zring['n'] = 4
                    for f in range(2):
                        for i8 in range(4):
                            for j in range(8):
                                i = i8 * 8 + j
                                P.op('pe', E('transpose', out=psB[:, j * 128:(j + 1) * 128],
                                                                                in_=ostage[:, i, f * 128:(f + 1) * 128],
                                                                                identity=identb[:]),
                                     r=['ostage', 'identb'], w=['psB'])
                            evac(oTn[:, 2 * g + f, i8 * 1024:(i8 + 1) * 1024], psB[:, :], r=['psB'], w=['oTn'])
                P.barrier()

        checkpoint('nsa')
        oTm = OS.enter_context(SB("oTm", [128, 4, NO * 128], BF16))
        with ExitStack() as MS:
            def sbm(name, shape, dt=F32):
                return MS.enter_context(SB(name, shape, dt))
            ckvT = sbm("ckvT", [128, 2, S], BF16)
            cqT = sbm("cqT", [128, 3, NO * 128], BF16)
            KT = sbm("KT", [128, S], BF16)
            with ExitStack() as PS:
                def sbp(name, shape, dt=F32):
                    return PS.enter_context(SB(name, shape, dt))
                wm = sbp("wm", [128, 8, 640], BF16)
                wkrA = sbp("wkrA", [128, 8, 96], BF16)
                wkrB = sbp("wkrB", [128, 8, 96], BF16)
                xr = [sbp(f"mxr{i}", [128, 1024]) for i in range(2)]
                hTr = [sbp(f"mhTr{i}", [128, 8, 128], BF16) for i in range(2)]
                zf = sbp("zf", [128, 5, 128])
                zs = sbp("zs", [128, 5, 128])
                rcb = sbp("rcb", [128, 2, 128])
                ctab = [sbp(f"ctab{i}", [128, 2, 128]) for i in range(2)]
                rt = sbp("rt", [128, 2, 128])
                wiv = kparts(I['w_in'])
                dma(wm[:, :, 0:384], wiv[:, :, C_QD:C_QD + 384], w=['wm'], q='pool')
                dma(wm[:, :, 384:640], wiv[:, :, C_KVD:C_KVD + 256], w=['wm'], q='pool')
                memset('pool', wkrA[:], 0.0, w=['wkrA'])
                memset('pool', wkrB[:], 0.0, w=['wkrB'])
                dma(wkrA[:, :, 64:96], wiv[:, :, C_KR:C_KR + 32], w=['wkrA'], q='pool')
                dma(wkrB[:, :, 80:96], wiv[:, :, C_KR:C_KR + 16], w=['wkrB'], q='pool')
                dma(wkrB[:, :, 64:80], wiv[:, :, C_KR + 16:C_KR + 32], w=['wkrB'], q='pool')
                ts('pool', wkrB[:, :, 64:80], wkrB[:, :, 64:80], -1.0, None, ALU.mult, r=['wkrB'], w=['wkrB'])
                dma(KT[96:97, :], I['KA'][4:5, :], w=['KT'])
                for lt in range(NT):
                    sl = lt % 2
                    hT = hTr[sl]
                    hres = f"mhT{sl}"

                    def m1_pre(t2):
                        s2 = t2 % 2
                        xt2 = xr[s2]
                        xres2 = f"mxr{s2}"
                        dma(xt2[:], I['xl'][t2 * 128:(t2 + 1) * 128, :], w=[xres2 + 'a', xres2 + 'b', xres2])
                        hT_pre(xt2, xres2, xt2, xres2, s2)
                    if lt == 0:
                        m1_pre(0)
                    dma(ctab[sl][64:96, 0, :], I['cosk'][:, lt * 128:(lt + 1) * 128], w=[f'ctab{sl}'])
                    dma(ctab[sl][64:96, 1, :], I['sink'][:, lt * 128:(lt + 1) * 128], w=[f'ctab{sl}'])
                    hT_post(hT, hres, gsA, shA)
                    tsl = slice(lt * 128, (lt + 1) * 128)
                    own = (lt % 2 == 1)
                    i = lt // 2
                    ntl = [(384, 0), (512, 1)] + ([(0, 2), (128, 3), (256, 4)] if own else [])
                    pz, pzn = nextZ()
                    pz2, pz2n = nextZ()
                    for (c0, slot) in ntl:
                        dst = pz[:, slot * 128:(slot + 1) * 128] if slot < 4 else pz2[:, 0:128]
                        dn = pzn if slot < 4 else pz2n
                        for k in range(8):
                            mm(dst, wm[:, k, c0:c0 + 128], hT[:, k, :], k == 0, k == 7, r=['wm', hres], w=[dn])
                    for k in range(8):
                        mm(pz2[0:96, 128:256], wkrA[:, k, :], hT[:, k, :], k == 0, k == 7, r=['wkrA', hres], w=[pz2n])
                    for k in range(8):
                        mm(pz2[0:96, 256:384], wkrB[:, k, :], hT[:, k, :], k == 0, k == 7, r=['wkrB', hres], w=[pz2n])
                    if lt + 1 < NT:
                        m1_pre(lt + 1)
                    nsl = 5 if own else 2
                    for (c0, slot) in ntl:
                        srcp = pz[:, slot * 128:(slot + 1) * 128] if slot < 4 else pz2[:, 0:128]
                        sn = pzn if slot < 4 else pz2n
                        evac(zf[:, slot, :], srcp, r=[sn], w=[f'zf{slot}'])
                        act(zs[:, slot, :], srcp, AF.Square, r=[sn], w=[f'zs{slot}'])
                    pz3, pz3n = nextZ()
                    for j, slot in enumerate((0, 1)):
                        mm(pz3[:, 0:128], onesf[:], zs[:, slot, :], j == 0, j == 1, r=['onesf', f'zs{slot}'], w=[pz3n])
                    if own:
                        for j, slot in enumerate((2, 3, 4)):
                            mm(pz3[:, 128:256], onesf[:], zs[:, slot, :], j == 0, j == 2, r=['onesf', f'zs{slot}'], w=[pz3n])
                    act(rcb[:, 0, :], pz3[:, 0:128], AF.Sqrt, r=[pz3n, 'epsb'], w=['rcb0'], scale=1.0 / 256, bias=epsb[:, 0:1])
                    P.op('dve', E('reciprocal', out=rcb[:, 0, :], in_=rcb[:, 0, :]), r=['rcb0'], w=['rcb0'])
                    for slot in (0, 1):
                        tt('pool' if slot else 'dve', ckvT[:, slot, tsl], zf[:, slot, :], rcb[:, 0, :], ALU.mult,
                           r=[f'zf{slot}', 'rcb0'], w=['ckvT'])
                    if own:
                        act(rcb[:, 1, :], pz3[:, 128:256], AF.Sqrt, r=[pz3n, 'epsb'], w=['rcb1'], scale=1.0 / 384, bias=epsb[:, 0:1])
                        P.op('dve', E('reciprocal', out=rcb[:, 1, :], in_=rcb[:, 1, :]), r=['rcb1'], w=['rcb1'])
                        for slot in (2, 3, 4):
                            tt('pool' if slot % 2 else 'dve', cqT[:, slot - 2, i * 128:(i + 1) * 128], zf[:, slot, :], rcb[:, 1, :],
                               ALU.mult, r=[f'zf{slot}', 'rcb1'], w=['cqT'])
                    tt('dve', rt[64:96, 0, :], pz2[64:96, 128:256], ctab[sl][64:96, 0, :], ALU.mult, r=[pz2n, f'ctab{sl}'], w=['rt0'])
                    tt('dve', rt[64:96, 1, :], pz2[64:96, 256:384], ctab[sl][64:96, 1, :], ALU.mult, r=[pz2n, f'ctab{sl}'], w=['rt1'])
                    tt('pool', KT[64:96, tsl], rt[64:96, 0, :], rt[64:96, 1, :], ALU.add, r=['rt0', 'rt1'], w=['KT'])
                dump('ckvT', ckvT[:, 0, 0:1024], ['ckvT'])
                dump('cqT', cqT[:, 0, 0:512], ['cqT'])
                dump('krot', KT[64:96, 0:1024], ['KT'])
            P.barrier()
            checkpoint('m1')
            with ExitStack() as AS:
                def sba(name, shape, dt=F32):
                    return AS.enter_context(SB(name, shape, dt))
                wuq = sba("wuq", [128, 3, 768], BF16)
                wuqB = sba("wuqB", [128, 3, 768], BF16)
                wuk = sba("wuk", [128, 2, 512], BF16)
                wuv = sba("wuv", [128, 2, 512], BF16)
                VH = sba("VH", [128, NT, 66], BF16)
                QT = sba("QT", [128, NO * 128], BF16)
                PT = [sba(f"MPT{i}", [128, 512], BF16) for i in range(4)]
                ostage = sba("mostage", [128, NO, 128], BF16)
                qtab = [sba(f"qtab{i}", [128, 2, 512]) for i in range(2)]
                rt = sba("mrt", [128, 2, 512])
                sm = sba("msm", [128, 8])
                dma(wuq[:], kparts(I['w_uq']), w=['wuq'], q='pool')
                dma(wuk[:], kparts(I['w_uk']), w=['wuk'], q='pool')
                dma(wuv[:], kparts(I['w_uv']), w=['wuv'], q='pool')
                for k in range(3):
                    ts('pool', wuq[:, k, :], wuq[:, k, :], gl[:, 16 + k:17 + k], None, ALU.mult, r=['wuq', 'gl'], w=['wuq'])
                for k in range(2):
                    ts('pool', wuk[:, k, :], wuk[:, k, :], gl[:, 19 + k:20 + k], None, ALU.mult, r=['wuk', 'gl'], w=['wuk'])
                    ts('pool', wuv[:, k, :], wuv[:, k, :], gl[:, 19 + k:20 + k], None, ALU.mult, r=['wuv', 'gl'], w=['wuv'])
                memset('pool', wuqB[:], 0.0, w=['wuqB'])
                wq4 = wuq[:].rearrange("p k (h c) -> p k h c", c=96)
                wb4 = wuqB[:].rearrange("p k (h c) -> p k h c", c=96)
                for k in range(3):
                    ts('pool', wb4[:, k, :, 64:80], wq4[:, k, :, 80:96], -1.0, None, ALU.mult, r=['wuq'], w=['wuqB'])
                    P.op('pool', E('tensor_copy', out=wb4[:, k, :, 80:96], in_=wq4[:, k, :, 64:80]), r=['wuq'], w=['wuqB'])
                memset('pool', VH[:, :, 64:66], 1.0, w=['VH'])
                memset('pool', QT[96:97, :], NEG, w=['QT'])
                zring['n'] = 3
                pti = {'i': 0}
                hz = {'i': 0}

                def nextH():
                    hz['i'] = (hz['i'] + 1) % 2
                    return (psT[:, 0:512], 'psTa') if hz['i'] == 0 else (psT[:, 512:1024], 'psTb')

                for h in range(8):
                    for ch in range(16):
                        pz, pzn = nextH()
                        for k in range(2):
                            mm(pz[0:64, :], wuk[:, k, h * 64:(h + 1) * 64], ckvT[:, k, ch * 512:(ch + 1) * 512], k == 0, k == 1,
                               r=['wuk', 'ckvT'], w=[pzn])
                        evac(KT[0:64, ch * 512:(ch + 1) * 512], pz[0:64, :], r=[pzn], w=['KT'])
                    for t8 in range(8):
                        pz, pzn = nextH()
                        for j in range(8):
                            lt = t8 * 8 + j
                            for k in range(2):
                                mm(pz[:, j * 64:(j + 1) * 64], ckvT[:, k, lt * 128:(lt + 1) * 128], wuv[:, k, h * 64:(h + 1) * 64],
                                   k == 0, k == 1, r=['wuv', 'ckvT'], w=[pzn])
                        evac(VH[:, t8 * 8:(t8 + 1) * 8, 0:64], pz.rearrange("p (a b) -> p a b", b=64), r=[pzn], w=['VH'])
                    for c in range(8):
                        csl = slice(c * 512, (c + 1) * 512)
                        qs = c % 2
                        dma(qtab[qs][64:96, 0, :], I['cosq'][:, csl], w=[f'qtab{qs}'])
                        dma(qtab[qs][64:96, 1, :], I['sinq'][:, csl], w=[f'qtab{qs}'])
                        pzA, pzAn = nextH()
                        for k in range(3):
                            mm(pzA[0:96, :], wuq[:, k, h * 96:(h + 1) * 96], cqT[:, k, csl], k == 0, k == 2, r=['wuq', 'cqT'], w=[pzAn])
                        pzB, pzBn = nextH()
                        for k in range(3):
                            mm(pzB[0:96, :], wuqB[:, k, h * 96:(h + 1) * 96], cqT[:, k, csl], k == 0, k == 2, r=['wuqB', 'cqT'], w=[pzBn])
                        act(QT[0:64, csl], pzA[0:64, :], AF.Copy, r=[pzAn], w=['QT'])
                        tt('dve', rt[64:96, 0, :], pzA[64:96, :], qtab[qs][64:96, 0, :], ALU.mult, r=[pzAn, f'qtab{qs}'], w=['mrt0'])
                        tt('dve', rt[64:96, 1, :], pzB[64:96, :], qtab[qs][64:96, 1, :], ALU.mult, r=[pzBn, f'qtab{qs}'], w=['mrt1'])
                        tt('pool', QT[64:96, csl], rt[64:96, 0, :], rt[64:96, 1, :], ALU.add, r=['mrt0', 'mrt1'], w=['QT'])
                    if h == 0:
                        dump('QT', QT[0:96, 0:512], ['QT'])
                        dump('KT', KT[0:96, 0:1024], ['KT'])
                    for c in range(8):
                        ob, obn = (psO, 'psO') if c % 2 == 0 else (psZ[3], 'psZ3')
                        ktmax = 8 * c + 7
                        for kt in range(ktmax + 1):
                            qmin = max(0, (kt - (8 * c + 1) + 1) // 2)
                            cs = slice(qmin * 128, 512)
                            qs_ = slice(c * 512 + qmin * 128, (c + 1) * 512)
                            ksl = slice(kt * 128, (kt + 1) * 128)
                            diag = (kt % 2 == 1) and (kt >= 8 * c + 1)
                            pz, pzn = nextZ()
                            if pzn == 'psZ3':
                                pz, pzn = nextZ()
                            mm(pz[:, cs], KT[0:97, ksl], QT[0:97, qs_], True, not diag, r=['KT', 'QT'], w=[pzn])
                            if diag:
                                qd = (kt - (8 * c + 1)) // 2
                                mm(pz[:, qd * 128:(qd + 1) * 128], identb[:], tri[:], False, True, r=['identb', 'tri'], w=[pzn])
                            ptc = PT[pti['i'] % 4]
                            ptn = f"MPT{pti['i'] % 4}"
                            pti['i'] += 1
                            act(ptc[:, cs], pz[:, cs], AF.Exp, r=[pzn], w=[ptn], scale=SC_MLA)
                            def mlaB(kt=kt, qmin=qmin, ptc=ptc, ptn=ptn, c=c, ob=ob, obn=obn):
                                for qi in range(qmin, 4):
                                    mm(ob[:, qi * 128:qi * 128 + 65], ptc[:, qi * 128:(qi + 1) * 128], VH[:, kt, 0:65],
                                       (kt == 0 and qi == 0), kt == 8 * c + 1 + 2 * qi, r=[ptn, 'VH'], w=[obn], skip=True)
                            defer(mlaB)
                        def mlaF(c=c, h=h, ob=ob, obn=obn):
                            ts('dve', sm[:, 0:4], ob[:, 64:512:128], 1e-30, None, ALU.max, r=[obn], w=['msm0'])
                            P.op('dve', E('reciprocal', out=sm[:, 4:8], in_=sm[:, 0:4]), r=['msm0'], w=['msm1'])
                            for qi in range(4):
                                ts('dve', ostage[:, 4 * c + qi, (h % 2) * 64:(h % 2) * 64 + 64], ob[:, qi * 128:qi * 128 + 64],
                                   sm[:, 4 + qi:5 + qi], None, ALU.mult, r=[obn, 'msm1'], w=['mostage'])
                        defer(mlaF)
                    flush()
                    if h % 2 == 1:
                        if h == 1:
                            dump('mostage', ostage[:, 0:4, :], ['mostage'])
                        for i8 in range(4):
                            for j in range(8):
                                i = i8 * 8 + j
                                P.op('pe', E('transpose', out=psB[:, j * 128:(j + 1) * 128], in_=ostage[:, i, :],
                                                                           identity=identb[:]),
                                     r=['mostage', 'identb'], w=['psB'])
                            evac(oTm[:, h // 2, i8 * 1024:(i8 + 1) * 1024], psB[:, :], r=['psB'], w=['oTm'])
            P.barrier()

        zring['n'] = 4
        checkpoint('mla')
        def bcast_vec(col0):
            for j in range(8):
                pz, pzn = nextZ()
                src = col0(j)
                ts('dve', dg[:], ident[:], src, None, ALU.mult, r=['ident', 'modT', 'gl'], w=['dg'])
                mm(pz[:, 0:128], onesf[:], dg[:], True, True, r=['onesf', 'dg'], w=[pzn])
                evac(gbc[:, j * 128:(j + 1) * 128], pz[:, 0:128], r=[pzn], w=['gbc'])

        with ExitStack() as TS:
            def sbt(name, shape, dt=F32):
                return TS.enter_context(SB(name, shape, dt))
            wg = sbt("wg", [128, 8, 2048], BF16)
            won = sbt("won", [128, 4, 1024], BF16)
            wom = sbt("wom", [128, 4, 1024], BF16)
            wout = sbt("wout", [128, 8, 1024], BF16)
            xr = [sbt(f"txr{i}", [128, 1024]) for i in range(2)]
            xn = sbt("txn", [128, 1024])
            hTr = [sbt(f"thT{i}", [128, 8, 128], BF16) for i in range(2)]
            sgA = sbt("sgA", [128, 1024])
            sgB = sbt("sgB", [128, 1024])
            m1 = sbt("m1", [128, 1024])
            m2 = sbt("m2", [128, 1024])
            yT = sbt("yT", [128, 8, 128], BF16)
            yb = sbt("yb", [128, 1024], BF16)
            x1r = [sbt(f"x1r{i}", [128, 1024]) for i in range(2)]
            wiv = kparts(I['w_in'])
            dma(wg[:, :, 0:1024], wiv[:, :, C_GA:C_GA + 1024], w=['wg'], q='pool')
            dma(wg[:, :, 1024:2048], wiv[:, :, C_GB:C_GB + 1024], w=['wg'], q='pool')
            dma(won[:], kparts(I['w_o_nsa']), w=['won'], q='pool')
            dma(wom[:], kparts(I['w_o_mla']), w=['wom'], q='pool')
            dma(wout[:], kparts(I['w_out']), w=['wout'], q='pool')
            bcast_vec(lambda j: modT[:, 16 + j:17 + j])
            m3 = sbt("m3", [128, 1024])
            yT2 = sbt("yT2", [128, 8, 128], BF16)
            xr3 = sbt("txr2", [128, 1024])
            xr = xr + [xr3]
            yTs = [yT, yT2]

            def t1_pre(i2):
                s2 = i2 % 3
                dma(xr[s2][:], I['xo'][i2 * 128:(i2 + 1) * 128, :], w=[f"txr{s2}"])
                hT_pre(xr[s2], f"txr{s2}", xn, 'txn', i2 % 2)

            def t1_A(i):
                isl = slice(i * 128, (i + 1) * 128)
                hT = hTr[i % 2]
                hres = f"thT{i % 2}"
                for n in range(2):
                    pz, pzn = nextZ()
                    for k in range(8):
                        mm(pz[:, :], hT[:, k, :], wg[:, k, n * 512:(n + 1) * 512], k == 0, k == 7, r=['wg', hres], w=[pzn])
                    act(sgA[:, n * 512:(n + 1) * 512], pz[:, :], AF.Sigmoid, r=[pzn], w=['sgA'])
                if i + 1 < NO:
                    t1_pre(i + 1)
                for n in range(2):
                    pz, pzn = nextZ()
                    for k in range(4):
                        mm(pz[:, :], oTn[:, k, isl], won[:, k, n * 512:(n + 1) * 512], k == 0, k == 3, r=['won', 'oTn'], w=[pzn])
                    tt('dve', m1[:, n * 512:(n + 1) * 512], pz[:, :], sgA[:, n * 512:(n + 1) * 512], ALU.mult, r=[pzn, 'sgA'], w=['m1'])
                for n in range(2):
                    pz, pzn = nextZ()
                    for k in range(8):
                        mm(pz[:, :], hT[:, k, :], wg[:, k, 1024 + n * 512:1024 + (n + 1) * 512], k == 0, k == 7, r=['wg', hres], w=[pzn])
                    act(sgB[:, n * 512:(n + 1) * 512], pz[:, :], AF.Sigmoid, r=[pzn], w=['sgB'])
                if i + 1 < NO:
                    hT_post(hTr[(i + 1) % 2], f"thT{(i + 1) % 2}", gsA, shA)
                for n in range(2):
                    pz, pzn = nextZ()
                    for k in range(4):
                        mm(pz[:, :], oTm[:, k, isl], wom[:, k, n * 512:(n + 1) * 512], k == 0, k == 3, r=['wom', 'oTm'], w=[pzn])
                    tt('dve', m2[:, n * 512:(n + 1) * 512], pz[:, :], sgB[:, n * 512:(n + 1) * 512], ALU.mult, r=[pzn, 'sgB'], w=['m2'])
                tt('dve', yb[:], m1[:], m2[:], ALU.add, r=['m1', 'm2'], w=['yb'])
                for k in range(8):
                    P.op('pe', E('transpose', out=psB[:, k * 128:(k + 1) * 128], in_=yb[:, k * 128:(k + 1) * 128],
                                 identity=identb[:]), r=['yb', 'identb'], w=['psB'])
                yTc = yTs[i % 2]
                yv = yTc[:].rearrange("p k t -> p (k t)")
                act(yv[:, 0:512], psB[:, 0:512], AF.Copy, r=['psB'], w=[f'yT{i % 2}'])
                ts('dve', yv[:, 512:1024], psB[:, 512:1024], 1.0, None, ALU.mult, r=['psB'], w=[f'yT{i % 2}'])

            def t1_B(i):
                isl = slice(i * 128, (i + 1) * 128)
                yTc = yTs[i % 2]
                xt = xr[i % 3]
                x1 = x1r[i % 2]
                x1n = f"x1r{i % 2}"
                for n in range(2):
                    pz, pzn = nextZ()
                    for k in range(8):
                        mm(pz[:, :], yTc[:, k, :], wout[:, k, n * 512:(n + 1) * 512], k == 0, k == 7, r=['wout', f'yT{i % 2}'], w=[pzn])
                    tt('dve', m3[:, n * 512:(n + 1) * 512], pz[:, :], gbc[:, n * 512:(n + 1) * 512], ALU.mult, r=[pzn, 'gbc'], w=['m3'])
                tt('dve', x1[:], m3[:], xt[:], ALU.add, r=['m3', f"txr{i % 3}"], w=[x1n])
                dma(x1s[isl, :], x1[:], r=[x1n], w=['x1s'])
                if i == 0:
                    dump('x1', x1[:], [x1n])

            t1_pre(0)
            hT_post(hTr[0], "thT0", gsA, shA)
            for i in range(NO):
                t1_A(i)
                if i >= 1:
                    t1_B(i - 1)
            t1_B(NO - 1)
        P.barrier()
        OS.close()
        checkpoint('t1')

        with ExitStack() as TS:
            def sbt(name, shape, dt=F32):
                return TS.enter_context(SB(name, shape, dt))
            wf1 = sbt("wf1", [128, 8, 4096], BF16)
            wf2 = sbt("wf2", [128, 32, 1024], BF16)
            gfb = sbt("gfb", [128, 1024])
            xr = [sbt(f"uxr{i}", [128, 1024]) for i in range(2)]
            xn = sbt("uxn", [128, 1024])
            hTr = [sbt(f"uhT{i}", [128, 8, 128], BF16) for i in range(2)]
            aT = [sbt(f"aT{i}", [128, 32, 128], BF16) for i in range(2)]
            rl = [sbt(f"rl{i}", [128, 512]) for i in range(2)]
            o2 = sbt("o2", [128, 1024])
            tmp = sbt("utmp", [128, 1024])
            res = [sbt(f"ures{i}", [128, 1024]) for i in range(2)]
            fs = sbt("fs", [128, 4])
            wf1v = kparts(I['w_fc1'])
            wf2v = kparts(I['w_fc2'])
            for q4 in range(4):
                dma(wf1[:, :, q4 * 1024:(q4 + 1) * 1024], wf1v[:, :, q4 * 1024:(q4 + 1) * 1024], w=[f'wf1_{q4}'], q='pool')
            for q4 in range(4):
                dma(wf2[:, q4 * 8:(q4 + 1) * 8, :], wf2v[:, q4 * 8:(q4 + 1) * 8, :], w=[f'wf2_{q4}'], q='pool')
            bcast_vec(lambda j: gl[:, 21 + j:22 + j])
            P.op('pool', E('tensor_copy', out=gfb[:], in_=gbc[:]), r=['gbc'], w=['gfb'])
            bcast_vec(lambda j: modT[:, 40 + j:41 + j])
            def t2_pre(i2):
                s2 = i2 % 2
                dma(xr[s2][:], x1s[i2 * 128:(i2 + 1) * 128, :], r=['x1s'], w=[f"uxr{s2}"])
                hT_pre(xr[s2], f"uxr{s2}", xn, 'uxn', s2)

            t2_pre(0)
            hT_post(hTr[0], "uhT0", gsM, shM)
            for i in range(NO):
                sl = i % 2
                xt = xr[sl]
                xres = f"uxr{sl}"
                isl = slice(i * 128, (i + 1) * 128)
                hT = hTr[sl]
                hres = f"uhT{sl}"
                a = aT[sl]
                an = f"aT{sl}"
                for jg in range(8):
                    pz, pzn = (psZ[jg % 2], f"psZ{jg % 2}")
                    for jj in range(4):
                        j = jg * 4 + jj
                        for k in range(8):
                            mm(pz[:, jj * 128:(jj + 1) * 128], wf1[:, k, j * 128:(j + 1) * 128], hT[:, k, :], k == 0, k == 7,
                               r=[f'wf1_{jg // 2}', hres], w=[pzn])
                    rb = rl[jg % 2]
                    rbn = f"rl{jg % 2}"
                    act(rb[:], pz[:, :], AF.Relu, r=[pzn], w=[rbn])
                    av = a[:, jg * 4:(jg + 1) * 4, :].rearrange("p a b -> p (a b)")
                    tt('pool', av, rb[:], rb[:], ALU.mult, r=[rbn], w=[an])
                if i + 1 < NO:
                    t2_pre(i + 1)
                for n in range(2):
                    pz, pzn = (psZ[2 + n], f"psZ{2 + n}")
                    for j in range(32):
                        mm(pz[:, :], a[:, j, :], wf2[:, j, n * 512:(n + 1) * 512], j == 0, j == 31, r=[f'wf2_{j // 8}', an], w=[pzn])
                if i + 1 < NO:
                    hT_post(hTr[(i + 1) % 2], f"uhT{(i + 1) % 2}", gsM, shM)
                for n in range(2):
                    pz, pzn = (psZ[2 + n], f"psZ{2 + n}")
                    tt('dve', tmp[:, n * 512:(n + 1) * 512], pz[:, :], gbc[:, n * 512:(n + 1) * 512], ALU.mult, r=[pzn, 'gbc'], w=['utmp'])
                tt('dve', o2[:], tmp[:], xt[:], ALU.add, r=['utmp', xres], w=['o2'])
                act(junk[:], o2[:], AF.Square, r=['o2'], w=['junk', 'fs0'], accum_out=fs[:, 0:1])
                act(fs[:, 1:2], fs[:, 0:1], AF.Sqrt, r=['fs0', 'epsb'], w=['fs1'], scale=1.0 / D, bias=epsb[:, 0:1])
                P.op('dve', E('reciprocal', out=fs[:, 2:3], in_=fs[:, 1:2]), r=['fs1'], w=['fs2'])
                rs_ = res[sl]
                rn = f"ures{sl}"
                stt('dve', rs_[:], o2[:], fs[:, 2:3], gfb[:], ALU.mult, ALU.mult, r=['o2', 'fs2', 'gfb'], w=[rn])
                dma(out[isl, :], rs_[:], r=[rn], w=['out'])
    return nc


_CACHE = {}


def _lay(v, k):
    return np.ascontiguousarray(np.asarray(v, np.float32).reshape(k, 128).T)


def make_in_maps(inputs, ncores=8):
    x = np.asarray(inputs['x'], np.float32)
    shared = {
        'w_ada': np.ascontiguousarray(inputs['w_ada'][0], dtype=np.float32),
        'bada_l': _lay(inputs['b_ada'][0], 48),
        'gmix_l': _lay(inputs['g_mix'][0], 8), 'gmlp_l': _lay(inputs['g_mlp'][0], 8),
        'gcq_l': _lay(inputs['g_cq'][0], 3), 'gckv_l': _lay(inputs['g_ckv'][0], 2),
        'gfin_l': _lay(inputs['g_final'], 8),
        'w_in': np.ascontiguousarray(inputs['w_in'][0], dtype=np.float32),
        'peck_t': np.ascontiguousarray(np.asarray(inputs['pe_ck'][0], np.float32).T),
        'w_ck1': np.ascontiguousarray(inputs['w_ck1'][0], dtype=np.float32),
        'w_ck2': np.ascontiguousarray(inputs['w_ck2'][0], dtype=np.float32),
        'pecv_t': np.ascontiguousarray(np.asarray(inputs['pe_cv'][0], np.float32).T),
        'w_cv1': np.ascontiguousarray(inputs['w_cv1'][0], dtype=np.float32),
        'w_cv2': np.ascontiguousarray(inputs['w_cv2'][0], dtype=np.float32),
        'w_uq': np.ascontiguousarray(inputs['w_uq'][0], dtype=np.float32),
        'w_uk': np.ascontiguousarray(inputs['w_uk'][0], dtype=np.float32),
        'w_uv': np.ascontiguousarray(inputs['w_uv'][0], dtype=np.float32),
        'w_o_nsa': np.ascontiguousarray(inputs['w_o_nsa'][0], dtype=np.float32),
        'w_o_mla': np.ascontiguousarray(inputs['w_o_mla'][0], dtype=np.float32),
        'w_out': np.ascontiguousarray(inputs['w_out'][0], dtype=np.float32),
        'w_fc1': np.ascontiguousarray(inputs['w_fc1'][0], dtype=np.float32),
        'w_fc2': np.ascontiguousarray(inputs['w_fc2'][0], dtype=np.float32),
    }
    consts = [host_consts(0), host_consts(1)]
    in_maps = []
    for core in range(ncores):
        b, p = core // 2, core % 2
        if p == 1:
            xl = np.ascontiguousarray(x[b])
        else:
            xl = np.concatenate([np.zeros((128, D), np.float32), x[b, :S - 128]], axis=0)
        xo = np.ascontiguousarray(xl.reshape(NT, 128, D)[1::2].reshape(NO * 128, D))
        m = dict(shared)
        m['xl'] = xl
        m['xo'] = xo
        m['c_l'] = _lay(np.asarray(inputs['c'], np.float32)[b], 8)
        m.update(consts[p])
        in_maps.append(m)
    return in_maps


def assemble(results, ncores=8):
    outp = np.zeros((4, S, D), np.float32)
    for core in range(ncores):
        b, p = core // 2, core % 2
        o = np.asarray(results[core]['out'], np.float32).reshape(NO, 128, D)
        outp[b].reshape(NT, 128, D)[p::2] = o
    return outp


def kernel(**inputs):
    if 'nc' not in _CACHE:
        _CACHE['nc'] = build_program()
    nc = _CACHE['nc']
    in_maps = make_in_maps(inputs)
    res = run_bass_kernel_spmd(nc, in_maps, core_ids=list(range(8)))
    return assemble(res.results)
```

```python
import numpy as np
import ml_dtypes
from contextlib import ExitStack
import concourse.bass as bass
import concourse.mybir as mybir
from concourse.bass_utils import run_bass_kernel_spmd

F32 = mybir.dt.float32
BF16 = mybir.dt.bfloat16
AF = mybir.ActivationFunctionType
ALU = mybir.AluOpType
NPBF = ml_dtypes.bfloat16

D = 1024
S = 8192
NT = 64
NO = 32
NEG = -30000.0
EPS = 1e-6
GEN = 8192
EVAC_ACT_ONLY = True
NDSEM = 12
SC_NSA = 0.125
SC_MLA = 96.0 ** -0.5
C_Q, C_KC, C_VC, C_KS, C_VS, C_KW, C_VW, C_G, C_QD, C_KVD, C_KR, C_GA, C_GB = (
    0, 512, 640, 768, 896, 1024, 1152, 1280, 1304, 1688, 1944, 1976, 3000)


class Prog:
    ENGS = ('pe', 'act', 'dve', 'pool', 'sp')

    def __init__(self, nc):
        self.nc = nc
        self.ops = {e: [] for e in self.ENGS}
        self.cnt = {e: 0 for e in self.ENGS}
        self.lastw = {}
        self.readers = {}
        self.dcnt = {}
        self.dnext = {e: 0 for e in self.ENGS}
        self.floor = {}

    def _deps(self, r, w):
        deps = dict(self.floor)

        def add(tok):
            if tok is None:
                return
            k, v = tok
            if deps.get(k, 0) < v:
                deps[k] = v
        for x in r:
            add(self.lastw.get(x))
        for x in w:
            add(self.lastw.get(x))
            for t in self.readers.get(x, ()):
                add(t)
        return deps

    def _commit(self, tok, r, w):
        for x in r:
            self.readers.setdefault(x, []).append(tok)
        for x in w:
            self.lastw[x] = tok
            self.readers[x] = []

    def barrier(self):
        fl = {}
        for e in self.ENGS:
            c = self.cnt[e]
            if c > 0:
                fl[(e, (c - 1) // GEN)] = (c - 1) % GEN + 1
        for key, n in self.dcnt.items():
            fl[key] = 16 * n
        self.floor = fl
        self.lastw = {}
        self.readers = {}

    def op(self, eng, fn, r=(), w=()):
        deps = self._deps(r, w)
        idx = self.cnt[eng]
        self.cnt[eng] += 1
        tok = ((eng, idx // GEN), idx % GEN + 1)
        if eng == 'pe':
            deps = {k: v for k, v in deps.items() if k[0] != 'pe'}
        self.ops[eng].append(('c', fn, deps, tok))
        self._commit(tok, r, w)
        return tok

    def dma(self, q, fn, r=(), w=()):
        deps = self._deps(r, w)
        slot = self.dnext[q] % NDSEM
        self.dnext[q] += 1
        key = ('dma_' + q, slot)
        n = self.dcnt.get(key, 0)
        if n > 0 and deps.get(key, 0) < 16 * n:
            deps[key] = 16 * n
        self.dcnt[key] = n + 1
        tok = (key, 16 * (n + 1))
        self.ops[q].append(('d', fn, deps, tok))
        self._commit(tok, r, w)
        return tok

    def emit(self):
        nc = self.nc
        with ExitStack() as es:
            sems = {}
            for e in self.ENGS:
                for g in range((self.cnt[e] + GEN - 1) // GEN):
                    sems[(e, g)] = es.enter_context(nc.semaphore(f"s_{e}_{g}"))
            for key in self.dcnt:
                sems[key] = es.enter_context(nc.semaphore(f"s_{key[0]}_{key[1]}"))
            block = es.enter_context(nc.Block())
            engobj = {'pe': 'tensor', 'act': 'scalar', 'dve': 'vector', 'pool': 'gpsimd', 'sp': 'sync'}

            def make(ename):
                def body(e):
                    waited = {}
                    for kind, fn, deps, tok in self.ops[ename]:
                        for k, v in deps.items():
                            if waited.get(k, 0) < v:
                                e.wait_ge(sems[k], v)
                                waited[k] = v
                        ins = fn(e)
                        ins.then_inc(sems[tok[0]], 1 if kind == 'c' else 16)
                    if ename == 'sp':
                        fin = {}
                        for e2 in self.ENGS:
                            c = self.cnt[e2]
                            if c > 0:
                                fin[(e2, (c - 1) // GEN)] = (c - 1) % GEN + 1
                        for key, n in self.dcnt.items():
                            fin[key] = 16 * n
                        for k, v in fin.items():
                            if waited.get(k, 0) < v:
                                e.wait_ge(sems[k], v)
                return body
            for ename in self.ENGS:
                getattr(block, engobj[ename])(make(ename))


def host_consts(p):
    shift = 128 * (1 - p)
    c = {}
    c['ident'] = np.eye(128, dtype=np.float32)
    c['identb'] = np.eye(128, dtype=np.float32).astype(NPBF)
    k = np.arange(128)[:, None]
    q = np.arange(128)[None, :]
    c['tri'] = np.where(k > q, NEG, 0.0).astype(NPBF)
    c['band'] = np.where(k <= q, NEG, 0.0).astype(NPBF)
    half = 16
    inv_freq = (10000.0 ** (-np.arange(half, dtype=np.float32) / half)).astype(np.float32)
    L = np.arange(S)
    gpos = (L - shift).astype(np.float32)
    ang = (gpos[:, None] * inv_freq[None, :]).astype(np.float32)
    cos2 = np.concatenate([np.cos(ang), np.cos(ang)], axis=1).T.astype(np.float32)
    sin2 = np.concatenate([np.sin(ang), np.sin(ang)], axis=1).T.astype(np.float32)
    c['cosk'] = np.ascontiguousarray(cos2)
    c['sink'] = np.ascontiguousarray(sin2)
    own = (np.arange(NO)[:, None] * 256 + 128 + np.arange(128)[None, :]).reshape(-1)
    c['cosq'] = np.ascontiguousarray(cos2[:, own])
    c['sinq'] = np.ascontiguousarray(sin2[:, own])
    ka = np.zeros((5, S), np.float32)
    ka[0] = 1.0
    ka[1] = 1.0
    ka[2] = 128.0 * (L // 128)
    ka[3] = L % 128
    ka[4] = (L < shift).astype(np.float32)
    c['KA'] = ka.astype(NPBF)
    li = np.arange(512)
    cend = 16 * li + 31
    kca = np.zeros((5, 512), np.float32)
    kca[0] = 1.0
    kca[1] = 1.0
    kca[2] = 128.0 * (cend // 128)
    kca[3] = cend % 128
    kca[4] = (16 * li < shift).astype(np.float32)
    c['KCA'] = kca.astype(NPBF)
    slopes = 2.0 ** (-8.0 * np.arange(1, 9) / 8.0)
    qa = np.zeros((8, 5, NO * 128), np.float32)
    for h in range(8):
        cc = slopes[h] / SC_NSA
        qa[h, 0] = -cc * 128.0 * (own // 128)
        qa[h, 1] = -cc * (own % 128)
        qa[h, 2] = cc
        qa[h, 3] = cc
        qa[h, 4] = NEG
    c['QAh'] = qa.astype(NPBF)
    e32 = np.zeros((32, 16, 128), np.float32)
    for v in range(16):
        e32[2 * v, v, 0:64] = 1.0
        e32[2 * v + 1, v, 64:128] = 1.0

    ef = np.zeros((59, S), np.float32)
    lbt = (L // 64)
    for j in range(58):
        ef[j] = (lbt % 58 == j)
    c['EF'] = ef.astype(NPBF)
    cm = np.zeros((128, 8, 128), np.float32)
    for mi in range(8):
        m = 2 * mi + 1
        delta = 128 * m
        valid = (16 * k + 31) <= (delta + q)
        cm[:, mi, :] = np.where(valid, 0.0, NEG)
    c['cmask'] = cm.astype(NPBF)
    lia = np.arange(512)[:, None]
    lb = np.arange(128)[None, :]
    ov = ((16 * lia < 64 * lb + 64) & (16 * lia + 31 >= 64 * lb)).astype(np.float32)
    c['OVL'] = np.ascontiguousarray(ov.reshape(4, 128, 128).transpose(1, 0, 2)).astype(NPBF)
    bon = np.zeros((NO, 128, 128), np.float32)
    blk0 = 2 * (1 - p)
    for i in range(NO):
        t = 128 * (2 * i + 1) + np.arange(128)[:, None]
        cur = t // 64
        lbb = np.arange(128)[None, :]
        valid = (lbb <= cur) & (lbb >= blk0)
        forced = (lbb == blk0) | (lbb >= cur - 1)
        bon[i] = np.where(valid, np.where(forced, 1000.0, 0.0), np.where(lbb > cur, -1e9, -2e9))
    c['bonus'] = bon
    return c


CONST_SHAPES = {
    'ident': ([128, 128], F32), 'identb': ([128, 128], BF16), 'tri': ([128, 128], BF16), 'band': ([128, 128], BF16),
    'cosk': ([32, S], F32), 'sink': ([32, S], F32), 'cosq': ([32, NO * 128], F32), 'sinq': ([32, NO * 128], F32),
    'KA': ([5, S], BF16), 'KCA': ([5, 512], BF16), 'QAh': ([8, 5, NO * 128], BF16), 'EF': ([59, S], BF16),
    'cmask': ([128, 8, 128], BF16), 'OVL': ([128, 4, 128], BF16), 'bonus': ([NO, 128, 128], F32),
}
IN_SHAPES = {
    'xl': [S, D], 'xo': [NO * 128, D], 'c_l': [128, 8], 'w_ada': [D, 6 * D], 'bada_l': [128, 48],
    'gmix_l': [128, 8], 'gmlp_l': [128, 8], 'gcq_l': [128, 3], 'gckv_l': [128, 2], 'gfin_l': [128, 8],
    'w_in': [D, 4024], 'peck_t': [64, 32], 'w_ck1': [2048, 256], 'w_ck2': [256, 64],
    'pecv_t': [64, 32], 'w_cv1': [2048, 256], 'w_cv2': [256, 64],
    'w_uq': [384, 768], 'w_uk': [256, 512], 'w_uv': [256, 512], 'w_o_nsa': [512, D], 'w_o_mla': [512, D],
    'w_out': [D, D], 'w_fc1': [D, 4 * D], 'w_fc2': [4 * D, D],
}


class StopBuild(Exception):
    pass


def build_program(debug=None, stop=None):
    try:
        return _build_program(debug, stop)
    except StopBuild as ex:
        return ex.args[0]


def _build_program(debug=None, stop=None):
    nc = bass.Bass("TRN2", target_bir_lowering=False)
    I = {}
    for name, shp in IN_SHAPES.items():
        I[name] = nc.dram_tensor(name, shp, F32, kind="ExternalInput").ap()
    for name, (shp, dt) in CONST_SHAPES.items():
        I[name] = nc.dram_tensor(name, shp, dt, kind="ExternalInput").ap()
    out = nc.dram_tensor("out", [NO * 128, D], F32, kind="ExternalOutput").ap()
    x1s = nc.dram_tensor("x1s", [NO * 128, D], F32, kind="Internal").ap()
    dbg = {}
    if debug:
        for name, shp in debug.items():
            dbg[name] = nc.dram_tensor("dbg_" + name, shp, F32, kind="ExternalOutput").ap()

    P = Prog(nc)
    rr = {'ev': 0}

    def E(meth, **kw):
        return lambda e: getattr(e, meth)(**kw)

    def SB(name, shape, dt=F32):
        return nc.sbuf_tensor("sb_" + name, shape, dt)

    def kparts(ap, p=128):
        return ap.rearrange("(k p) n -> p k n", p=p)

    def checkpoint(name):
        if stop == name:
            raise StopBuild(nc)

    with ExitStack() as G:
        G.callback(P.emit)

        def sbg(name, shape, dt=F32):
            return G.enter_context(SB(name, shape, dt))
        psT = G.enter_context(nc.psum_tensor("psT", [128, 1024], F32))
        psZ = [G.enter_context(nc.psum_tensor(f"psZ{i}", [128, 512], F32)) for i in range(4)]
        psO = G.enter_context(nc.psum_tensor("psO", [128, 512], F32))
        psB = G.enter_context(nc.psum_tensor("psB", [128, 1024], BF16))
        ident = sbg("ident", [128, 128]); identb = sbg("identb", [128, 128], BF16)
        tri = sbg("tri", [128, 128], BF16); band = sbg("band", [128, 128], BF16)
        onesf = sbg("onesf", [128, 128])
        epsb = sbg("epsb", [128, 1])
        modT = sbg("modT", [128, 48])
        gsA = sbg("gsA", [128, 8]); gsM = sbg("gsM", [128, 8])
        gl = sbg("gl", [128, 8 + 8 + 3 + 2 + 8])
        ssr = sbg("ssr", [128, 8])
        junk = sbg("junk", [128, 1024], BF16)
        gbc = sbg("gbc", [128, 1024])
        dg = sbg("dg", [128, 128])
        OS = ExitStack()
        oTn = OS.enter_context(SB("oTn", [128, 4, NO * 128], BF16))

        def dma(out_ap, in_ap, r=(), w=(), q='sp'):
            return P.dma(q, lambda e: e.dma_start(out=out_ap, in_=in_ap), r=r, w=w)

        def mm(out_ap, lhsT, rhs, start, stop, r=(), w=(), skip=False):
            if skip:
                return P.op('pe', lambda e: e.matmul(out_ap, lhsT=lhsT, rhs=rhs, start=start, stop=stop,
                                                     skip_group_check=True), r=r, w=w)
            return P.op('pe', lambda e: e.matmul(out_ap, lhsT=lhsT, rhs=rhs, start=start, stop=stop), r=r, w=w)

        def act(out_ap, in_ap, func, r=(), w=(), **kw):
            return P.op('act', lambda e: e.activation(out=out_ap, in_=in_ap, func=func, **kw), r=r, w=w)

        def evac(out_ap, in_ap, r=(), w=()):
            rr['ev'] += 1
            if EVAC_ACT_ONLY or rr['ev'] % 2:
                return P.op('act', lambda e: e.activation(out=out_ap, in_=in_ap, func=AF.Copy), r=r, w=w)
            return P.op('dve', lambda e: e.tensor_scalar(out=out_ap, in0=in_ap, scalar1=1.0, scalar2=None, op0=ALU.mult), r=r, w=w)

        def ts(eng, out_ap, in0, s1, s2, op0, op1=None, r=(), w=()):
            if op1 is None:
                return P.op(eng, lambda e: e.tensor_scalar(out=out_ap, in0=in0, scalar1=s1, scalar2=None, op0=op0), r=r, w=w)
            return P.op(eng, lambda e: e.tensor_scalar(out=out_ap, in0=in0, scalar1=s1, scalar2=s2, op0=op0, op1=op1), r=r, w=w)

        def tt(eng, out_ap, in0, in1, op, r=(), w=()):
            return P.op(eng, lambda e: e.tensor_tensor(out=out_ap, in0=in0, in1=in1, op=op), r=r, w=w)

        def stt(eng, out_ap, in0, scalar, in1, op0, op1, r=(), w=()):
            return P.op(eng, lambda e: e.scalar_tensor_tensor(out=out_ap, in0=in0, scalar=scalar, in1=in1, op0=op0, op1=op1), r=r, w=w)

        def memset(eng, ap, val, w=()):
            return P.op(eng, lambda e: e.memset(ap, val), w=w)

        def dump(name, ap, r):
            if name in dbg:
                dma(dbg[name], ap, r=r, w=['dbg_' + name], q='pool')

        dma(ident[:], I['ident'][:, :], w=['ident'])
        dma(identb[:], I['identb'][:, :], w=['identb'])
        dma(tri[:], I['tri'][:, :], w=['tri'])
        dma(band[:], I['band'][:, :], w=['band'])
        dma(gl[:, 0:8], I['gmix_l'][:, :], w=['gl'])
        dma(gl[:, 8:16], I['gmlp_l'][:, :], w=['gl'])
        dma(gl[:, 16:19], I['gcq_l'][:, :], w=['gl'])
        dma(gl[:, 19:21], I['gckv_l'][:, :], w=['gl'])
        dma(gl[:, 21:29], I['gfin_l'][:, :], w=['gl'])
        memset('pool', onesf[:], 1.0, w=['onesf'])
        memset('pool', epsb[:], EPS, w=['epsb'])

        with ExitStack() as es:
            wad = [es.enter_context(SB(f"wad{i}", [128, 8, 512], F32)) for i in range(2)]
            cT = es.enter_context(SB("cT", [128, 8], F32))
            bl = es.enter_context(SB("bl", [128, 48], F32))
            dma(cT[:], I['c_l'][:, :], w=['cT'])
            dma(bl[:], I['bada_l'][:, :], w=['bl'])
            wv = kparts(I['w_ada'])
            for piece in range(12):
                buf = wad[piece % 2]
                bn = f"wad{piece % 2}"
                dma(buf[:], wv[:, :, piece * 512:(piece + 1) * 512], w=[bn])
                for jj in range(4):
                    j = piece * 4 + jj
                    for k in range(8):
                        mm(psZ[0][:, j:j + 1], buf[:, k, jj * 128:(jj + 1) * 128], cT[:, k:k + 1],
                           k == 0, k == 7, r=[bn, 'cT'], w=['psZ0'])
            tt('dve', modT[:], psZ[0][:, 0:48], bl[:], ALU.add, r=['psZ0', 'bl'], w=['modT'])
            stt('dve', gsA[:], modT[:, 8:16], 1.0, gl[:, 0:8], ALU.add, ALU.mult, r=['modT', 'gl'], w=['gsA'])
            stt('dve', gsM[:], modT[:, 32:40], 1.0, gl[:, 8:16], ALU.add, ALU.mult, r=['modT', 'gl'], w=['gsM'])
            dump('modT', modT[:], ['modT'])
        P.barrier()
        checkpoint('p0')
        shA = modT[:, 0:8]
        shM = modT[:, 24:32]

        def make_hT(xin, xin_res, xn, xn_res, hT, hT_res, gs, sh, slot):
            hT_pre(xin, xin_res, xn, xn_res, slot)
            hT_post(hT, hT_res, gs, sh)

        def hT_pre(xin, xin_res, xn, xn_res, slot):
            act(junk[:], xin[:], AF.Square, r=[xin_res], w=['junk', f'ss{slot}'], accum_out=ssr[:, slot:slot + 1])
            act(ssr[:, slot + 2:slot + 3], ssr[:, slot:slot + 1], AF.Sqrt, r=[f'ss{slot}', 'epsb'], w=[f'sq{slot}'],
                scale=1.0 / D, bias=epsb[:, 0:1])
            P.op('dve', E('reciprocal', out=ssr[:, slot + 4:slot + 5], in_=ssr[:, slot + 2:slot + 3]),
                 r=[f'sq{slot}'], w=[f'rs{slot}'])
            rstd = ssr[:, slot + 4:slot + 5]
            act(xn[:, 0:512], xin[:, 0:512], AF.Identity, r=[xin_res, f'rs{slot}'], w=[xn_res + 'a'], scale=rstd)
            ts('dve', xn[:, 512:1024], xin[:, 512:1024], rstd, None, ALU.mult, r=[xin_res, f'rs{slot}'], w=[xn_res + 'b'])
            for k in range(8):
                hf = 'a' if k < 4 else 'b'
                P.op('pe', E('transpose', out=psT[:, k * 128:(k + 1) * 128], in_=xn[:, k * 128:(k + 1) * 128],
                                                      identity=ident[:]),
                     r=[xn_res + hf, 'ident'], w=['psT' + hf])

        def hT_post(hT, hT_res, gs, sh):
            for k in range(8):
                hf = 'a' if k < 4 else 'b'
                if k % 2 == 0:
                    ts('dve', hT[:, k, :], psT[:, k * 128:(k + 1) * 128], gs[:, k:k + 1], sh[:, k:k + 1], ALU.mult, ALU.add,
                       r=['psT' + hf, 'gs', 'modT'], w=[hT_res])
                else:
                    act(hT[:, k, :], psT[:, k * 128:(k + 1) * 128], AF.Identity, r=['psT' + hf, 'gs', 'modT'], w=[hT_res],
                        scale=gs[:, k:k + 1], bias=sh[:, k:k + 1])

        dq = []
        LA = 2

        def defer(fn):
            dq.append(fn)
            while len(dq) > LA:
                dq.pop(0)()

        def flush():
            while dq:
                dq.pop(0)()

        zring = {'i': 0, 'n': 4}

        def nextZ():
            zring['i'] = (zring['i'] + 1) % zring['n']
            return psZ[zring['i']], f"psZ{zring['i']}"

        for g in range(2):
            with ExitStack() as NS:
                def sbn(name, shape, dt=F32):
                    return NS.enter_context(SB(f"{name}_g{g}", shape, dt))
                zqT = sbn("zqT", [128, 2, NO * 128], BF16)
                ksA = sbn("ksA", [128, S], BF16)
                kwA = sbn("kwA", [128, S], BF16)
                vsA = sbn("vsA", [128, NT, 66], BF16)
                vwA = sbn("vwA", [128, NT, 66], BF16)
                sg = sbn("sg", [128, NO, 24])
                kcA = sbn("kcA", [128, 512], BF16)
                VCA = sbn("VCA", [128, 4, 194], BF16)
                memset('pool', vsA[:, :, 64:66], 1.0, w=['vsA'])
                memset('pool', vwA[:, :, 64:66], 1.0, w=['vwA'])
                memset('pool', kcA[:], 0.0, w=['kcA'])
                memset('pool', VCA[:, :, 192:194], 1.0, w=['VCA'])
                checkpoint('n_a')
                dma(VCA[:, :, 64:192], I['OVL'][:, :, :], w=['VCA'])
                checkpoint('n_b')
                with ExitStack() as PS:
                    def sbp(name, shape, dt=F32):
                        return PS.enter_context(SB(f"{name}_g{g}", shape, dt))
                    wn = sbp("wn", [128, 8, 664], BF16)
                    xr = [sbp(f"xr{i}", [128, 1024]) for i in range(2)]
                    hTr = [sbp(f"hTr{i}", [128, 8, 128], BF16) for i in range(2)]
                    kcT = sbp("kcT", [128, S], BF16)
                    vcT = sbp("vcT", [128, S], BF16)
                    w1 = sbp("w1", [64, 32, 256], BF16)
                    w2 = sbp("w2", [128, 2, 64], BF16)
                    peT = sbp("peT", [64, 32], BF16)
                    hb = sbp("hb", [128, 2])
                    hid = sbp("hid", [128, 2, 512], BF16)
                    wiv = kparts(I['w_in'])
                    colmap = [(0, C_Q + 256 * g, 256), (256, C_KC + 64 * g, 64), (320, C_VC + 64 * g, 64),
                              (384, C_KS + 64 * g, 64), (448, C_VS + 64 * g, 64), (512, C_KW + 64 * g, 64),
                              (576, C_VW + 64 * g, 64), (640, C_G, 24)]
                    for (d0, s0, n) in colmap:
                        dma(wn[:, :, d0:d0 + n], wiv[:, :, s0:s0 + n], w=['wn'], q='pool')
                    print("sbuf remaining after proj alloc", nc.sbuf_bytes_remaining, flush=True)
                    checkpoint('n_w')
                    for lt in range(NT):
                        if lt == 1:
                            checkpoint('n_t0')
                        if lt == 2:
                            checkpoint('n_t1')
                        sl = lt % 2
                        hT = hTr[sl]
                        hres = f"hT{sl}"

                        def nsa_pre(t2):
                            s2 = t2 % 2
                            xt2 = xr[s2]
                            xres2 = f"xr{s2}"
                            dma(xt2[:], I['xl'][t2 * 128:(t2 + 1) * 128, :], w=[xres2 + 'a', xres2 + 'b', xres2])
                            hT_pre(xt2, xres2, xt2, xres2, s2)
                        if lt == 0:
                            nsa_pre(0)
                        hT_post(hT, hres, gsA, shA)
                        tsl = slice(lt * 128, (lt + 1) * 128)
                        pz, pzn = nextZ()
                        for j, c0 in enumerate((256, 320, 384, 512)):
                            for k in range(8):
                                mm(pz[0:64, j * 128:(j + 1) * 128], wn[:, k, c0:c0 + 64], hT[:, k, :], k == 0, k == 7,
                                   r=['wn', hres], w=[pzn])
                        pv, pvn = nextZ()
                        for j, c0 in enumerate((448, 576)):
                            for k in range(8):
                                mm(pv[:, j * 64:(j + 1) * 64], hT[:, k, :], wn[:, k, c0:c0 + 64], k == 0, k == 7,
                                   r=['wn', hres], w=[pvn])
                        own = (lt % 2 == 1)
                        i = lt // 2
                        if own:
                            for n2 in range(2):
                                for k in range(8):
                                    mm(pv[:, 128 + n2 * 128:256 + n2 * 128], wn[:, k, n2 * 128:(n2 + 1) * 128], hT[:, k, :],
                                       k == 0, k == 7, r=['wn', hres], w=[pvn])
                            for k in range(8):
                                mm(pv[:, 384:408], hT[:, k, :], wn[:, k, 640:664], k == 0, k == 7, r=['wn', hres], w=[pvn])
                        if lt + 1 < NT:
                            nsa_pre(lt + 1)
                        evac(kcT[0:64, tsl], pz[0:64, 0:128], r=[pzn], w=['kcT'])
                        evac(vcT[0:64, tsl], pz[0:64, 128:256], r=[pzn], w=['vcT'])
                        evac(ksA[0:64, tsl], pz[0:64, 256:384], r=[pzn], w=['ksA'])
                        evac(kwA[0:64, tsl], pz[0:64, 384:512], r=[pzn], w=['kwA'])
                        evac(vsA[:, lt, 0:64], pv[:, 0:64], r=[pvn], w=['vsA'])
                        evac(vwA[:, lt, 0:64], pv[:, 64:128], r=[pvn], w=['vwA'])
                        if own:
                            evac(zqT[:, 0, i * 128:(i + 1) * 128], pv[:, 128:256], r=[pvn], w=['zqT'])
                            evac(zqT[:, 1, i * 128:(i + 1) * 128], pv[:, 256:384], r=[pvn], w=['zqT'])
                            act(sg[:, i, :], pv[:, 384:408], AF.Sigmoid, r=[pvn], w=['sg'])
                    checkpoint('n_tiles')
                    dma(ksA[64:69, :], I['KA'][:, :], w=['ksA'])
                    dma(ksA[69:128, :], I['EF'][:, :], w=['ksA'])
                    dma(kwA[64:69, :], I['KA'][:, :], w=['kwA'])
                    dma(kcA[64:69, :], I['KCA'][:, :], w=['kcA'])
                    checkpoint('n_aug')
                    for which in range(2):
                        src = kcT if which == 0 else vcT
                        srcn = 'kcT' if which == 0 else 'vcT'
                        w1d = I['w_ck1'] if which == 0 else I['w_cv1']
                        w2d = I['w_ck2'] if which == 0 else I['w_cv2']
                        ped = I['peck_t'] if which == 0 else I['pecv_t']
                        dma(w1[:], w1d.rearrange("(l d) h -> d l h", d=64), w=['w1'], q='pool')
                        dma(w2[:], kparts(w2d), w=['w2'], q='pool')
                        dma(peT[:], ped[:, :], w=['peT'], q='pool')
                        pz, pzn = nextZ()
                        for hc in range(2):
                            for l in range(32):
                                mm(pz[:, hc:hc + 1], w1[:, l, hc * 128:(hc + 1) * 128], peT[:, l:l + 1], l == 0, l == 31,
                                   r=['w1', 'peT'], w=[pzn])
                        evac(hb[:], pz[:, 0:2], r=[pzn], w=['hb'])
                        memset('pool', hid[:], 0.0, w=['hid'])
                        for hc in range(2):
                            pz, pzn = nextZ()
                            for l in range(32):
                                mm(pz[:, 0:511], w1[:, l, hc * 128:(hc + 1) * 128], src[0:64, l:l + 16 * 510 + 1:16],
                                   l == 0, l == 31, r=['w1', srcn], w=[pzn])
                            act(hid[:, hc, 0:511], pz[:, 0:511], AF.Silu, r=[pzn, 'hb'], w=['hid'], bias=hb[:, hc:hc + 1])
                        if which == 0:
                            pz, pzn = nextZ()
                            for hc in range(2):
                                mm(pz[0:64, 0:511], w2[:, hc, :], hid[:, hc, 0:511], hc == 0, hc == 1, r=['w2', 'hid'], w=[pzn])
                            evac(kcA[0:64, 0:511], pz[0:64, 0:511], r=[pzn], w=['kcA'])
                        else:
                            pz, pzn = nextZ()
                            for ct in range(4):
                                for hc in range(2):
                                    mm(pz[:, ct * 64:(ct + 1) * 64], hid[:, hc, ct * 128:(ct + 1) * 128], w2[:, hc, :],
                                       hc == 0, hc == 1, r=['w2', 'hid'], w=[pzn])
                            evac(VCA[:, :, 0:64], pz[:, 0:256].rearrange("p (a b) -> p a b", b=64), r=[pzn], w=['VCA'])
                    if g == 0:
                        dump('kcA', kcA[0:64, :], ['kcA'])
                        dump('ksA', ksA[0:64, 0:1024], ['ksA'])
                        dump('zqT', zqT[:, 0, 0:512], ['zqT'])
                P.barrier()
                checkpoint(f'nproj{g}')
                with ExitStack() as AS:
                    def sba(name, shape, dt=F32):
                        return AS.enter_context(SB(f"{name}_g{g}", shape, dt))
                    QA = [sba(f"QA{h}", [128, NO * 128], BF16) for h in range(4)]
                    PT = [sba(f"PT{i}", [128, 512], BF16) for i in range(4)]
                    PTw = [sba(f"PTw{i}", [128, 640], BF16) for i in range(3)]
                    Zm4 = [[sba(f"Zm{q_}_{i}", [128, 128]) for i in range(3)] for q_ in range(4)]
                    late_tr = []
                    RqAll = [[sba(f"Rq{p_}_{i}", [128, 512], BF16) for i in range(3)] for p_ in range(2)]
                    ocmpAll = [sba(f"ocmp{p_}", [128, 4, 4, 64]) for p_ in range(2)]
                    ostage = sba("ostage", [128, NO, 256], BF16)
                    cmask = sba("cmask", [128, 8, 128], BF16)
                    bon = [sba(f"bon{i}", [128, 128]) for i in range(2)]
                    pslc = sba("pslc", [128, 128])
                    scb = sba("scb", [128, 128])
                    mrb = sba("mrb", [128, 128])
                    mx = sba("mx", [128, 16])
                    negm = sba("negm", [128, 128])
                    sm = sba("sm", [128, 64])
                    t1b = sba("t1b", [128, 4, 64])
                    zring['n'] = 3
                    for q_ in range(4):
                        for r_ in range(3):
                            memset('pool', Zm4[q_][r_][:], 0.0, w=[f'Zm{q_}_{r_}'])
                    dma(cmask[:], I['cmask'][:, :, :], w=['cmask'])
                    for hg in range(4):
                        h = 4 * g + hg
                        r0 = (hg % 2) * 64
                        P.op('pool', E('tensor_copy', out=QA[hg][0:64, :], in_=zqT[r0:r0 + 64, hg // 2, :]),
                             r=['zqT'], w=[f'QA{hg}'])
                        dma(QA[hg][64:69, :], I['QAh'][h, :, :], w=[f'QA{hg}'])
                    pti = {'i': 0, 'w': 0}
                    def cmp_topk(c):
                        for qi in range(4):
                            i = 4 * c + qi
                            qt = 2 * i + 1
                            qsl = slice(i * 128, (i + 1) * 128)
                            ctm = (qt - 1) // 16
                            bsl = i % 2
                            dma(bon[bsl][:], I['bonus'][i, :, :], w=[f'bon{bsl}'])
                            for hg in range(4):
                                h = 4 * g + hg
                                pz, pzn = nextZ()
                                for ct in range(ctm + 1):
                                    m = qt - 16 * ct
                                    partial = m < 17
                                    mm(pz[:, ct * 128:(ct + 1) * 128], kcA[0:69, ct * 128:(ct + 1) * 128], QA[hg][0:69, qsl],
                                       True, not partial, r=['kcA', f'QA{hg}'], w=[pzn])
                                    if partial:
                                        mm(pz[:, ct * 128:(ct + 1) * 128], identb[:], cmask[:, (m - 1) // 2, :], False, True,
                                           r=['identb', 'cmask'], w=[pzn])
                                ptc = PT[pti['i'] % 4]
                                ptn = f"PT{pti['i'] % 4}"
                                pti['i'] += 1
                                ncol = (ctm + 1) * 128
                                act(ptc[:, 0:ncol], pz[:, 0:ncol], AF.Exp, r=[pzn], w=[ptn], scale=SC_NSA)
                                def cmpB(hg=hg, h=h, i=i, qi=qi, ctm=ctm, ptc=ptc, ptn=ptn):
                                    ub = 'psTa' if hg < 2 else 'psTb'
                                    uo = hg * 256
                                    for ct in range(ctm + 1):
                                        mm(psT[:, uo:uo + 193], ptc[:, ct * 128:(ct + 1) * 128], VCA[:, ct, 0:193],
                                           ct == 0, ct == ctm, r=[ptn, 'VCA'], w=[ub])
                                    ts('dve', sm[:, hg:hg + 1], psT[:, uo + 192:uo + 193], 1e-30, None, ALU.max, r=[ub], w=[f'sm{hg}'])
                                    P.op('dve', E('reciprocal', out=sm[:, 4 + hg:5 + hg], in_=sm[:, hg:hg + 1]),
                                         r=[f'sm{hg}'], w=[f'smr{hg}'])
                                    if hg == 0:
                                        ts('dve', pslc[:], psT[:, uo + 64:uo + 192], sm[:, 4 + hg:5 + hg], None, ALU.mult,
                                           r=[ub, f'smr{hg}'], w=['pslc'])
                                    else:
                                        stt('dve', pslc[:], psT[:, uo + 64:uo + 192], sm[:, 4 + hg:5 + hg], pslc[:], ALU.mult, ALU.add,
                                            r=[ub, f'smr{hg}', 'pslc'], w=['pslc'])
                                    tt('dve', sm[:, 8 + hg:9 + hg], sm[:, 4 + hg:5 + hg], sg[:, i, h:h + 1], ALU.mult,
                                       r=[f'smr{hg}', 'sg'], w=[f'smg{hg}'])
                                    ts('dve', ocmpAll[c % 2][:, qi, hg, :], psT[:, uo:uo + 64], sm[:, 8 + hg:9 + hg], None, ALU.mult,
                                       r=[ub, f'smg{hg}'], w=[f'ocmp{c % 2}'])
                                defer(cmpB)
                            flush()
                            tt('dve', scb[:], pslc[:], bon[bsl][:], ALU.add, r=['pslc', f'bon{bsl}'], w=['scb'])
                            P.op('dve', E('max', out=mx[:, 0:8], in_=scb[:]), r=['scb'], w=['mx0'])
                            P.op('dve', E('match_replace', out=mrb[:], in_to_replace=mx[:, 0:8], in_values=scb[:],
                                                                  imm_value=-3e9), r=['scb', 'mx0'], w=['mrb'])
                            P.op('dve', E('max', out=mx[:, 8:16], in_=mrb[:]), r=['mrb'], w=['mx1'])
                            for r_ in range((2 * qt + 1) // 58 + 1):
                                nb = min(58, 128 - 58 * r_)
                                ts('dve', Zm4[qi][r_][:, 69:69 + nb], scb[:, 58 * r_:58 * r_ + nb], mx[:, 15:16], NEG, ALU.is_lt, ALU.mult,
                                   r=['scb', 'mx1'], w=[f'Zm{qi}_{r_}'])
                                def trB(r_=r_, qi=qi, c=c, zt=Zm4[qi][r_], ztn=f'Zm{qi}_{r_}'):
                                    pz, pzn = nextZ()
                                    P.op('pe', E('transpose', out=pz[:, 0:128], in_=zt[:], identity=ident[:]),
                                         r=[ztn, 'ident'], w=[pzn])
                                    act(RqAll[c % 2][r_][64:128, qi * 128:(qi + 1) * 128], pz[64:128, 0:128], AF.Copy, r=[pzn],
                                        w=[f'Rq{c % 2}_{r_}'])
                                late_tr.append(trB)
                            if g == 0 and c == 1 and qi == 0:
                                dump('pslc', pslc[:], ['pslc'])
                                dump('negm', Zm4[0][0][:], ['Zm0_0'])
                    def run_late():
                        while late_tr:
                            late_tr.pop(0)()
                    cmp_topk(0)
                    run_late()
                    for c in range(8):
                        for hg in range(4):
                            h = 4 * g + hg
                            if hg == 1 and c + 1 < 8:
                                cmp_topk(c + 1)
                            if hg == 3:
                                run_late()
                            for qi in range(4):
                                i = 4 * c + qi
                                qt = 2 * i + 1
                                qsl = slice(i * 128, (i + 1) * 128)
                                kts = [kt for kt in range(qt - 4, qt + 1) if kt >= 0]
                                pw = PTw[pti['w'] % 3]
                                pwn = f"PTw{pti['w'] % 3}"
                                pti['w'] += 1
                                pzA, pzAn = nextZ()
                                pzB, pzBn = nextZ()
                                for j, kt in enumerate(kts):
                                    ksl = slice(kt * 128, (kt + 1) * 128)
                                    last = (kt == qt)
                                    dst = pzB[:, 0:128] if last else pzA[:, j * 128:(j + 1) * 128]
                                    dn = pzBn if last else pzAn
                                    masked = last or (kt == qt - 4)
                                    mm(dst, kwA[0:69, ksl], QA[hg][0:69, qsl], True, not masked, r=['kwA', f'QA{hg}'], w=[dn])
                                    if masked:
                                        mm(dst, identb[:], tri[:] if last else band[:], False, True, r=['identb', 'tri', 'band'], w=[dn])
                                na = len(kts) - 1
                                if na > 0:
                                    act(pw[:, 0:na * 128], pzA[:, 0:na * 128], AF.Exp, r=[pzAn], w=[pwn + 'a'], scale=SC_NSA)
                                act(pw[:, 512:640], pzB[:, 0:128], AF.Exp, r=[pzBn], w=[pwn + 'b'], scale=SC_NSA)
                                def winB(kts=kts, qt=qt, qi=qi, pw=pw, pwn=pwn):
                                    for j, kt in enumerate(kts):
                                        last = (kt == qt)
                                        src = pw[:, 512:640] if last else pw[:, j * 128:(j + 1) * 128]
                                        mm(psO[:, qi * 128:qi * 128 + 65], src, vwA[:, kt, 0:65], (j == 0 and qi == 0), last,
                                           r=[pwn + 'a', pwn + 'b', 'vwA'], w=['psO'], skip=True)
                                defer(winB)
                            for r_ in range((16 * c + 15) // 58 + 1):
                                P.op('pool', E('tensor_copy', out=RqAll[c % 2][r_][0:69, :], in_=QA[hg][0:69, c * 512:(c + 1) * 512]),
                                     r=[f'QA{hg}'], w=[f'Rq{c % 2}_{r_}'])
                            ktmax = 8 * c + 7
                            for kt in range(ktmax + 1):
                                qmin = max(0, (kt - (8 * c + 1) + 1) // 2)
                                cs = slice(qmin * 128, 512)
                                qs = slice(c * 512 + qmin * 128, (c + 1) * 512)
                                ksl = slice(kt * 128, (kt + 1) * 128)
                                diag = (kt % 2 == 1) and (kt >= 8 * c + 1)
                                pz, pzn = nextZ()
                                if pzn == 'psZ3':
                                    pz, pzn = nextZ()
                                rr_ = (2 * kt) // 58
                                mm(pz[:, cs], ksA[:, ksl], RqAll[c % 2][rr_][:, cs], True, not diag, r=['ksA', f'Rq{c % 2}_{rr_}'], w=[pzn])
                                if diag:
                                    qd = (kt - (8 * c + 1)) // 2
                                    mm(pz[:, qd * 128:(qd + 1) * 128], identb[:], tri[:], False, True, r=['identb', 'tri'], w=[pzn])
                                ptc = PT[pti['i'] % 4]
                                ptn = f"PT{pti['i'] % 4}"
                                pti['i'] += 1
                                act(ptc[:, cs], pz[:, cs], AF.Exp, r=[pzn], w=[ptn], scale=SC_NSA)
                                def selB(kt=kt, qmin=qmin, ptc=ptc, ptn=ptn, c=c):
                                    for qi in range(qmin, 4):
                                        mm(psZ[3][:, qi * 128:qi * 128 + 65], ptc[:, qi * 128:(qi + 1) * 128], vsA[:, kt, 0:65],
                                           (kt == 0 and qi == 0), kt == 8 * c + 1 + 2 * qi, r=[ptn, 'vsA'], w=['psZ3'], skip=True)
                                defer(selB)
                            def finB(c=c, hg=hg, h=h):
                                i0 = 4 * c
                                ts('dve', sm[:, 16:20], psO[:, 64:512:128], 1e-30, None, ALU.max, r=['psO'], w=['smw'])
                                P.op('dve', E('reciprocal', out=sm[:, 20:24], in_=sm[:, 16:20]), r=['smw'], w=['smwr'])
                                tt('dve', sm[:, 24:28], sm[:, 20:24], sg[:, i0:i0 + 4, 16 + h], ALU.mult, r=['smwr', 'sg'], w=['smwm'])
                                ts('dve', sm[:, 28:32], psZ[3][:, 64:512:128], 1e-30, None, ALU.max, r=['psZ3'], w=['sms'])
                                P.op('dve', E('reciprocal', out=sm[:, 32:36], in_=sm[:, 28:32]), r=['sms'], w=['smsr'])
                                tt('dve', sm[:, 36:40], sm[:, 32:36], sg[:, i0:i0 + 4, 8 + h], ALU.mult, r=['smsr', 'sg'], w=['smsm'])
                                for qi in range(4):
                                    stt('dve', t1b[:, qi, :], psO[:, qi * 128:qi * 128 + 64], sm[:, 24 + qi:25 + qi], ocmpAll[c % 2][:, qi, hg, :],
                                        ALU.mult, ALU.add, r=['psO', 'smwm', f'ocmp{c % 2}'], w=['t1b'])
                                    stt('dve', ostage[:, i0 + qi, hg * 64:(hg + 1) * 64], psZ[3][:, qi * 128:qi * 128 + 64],
                                        sm[:, 36 + qi:37 + qi], t1b[:, qi, :], ALU.mult, ALU.add, r=['psZ3', 'smsm', 't1b'], w=['ostage'])
                            defer(finB)
                    flush()
                    if g == 0:
                        dump('ostage', ostage[:, 0:4, :], ['ostage'])
                    zring['n'] = 4
                    for f in range(2):
                        for i8 in range(4):
                            for j in range(8):
                                i = i8 * 8 + j
                                P.op('pe', E('transpose', out=psB[:, j * 128:(j + 1) * 128],
                                                                                in_=ostage[:, i, f * 128:(f + 1) * 128],
                                                                                identity=identb[:]),
                                     r=['ostage', 'identb'], w=['psB'])
                            evac(oTn[:, 2 * g + f, i8 * 1024:(i8 + 1) * 1024], psB[:, :], r=['psB'], w=['oTn'])
                P.barrier()

        checkpoint('nsa')
        oTm = OS.enter_context(SB("oTm", [128, 4, NO * 128], BF16))
        with ExitStack() as MS:
            def sbm(name, shape, dt=F32):
                return MS.enter_context(SB(name, shape, dt))
            ckvT = sbm("ckvT", [128, 2, S], BF16)
            cqT = sbm("cqT", [128, 3, NO * 128], BF16)
            KT = sbm("KT", [128, S], BF16)
            with ExitStack() as PS:
                def sbp(name, shape, dt=F32):
                    return PS.enter_context(SB(name, shape, dt))
                wm = sbp("wm", [128, 8, 640], BF16)
                wkrA = sbp("wkrA", [128, 8, 96], BF16)
                wkrB = sbp("wkrB", [128, 8, 96], BF16)
                xr = [sbp(f"mxr{i}", [128, 1024]) for i in range(2)]
                hTr = [sbp(f"mhTr{i}", [128, 8, 128], BF16) for i in range(2)]
                zf = sbp("zf", [128, 5, 128])
                zs = sbp("zs", [128, 5, 128])
                rcb = sbp("rcb", [128, 2, 128])
                ctab = [sbp(f"ctab{i}", [128, 2, 128]) for i in range(2)]
                rt = sbp("rt", [128, 2, 128])
                wiv = kparts(I['w_in'])
                dma(wm[:, :, 0:384], wiv[:, :, C_QD:C_QD + 384], w=['wm'], q='pool')
                dma(wm[:, :, 384:640], wiv[:, :, C_KVD:C_KVD + 256], w=['wm'], q='pool')
                memset('pool', wkrA[:], 0.0, w=['wkrA'])
                memset('pool', wkrB[:], 0.0, w=['wkrB'])
                dma(wkrA[:, :, 64:96], wiv[:, :, C_KR:C_KR + 32], w=['wkrA'], q='pool')
                dma(wkrB[:, :, 80:96], wiv[:, :, C_KR:C_KR + 16], w=['wkrB'], q='pool')
                dma(wkrB[:, :, 64:80], wiv[:, :, C_KR + 16:C_KR + 32], w=['wkrB'], q='pool')
                ts('pool', wkrB[:, :, 64:80], wkrB[:, :, 64:80], -1.0, None, ALU.mult, r=['wkrB'], w=['wkrB'])
                dma(KT[96:97, :], I['KA'][4:5, :], w=['KT'])
                for lt in range(NT):
                    sl = lt % 2
                    hT = hTr[sl]
                    hres = f"mhT{sl}"

                    def m1_pre(t2):
                        s2 = t2 % 2
                        xt2 = xr[s2]
                        xres2 = f"mxr{s2}"
                        dma(xt2[:], I['xl'][t2 * 128:(t2 + 1) * 128, :], w=[xres2 + 'a', xres2 + 'b', xres2])
                        hT_pre(xt2, xres2, xt2, xres2, s2)
                    if lt == 0:
                        m1_pre(0)
                    dma(ctab[sl][64:96, 0, :], I['cosk'][:, lt * 128:(lt + 1) * 128], w=[f'ctab{sl}'])
                    dma(ctab[sl][64:96, 1, :], I['sink'][:, lt * 128:(lt + 1) * 128], w=[f'ctab{sl}'])
                    hT_post(hT, hres, gsA, shA)
                    tsl = slice(lt * 128, (lt + 1) * 128)
                    own = (lt % 2 == 1)
                    i = lt // 2
                    ntl = [(384, 0), (512, 1)] + ([(0, 2), (128, 3), (256, 4)] if own else [])
                    pz, pzn = nextZ()
                    pz2, pz2n = nextZ()
                    for (c0, slot) in ntl:
                        dst = pz[:, slot * 128:(slot + 1) * 128] if slot < 4 else pz2[:, 0:128]
                        dn = pzn if slot < 4 else pz2n
                        for k in range(8):
                            mm(dst, wm[:, k, c0:c0 + 128], hT[:, k, :], k == 0, k == 7, r=['wm', hres], w=[dn])
                    for k in range(8):
                        mm(pz2[0:96, 128:256], wkrA[:, k, :], hT[:, k, :], k == 0, k == 7, r=['wkrA', hres], w=[pz2n])
                    for k in range(8):
                        mm(pz2[0:96, 256:384], wkrB[:, k, :], hT[:, k, :], k == 0, k == 7, r=['wkrB', hres], w=[pz2n])
                    if lt + 1 < NT:
                        m1_pre(lt + 1)
                    nsl = 5 if own else 2
                    for (c0, slot) in ntl:
                        srcp = pz[:, slot * 128:(slot + 1) * 128] if slot < 4 else pz2[:, 0:128]
                        sn = pzn if slot < 4 else pz2n
                        evac(zf[:, slot, :], srcp, r=[sn], w=[f'zf{slot}'])
                        act(zs[:, slot, :], srcp, AF.Square, r=[sn], w=[f'zs{slot}'])
                    pz3, pz3n = nextZ()
                    for j, slot in enumerate((0, 1)):
                        mm(pz3[:, 0:128], onesf[:], zs[:, slot, :], j == 0, j == 1, r=['onesf', f'zs{slot}'], w=[pz3n])
                    if own:
                        for j, slot in enumerate((2, 3, 4)):
                            mm(pz3[:, 128:256], onesf[:], zs[:, slot, :], j == 0, j == 2, r=['onesf', f'zs{slot}'], w=[pz3n])
                    act(rcb[:, 0, :], pz3[:, 0:128], AF.Sqrt, r=[pz3n, 'epsb'], w=['rcb0'], scale=1.0 / 256, bias=epsb[:, 0:1])
                    P.op('dve', E('reciprocal', out=rcb[:, 0, :], in_=rcb[:, 0, :]), r=['rcb0'], w=['rcb0'])
                    for slot in (0, 1):
                        tt('pool' if slot else 'dve', ckvT[:, slot, tsl], zf[:, slot, :], rcb[:, 0, :], ALU.mult,
                           r=[f'zf{slot}', 'rcb0'], w=['ckvT'])
                    if own:
                        act(rcb[:, 1, :], pz3[:, 128:256], AF.Sqrt, r=[pz3n, 'epsb'], w=['rcb1'], scale=1.0 / 384, bias=epsb[:, 0:1])
                        P.op('dve', E('reciprocal', out=rcb[:, 1, :], in_=rcb[:, 1, :]), r=['rcb1'], w=['rcb1'])
                        for slot in (2, 3, 4):
                            tt('pool' if slot % 2 else 'dve', cqT[:, slot - 2, i * 128:(i + 1) * 128], zf[:, slot, :], rcb[:, 1, :],
                               ALU.mult, r=[f'zf{slot}', 'rcb1'], w=['cqT'])
                    tt('dve', rt[64:96, 0, :], pz2[64:96, 128:256], ctab[sl][64:96, 0, :], ALU.mult, r=[pz2n, f'ctab{sl}'], w=['rt0'])
                    tt('dve', rt[64:96, 1, :], pz2[64:96, 256:384], ctab[sl][64:96, 1, :], ALU.mult, r=[pz2n, f'ctab{sl}'], w=['rt1'])
                    tt('pool', KT[64:96, tsl], rt[64:96, 0, :], rt[64:96, 1, :], ALU.add, r=['rt0', 'rt1'], w=['KT'])
                dump('ckvT', ckvT[:, 0, 0:1024], ['ckvT'])
                dump('cqT', cqT[:, 0, 0:512], ['cqT'])
                dump('krot', KT[64:96, 0:1024], ['KT'])
            P.barrier()
            checkpoint('m1')
            with ExitStack() as AS:
                def sba(name, shape, dt=F32):
                    return AS.enter_context(SB(name, shape, dt))
                wuq = sba("wuq", [128, 3, 768], BF16)
                wuqB = sba("wuqB", [128, 3, 768], BF16)
                wuk = sba("wuk", [128, 2, 512], BF16)
                wuv = sba("wuv", [128, 2, 512], BF16)
                VH = sba("VH", [128, NT, 66], BF16)
                QT = sba("QT", [128, NO * 128], BF16)
                PT = [sba(f"MPT{i}", [128, 512], BF16) for i in range(4)]
                ostage = sba("mostage", [128, NO, 128], BF16)
                qtab = [sba(f"qtab{i}", [128, 2, 512]) for i in range(2)]
                rt = sba("mrt", [128, 2, 512])
                sm = sba("msm", [128, 8])
                dma(wuq[:], kparts(I['w_uq']), w=['wuq'], q='pool')
                dma(wuk[:], kparts(I['w_uk']), w=['wuk'], q='pool')
                dma(wuv[:], kparts(I['w_uv']), w=['wuv'], q='pool')
                for k in range(3):
                    ts('pool', wuq[:, k, :], wuq[:, k, :], gl[:, 16 + k:17 + k], None, ALU.mult, r=['wuq', 'gl'], w=['wuq'])
                for k in range(2):
                    ts('pool', wuk[:, k, :], wuk[:, k, :], gl[:, 19 + k:20 + k], None, ALU.mult, r=['wuk', 'gl'], w=['wuk'])
                    ts('pool', wuv[:, k, :], wuv[:, k, :], gl[:, 19 + k:20 + k], None, ALU.mult, r=['wuv', 'gl'], w=['wuv'])
                memset('pool', wuqB[:], 0.0, w=['wuqB'])
                wq4 = wuq[:].rearrange("p k (h c) -> p k h c", c=96)
                wb4 = wuqB[:].rearrange("p k (h c) -> p k h c", c=96)
                for k in range(3):
                    ts('pool', wb4[:, k, :, 64:80], wq4[:, k, :, 80:96], -1.0, None, ALU.mult, r=['wuq'], w=['wuqB'])
                    P.op('pool', E('tensor_copy', out=wb4[:, k, :, 80:96], in_=wq4[:, k, :, 64:80]), r=['wuq'], w=['wuqB'])
                memset('pool', VH[:, :, 64:66], 1.0, w=['VH'])
                memset('pool', QT[96:97, :], NEG, w=['QT'])
                zring['n'] = 3
                pti = {'i': 0}
                hz = {'i': 0}

                def nextH():
                    hz['i'] = (hz['i'] + 1) % 2
                    return (psT[:, 0:512], 'psTa') if hz['i'] == 0 else (psT[:, 512:1024], 'psTb')

                for h in range(8):
                    for ch in range(16):
                        pz, pzn = nextH()
                        for k in range(2):
                            mm(pz[0:64, :], wuk[:, k, h * 64:(h + 1) * 64], ckvT[:, k, ch * 512:(ch + 1) * 512], k == 0, k == 1,
                               r=['wuk', 'ckvT'], w=[pzn])
                        evac(KT[0:64, ch * 512:(ch + 1) * 512], pz[0:64, :], r=[pzn], w=['KT'])
                    for t8 in range(8):
                        pz, pzn = nextH()
                        for j in range(8):
                            lt = t8 * 8 + j
                            for k in range(2):
                                mm(pz[:, j * 64:(j + 1) * 64], ckvT[:, k, lt * 128:(lt + 1) * 128], wuv[:, k, h * 64:(h + 1) * 64],
                                   k == 0, k == 1, r=['wuv', 'ckvT'], w=[pzn])
                        evac(VH[:, t8 * 8:(t8 + 1) * 8, 0:64], pz.rearrange("p (a b) -> p a b", b=64), r=[pzn], w=['VH'])
                    for c in range(8):
                        csl = slice(c * 512, (c + 1) * 512)
                        qs = c % 2
                        dma(qtab[qs][64:96, 0, :], I['cosq'][:, csl], w=[f'qtab{qs}'])
                        dma(qtab[qs][64:96, 1, :], I['sinq'][:, csl], w=[f'qtab{qs}'])
                        pzA, pzAn = nextH()
                        for k in range(3):
                            mm(pzA[0:96, :], wuq[:, k, h * 96:(h + 1) * 96], cqT[:, k, csl], k == 0, k == 2, r=['wuq', 'cqT'], w=[pzAn])
                        pzB, pzBn = nextH()
                        for k in range(3):
                            mm(pzB[0:96, :], wuqB[:, k, h * 96:(h + 1) * 96], cqT[:, k, csl], k == 0, k == 2, r=['wuqB', 'cqT'], w=[pzBn])
                        act(QT[0:64, csl], pzA[0:64, :], AF.Copy, r=[pzAn], w=['QT'])
                        tt('dve', rt[64:96, 0, :], pzA[64:96, :], qtab[qs][64:96, 0, :], ALU.mult, r=[pzAn, f'qtab{qs}'], w=['mrt0'])
                        tt('dve', rt[64:96, 1, :], pzB[64:96, :], qtab[qs][64:96, 1, :], ALU.mult, r=[pzBn, f'qtab{qs}'], w=['mrt1'])
                        tt('pool', QT[64:96, csl], rt[64:96, 0, :], rt[64:96, 1, :], ALU.add, r=['mrt0', 'mrt1'], w=['QT'])
                    if h == 0:
                        dump('QT', QT[0:96, 0:512], ['QT'])
                        dump('KT', KT[0:96, 0:1024], ['KT'])
                    for c in range(8):
                        ob, obn = (psO, 'psO') if c % 2 == 0 else (psZ[3], 'psZ3')
                        ktmax = 8 * c + 7
                        for kt in range(ktmax + 1):
                            qmin = max(0, (kt - (8 * c + 1) + 1) // 2)
                            cs = slice(qmin * 128, 512)
                            qs_ = slice(c * 512 + qmin * 128, (c + 1) * 512)
                            ksl = slice(kt * 128, (kt + 1) * 128)
                            diag = (kt % 2 == 1) and (kt >= 8 * c + 1)
                            pz, pzn = nextZ()
                            if pzn == 'psZ3':
                                pz, pzn = nextZ()
                            mm(pz[:, cs], KT[0:97, ksl], QT[0:97, qs_], True, not diag, r=['KT', 'QT'], w=[pzn])
                            if diag:
                                qd = (kt - (8 * c + 1)) // 2
                                mm(pz[:, qd * 128:(qd + 1) * 128], identb[:], tri[:], False, True, r=['identb', 'tri'], w=[pzn])
                            ptc = PT[pti['i'] % 4]
                            ptn = f"MPT{pti['i'] % 4}"
                            pti['i'] += 1
                            act(ptc[:, cs], pz[:, cs], AF.Exp, r=[pzn], w=[ptn], scale=SC_MLA)
                            def mlaB(kt=kt, qmin=qmin, ptc=ptc, ptn=ptn, c=c, ob=ob, obn=obn):
                                for qi in range(qmin, 4):
                                    mm(ob[:, qi * 128:qi * 128 + 65], ptc[:, qi * 128:(qi + 1) * 128], VH[:, kt, 0:65],
                                       (kt == 0 and qi == 0), kt == 8 * c + 1 + 2 * qi, r=[ptn, 'VH'], w=[obn], skip=True)
                            defer(mlaB)
                        def mlaF(c=c, h=h, ob=ob, obn=obn):
                            ts('dve', sm[:, 0:4], ob[:, 64:512:128], 1e-30, None, ALU.max, r=[obn], w=['msm0'])
                            P.op('dve', E('reciprocal', out=sm[:, 4:8], in_=sm[:, 0:4]), r=['msm0'], w=['msm1'])
                            for qi in range(4):
                                ts('dve', ostage[:, 4 * c + qi, (h % 2) * 64:(h % 2) * 64 + 64], ob[:, qi * 128:qi * 128 + 64],
                                   sm[:, 4 + qi:5 + qi], None, ALU.mult, r=[obn, 'msm1'], w=['mostage'])
                        defer(mlaF)
                    flush()
                    if h % 2 == 1:
                        if h == 1:
                            dump('mostage', ostage[:, 0:4, :], ['mostage'])
                        for i8 in range(4):
                            for j in range(8):
                                i = i8 * 8 + j
                                P.op('pe', E('transpose', out=psB[:, j * 128:(j + 1) * 128], in_=ostage[:, i, :],
                                                                           identity=identb[:]),
                                     r=['mostage', 'identb'], w=['psB'])
                            evac(oTm[:, h // 2, i8 * 1024:(i8 + 1) * 1024], psB[:, :], r=['psB'], w=['oTm'])
            P.barrier()

        zring['n'] = 4
        checkpoint('mla')
        def bcast_vec(col0):
            for j in range(8):
                pz, pzn = nextZ()
                src = col0(j)
                ts('dve', dg[:], ident[:], src, None, ALU.mult, r=['ident', 'modT', 'gl'], w=['dg'])
                mm(pz[:, 0:128], onesf[:], dg[:], True, True, r=['onesf', 'dg'], w=[pzn])
                evac(gbc[:, j * 128:(j + 1) * 128], pz[:, 0:128], r=[pzn], w=['gbc'])

        with ExitStack() as TS:
            def sbt(name, shape, dt=F32):
                return TS.enter_context(SB(name, shape, dt))
            wg = sbt("wg", [128, 8, 2048], BF16)
            won = sbt("won", [128, 4, 1024], BF16)
            wom = sbt("wom", [128, 4, 1024], BF16)
            wout = sbt("wout", [128, 8, 1024], BF16)
            xr = [sbt(f"txr{i}", [128, 1024]) for i in range(2)]
            xn = sbt("txn", [128, 1024])
            hTr = [sbt(f"thT{i}", [128, 8, 128], BF16) for i in range(2)]
            sgA = sbt("sgA", [128, 1024])
            sgB = sbt("sgB", [128, 1024])
            m1 = sbt("m1", [128, 1024])
            m2 = sbt("m2", [128, 1024])
            yT = sbt("yT", [128, 8, 128], BF16)
            yb = sbt("yb", [128, 1024], BF16)
            x1r = [sbt(f"x1r{i}", [128, 1024]) for i in range(2)]
            wiv = kparts(I['w_in'])
            dma(wg[:, :, 0:1024], wiv[:, :, C_GA:C_GA + 1024], w=['wg'], q='pool')
            dma(wg[:, :, 1024:2048], wiv[:, :, C_GB:C_GB + 1024], w=['wg'], q='pool')
            dma(won[:], kparts(I['w_o_nsa']), w=['won'], q='pool')
            dma(wom[:], kparts(I['w_o_mla']), w=['wom'], q='pool')
            dma(wout[:], kparts(I['w_out']), w=['wout'], q='pool')
            bcast_vec(lambda j: modT[:, 16 + j:17 + j])
            m3 = sbt("m3", [128, 1024])
            yT2 = sbt("yT2", [128, 8, 128], BF16)
            xr3 = sbt("txr2", [128, 1024])
            xr = xr + [xr3]
            yTs = [yT, yT2]

            def t1_pre(i2):
                s2 = i2 % 3
                dma(xr[s2][:], I['xo'][i2 * 128:(i2 + 1) * 128, :], w=[f"txr{s2}"])
                hT_pre(xr[s2], f"txr{s2}", xn, 'txn', i2 % 2)

            def t1_A(i):
                isl = slice(i * 128, (i + 1) * 128)
                hT = hTr[i % 2]
                hres = f"thT{i % 2}"
                for n in range(2):
                    pz, pzn = nextZ()
                    for k in range(8):
                        mm(pz[:, :], hT[:, k, :], wg[:, k, n * 512:(n + 1) * 512], k == 0, k == 7, r=['wg', hres], w=[pzn])
                    act(sgA[:, n * 512:(n + 1) * 512], pz[:, :], AF.Sigmoid, r=[pzn], w=['sgA'])
                if i + 1 < NO:
                    t1_pre(i + 1)
                for n in range(2):
                    pz, pzn = nextZ()
                    for k in range(4):
                        mm(pz[:, :], oTn[:, k, isl], won[:, k, n * 512:(n + 1) * 512], k == 0, k == 3, r=['won', 'oTn'], w=[pzn])
                    tt('dve', m1[:, n * 512:(n + 1) * 512], pz[:, :], sgA[:, n * 512:(n + 1) * 512], ALU.mult, r=[pzn, 'sgA'], w=['m1'])
                for n in range(2):
                    pz, pzn = nextZ()
                    for k in range(8):
                        mm(pz[:, :], hT[:, k, :], wg[:, k, 1024 + n * 512:1024 + (n + 1) * 512], k == 0, k == 7, r=['wg', hres], w=[pzn])
                    act(sgB[:, n * 512:(n + 1) * 512], pz[:, :], AF.Sigmoid, r=[pzn], w=['sgB'])
                if i + 1 < NO:
                    hT_post(hTr[(i + 1) % 2], f"thT{(i + 1) % 2}", gsA, shA)
                for n in range(2):
                    pz, pzn = nextZ()
                    for k in range(4):
                        mm(pz[:, :], oTm[:, k, isl], wom[:, k, n * 512:(n + 1) * 512], k == 0, k == 3, r=['wom', 'oTm'], w=[pzn])
                    tt('dve', m2[:, n * 512:(n + 1) * 512], pz[:, :], sgB[:, n * 512:(n + 1) * 512], ALU.mult, r=[pzn, 'sgB'], w=['m2'])
                tt('dve', yb[:], m1[:], m2[:], ALU.add, r=['m1', 'm2'], w=['yb'])
                for k in range(8):
                    P.op('pe', E('transpose', out=psB[:, k * 128:(k + 1) * 128], in_=yb[:, k * 128:(k + 1) * 128],
                                 identity=identb[:]), r=['yb', 'identb'], w=['psB'])
                yTc = yTs[i % 2]
                yv = yTc[:].rearrange("p k t -> p (k t)")
                act(yv[:, 0:512], psB[:, 0:512], AF.Copy, r=['psB'], w=[f'yT{i % 2}'])
                ts('dve', yv[:, 512:1024], psB[:, 512:1024], 1.0, None, ALU.mult, r=['psB'], w=[f'yT{i % 2}'])

            def t1_B(i):
                isl = slice(i * 128, (i + 1) * 128)
                yTc = yTs[i % 2]
                xt = xr[i % 3]
                x1 = x1r[i % 2]
                x1n = f"x1r{i % 2}"
                for n in range(2):
                    pz, pzn = nextZ()
                    for k in range(8):
                        mm(pz[:, :], yTc[:, k, :], wout[:, k, n * 512:(n + 1) * 512], k == 0, k == 7, r=['wout', f'yT{i % 2}'], w=[pzn])
                    tt('dve', m3[:, n * 512:(n + 1) * 512], pz[:, :], gbc[:, n * 512:(n + 1) * 512], ALU.mult, r=[pzn, 'gbc'], w=['m3'])
                tt('dve', x1[:], m3[:], xt[:], ALU.add, r=['m3', f"txr{i % 3}"], w=[x1n])
                dma(x1s[isl, :], x1[:], r=[x1n], w=['x1s'])
                if i == 0:
                    dump('x1', x1[:], [x1n])

            t1_pre(0)
            hT_post(hTr[0], "thT0", gsA, shA)
            for i in range(NO):
                t1_A(i)
                if i >= 1:
                    t1_B(i - 1)
            t1_B(NO - 1)
        P.barrier()
        OS.close()
        checkpoint('t1')

        with ExitStack() as TS:
            def sbt(name, shape, dt=F32):
                return TS.enter_context(SB(name, shape, dt))
            wf1 = sbt("wf1", [128, 8, 4096], BF16)
            wf2 = sbt("wf2", [128, 32, 1024], BF16)
            gfb = sbt("gfb", [128, 1024])
            xr = [sbt(f"uxr{i}", [128, 1024]) for i in range(2)]
            xn = sbt("uxn", [128, 1024])
            hTr = [sbt(f"uhT{i}", [128, 8, 128], BF16) for i in range(2)]
            aT = [sbt(f"aT{i}", [128, 32, 128], BF16) for i in range(2)]
            rl = [sbt(f"rl{i}", [128, 512]) for i in range(2)]
            o2 = sbt("o2", [128, 1024])
            tmp = sbt("utmp", [128, 1024])
            res = [sbt(f"ures{i}", [128, 1024]) for i in range(2)]
            fs = sbt("fs", [128, 4])
            wf1v = kparts(I['w_fc1'])
            wf2v = kparts(I['w_fc2'])
            for q4 in range(4):
                dma(wf1[:, :, q4 * 1024:(q4 + 1) * 1024], wf1v[:, :, q4 * 1024:(q4 + 1) * 1024], w=[f'wf1_{q4}'], q='pool')
            for q4 in range(4):
                dma(wf2[:, q4 * 8:(q4 + 1) * 8, :], wf2v[:, q4 * 8:(q4 + 1) * 8, :], w=[f'wf2_{q4}'], q='pool')
            bcast_vec(lambda j: gl[:, 21 + j:22 + j])
            P.op('pool', E('tensor_copy', out=gfb[:], in_=gbc[:]), r=['gbc'], w=['gfb'])
            bcast_vec(lambda j: modT[:, 40 + j:41 + j])
            def t2_pre(i2):
                s2 = i2 % 2
                dma(xr[s2][:], x1s[i2 * 128:(i2 + 1) * 128, :], r=['x1s'], w=[f"uxr{s2}"])
                hT_pre(xr[s2], f"uxr{s2}", xn, 'uxn', s2)

            t2_pre(0)
            hT_post(hTr[0], "uhT0", gsM, shM)
            for i in range(NO):
                sl = i % 2
                xt = xr[sl]
                xres = f"uxr{sl}"
                isl = slice(i * 128, (i + 1) * 128)
                hT = hTr[sl]
                hres = f"uhT{sl}"
                a = aT[sl]
                an = f"aT{sl}"
                for jg in range(8):
                    pz, pzn = (psZ[jg % 2], f"psZ{jg % 2}")
                    for jj in range(4):
                        j = jg * 4 + jj
                        for k in range(8):
                            mm(pz[:, jj * 128:(jj + 1) * 128], wf1[:, k, j * 128:(j + 1) * 128], hT[:, k, :], k == 0, k == 7,
                               r=[f'wf1_{jg // 2}', hres], w=[pzn])
                    rb = rl[jg % 2]
                    rbn = f"rl{jg % 2}"
                    act(rb[:], pz[:, :], AF.Relu, r=[pzn], w=[rbn])
                    av = a[:, jg * 4:(jg + 1) * 4, :].rearrange("p a b -> p (a b)")
                    tt('pool', av, rb[:], rb[:], ALU.mult, r=[rbn], w=[an])
                if i + 1 < NO:
                    t2_pre(i + 1)
                for n in range(2):
                    pz, pzn = (psZ[2 + n], f"psZ{2 + n}")
                    for j in range(32):
                        mm(pz[:, :], a[:, j, :], wf2[:, j, n * 512:(n + 1) * 512], j == 0, j == 31, r=[f'wf2_{j // 8}', an], w=[pzn])
                if i + 1 < NO:
                    hT_post(hTr[(i + 1) % 2], f"uhT{(i + 1) % 2}", gsM, shM)
                for n in range(2):
                    pz, pzn = (psZ[2 + n], f"psZ{2 + n}")
                    tt('dve', tmp[:, n * 512:(n + 1) * 512], pz[:, :], gbc[:, n * 512:(n + 1) * 512], ALU.mult, r=[pzn, 'gbc'], w=['utmp'])
                tt('dve', o2[:], tmp[:], xt[:], ALU.add, r=['utmp', xres], w=['o2'])
                act(junk[:], o2[:], AF.Square, r=['o2'], w=['junk', 'fs0'], accum_out=fs[:, 0:1])
                act(fs[:, 1:2], fs[:, 0:1], AF.Sqrt, r=['fs0', 'epsb'], w=['fs1'], scale=1.0 / D, bias=epsb[:, 0:1])
                P.op('dve', E('reciprocal', out=fs[:, 2:3], in_=fs[:, 1:2]), r=['fs1'], w=['fs2'])
                rs_ = res[sl]
                rn = f"ures{sl}"
                stt('dve', rs_[:], o2[:], fs[:, 2:3], gfb[:], ALU.mult, ALU.mult, r=['o2', 'fs2', 'gfb'], w=[rn])
                dma(out[isl, :], rs_[:], r=[rn], w=['out'])
    return nc


_CACHE = {}


def _lay(v, k):
    return np.ascontiguousarray(np.asarray(v, np.float32).reshape(k, 128).T)


def make_in_maps(inputs, ncores=8):
    x = np.asarray(inputs['x'], np.float32)
    shared = {
        'w_ada': np.ascontiguousarray(inputs['w_ada'][0], dtype=np.float32),
        'bada_l': _lay(inputs['b_ada'][0], 48),
        'gmix_l': _lay(inputs['g_mix'][0], 8), 'gmlp_l': _lay(inputs['g_mlp'][0], 8),
        'gcq_l': _lay(inputs['g_cq'][0], 3), 'gckv_l': _lay(inputs['g_ckv'][0], 2),
        'gfin_l': _lay(inputs['g_final'], 8),
        'w_in': np.ascontiguousarray(inputs['w_in'][0], dtype=np.float32),
        'peck_t': np.ascontiguousarray(np.asarray(inputs['pe_ck'][0], np.float32).T),
        'w_ck1': np.ascontiguousarray(inputs['w_ck1'][0], dtype=np.float32),
        'w_ck2': np.ascontiguousarray(inputs['w_ck2'][0], dtype=np.float32),
        'pecv_t': np.ascontiguousarray(np.asarray(inputs['pe_cv'][0], np.float32).T),
        'w_cv1': np.ascontiguousarray(inputs['w_cv1'][0], dtype=np.float32),
        'w_cv2': np.ascontiguousarray(inputs['w_cv2'][0], dtype=np.float32),
        'w_uq': np.ascontiguousarray(inputs['w_uq'][0], dtype=np.float32),
        'w_uk': np.ascontiguousarray(inputs['w_uk'][0], dtype=np.float32),
        'w_uv': np.ascontiguousarray(inputs['w_uv'][0], dtype=np.float32),
        'w_o_nsa': np.ascontiguousarray(inputs['w_o_nsa'][0], dtype=np.float32),
        'w_o_mla': np.ascontiguousarray(inputs['w_o_mla'][0], dtype=np.float32),
        'w_out': np.ascontiguousarray(inputs['w_out'][0], dtype=np.float32),
        'w_fc1': np.ascontiguousarray(inputs['w_fc1'][0], dtype=np.float32),
        'w_fc2': np.ascontiguousarray(inputs['w_fc2'][0], dtype=np.float32),
    }
    consts = [host_consts(0), host_consts(1)]
    in_maps = []
    for core in range(ncores):
        b, p = core // 2, core % 2
        if p == 1:
            xl = np.ascontiguousarray(x[b])
        else:
            xl = np.concatenate([np.zeros((128, D), np.float32), x[b, :S - 128]], axis=0)
        xo = np.ascontiguousarray(xl.reshape(NT, 128, D)[1::2].reshape(NO * 128, D))
        m = dict(shared)
        m['xl'] = xl
        m['xo'] = xo
        m['c_l'] = _lay(np.asarray(inputs['c'], np.float32)[b], 8)
        m.update(consts[p])
        in_maps.append(m)
    return in_maps


def assemble(results, ncores=8):
    outp = np.zeros((4, S, D), np.float32)
    for core in range(ncores):
        b, p = core // 2, core % 2
        o = np.asarray(results[core]['out'], np.float32).reshape(NO, 128, D)
        outp[b].reshape(NT, 128, D)[p::2] = o
    return outp


def kernel(**inputs):
    if 'nc' not in _CACHE:
        _CACHE['nc'] = build_program()
    nc = _CACHE['nc']
    in_maps = make_in_maps(inputs)
    res = run_bass_kernel_spmd(nc, in_maps, core_ids=list(range(8)))
    return assemble(res.results)
```

```python
import numpy as np
import ml_dtypes
from contextlib import ExitStack
import concourse.bass as bass
import concourse.mybir as mybir
from concourse.bass_utils import run_bass_kernel_spmd

F32 = mybir.dt.float32
BF16 = mybir.dt.bfloat16
AF = mybir.ActivationFunctionType
ALU = mybir.AluOpType
NPBF = ml_dtypes.bfloat16

D = 1024
S = 8192
NT = 64
NO = 32
NEG = -30000.0
EPS = 1e-6
GEN = 8192
EVAC_ACT_ONLY = True
NDSEM = 12
SC_NSA = 0.125
SC_MLA = 96.0 ** -0.5
C_Q, C_KC, C_VC, C_KS, C_VS, C_KW, C_VW, C_G, C_QD, C_KVD, C_KR, C_GA, C_GB = (
    0, 512, 640, 768, 896, 1024, 1152, 1280, 1304, 1688, 1944, 1976, 3000)


class Prog:
    ENGS = ('pe', 'act', 'dve', 'pool', 'sp')

    def __init__(self, nc):
        self.nc = nc
        self.ops = {e: [] for e in self.ENGS}
        self.cnt = {e: 0 for e in self.ENGS}
        self.lastw = {}
        self.readers = {}
        self.dcnt = {}
        self.dnext = {e: 0 for e in self.ENGS}
        self.floor = {}

    def _deps(self, r, w):
        deps = dict(self.floor)

        def add(tok):
            if tok is None:
                return
            k, v = tok
            if deps.get(k, 0) < v:
                deps[k] = v
        for x in r:
            add(self.lastw.get(x))
        for x in w:
            add(self.lastw.get(x))
            for t in self.readers.get(x, ()):
                add(t)
        return deps

    def _commit(self, tok, r, w):
        for x in r:
            self.readers.setdefault(x, []).append(tok)
        for x in w:
            self.lastw[x] = tok
            self.readers[x] = []

    def barrier(self):
        fl = {}
        for e in self.ENGS:
            c = self.cnt[e]
            if c > 0:
                fl[(e, (c - 1) // GEN)] = (c - 1) % GEN + 1
        for key, n in self.dcnt.items():
            fl[key] = 16 * n
        self.floor = fl
        self.lastw = {}
        self.readers = {}

    def op(self, eng, fn, r=(), w=()):
        deps = self._deps(r, w)
        idx = self.cnt[eng]
        self.cnt[eng] += 1
        tok = ((eng, idx // GEN), idx % GEN + 1)
        if eng == 'pe':
            deps = {k: v for k, v in deps.items() if k[0] != 'pe'}
        self.ops[eng].append(('c', fn, deps, tok))
        self._commit(tok, r, w)
        return tok

    def dma(self, q, fn, r=(), w=()):
        deps = self._deps(r, w)
        slot = self.dnext[q] % NDSEM
        self.dnext[q] += 1
        key = ('dma_' + q, slot)
        n = self.dcnt.get(key, 0)
        if n > 0 and deps.get(key, 0) < 16 * n:
            deps[key] = 16 * n
        self.dcnt[key] = n + 1
        tok = (key, 16 * (n + 1))
        self.ops[q].append(('d', fn, deps, tok))
        self._commit(tok, r, w)
        return tok

    def emit(self):
        nc = self.nc
        with ExitStack() as es:
            sems = {}
            for e in self.ENGS:
                for g in range((self.cnt[e] + GEN - 1) // GEN):
                    sems[(e, g)] = es.enter_context(nc.semaphore(f"s_{e}_{g}"))
            for key in self.dcnt:
                sems[key] = es.enter_context(nc.semaphore(f"s_{key[0]}_{key[1]}"))
            block = es.enter_context(nc.Block())
            engobj = {'pe': 'tensor', 'act': 'scalar', 'dve': 'vector', 'pool': 'gpsimd', 'sp': 'sync'}

            def make(ename):
                def body(e):
                    waited = {}
                    for kind, fn, deps, tok in self.ops[ename]:
                        for k, v in deps.items():
                            if waited.get(k, 0) < v:
                                e.wait_ge(sems[k], v)
                                waited[k] = v
                        ins = fn(e)
                        ins.then_inc(sems[tok[0]], 1 if kind == 'c' else 16)
                    if ename == 'sp':
                        fin = {}
                        for e2 in self.ENGS:
                            c = self.cnt[e2]
                            if c > 0:
                                fin[(e2, (c - 1) // GEN)] = (c - 1) % GEN + 1
                        for key, n in self.dcnt.items():
                            fin[key] = 16 * n
                        for k, v in fin.items():
                            if waited.get(k, 0) < v:
                                e.wait_ge(sems[k], v)
                return body
            for ename in self.ENGS:
                getattr(block, engobj[ename])(make(ename))


def host_consts(p):
    shift = 128 * (1 - p)
    c = {}
    c['ident'] = np.eye(128, dtype=np.float32)
    c['identb'] = np.eye(128, dtype=np.float32).astype(NPBF)
    k = np.arange(128)[:, None]
    q = np.arange(128)[None, :]
    c['tri'] = np.where(k > q, NEG, 0.0).astype(NPBF)
    c['band'] = np.where(k <= q, NEG, 0.0).astype(NPBF)
    half = 16
    inv_freq = (10000.0 ** (-np.arange(half, dtype=np.float32) / half)).astype(np.float32)
    L = np.arange(S)
    gpos = (L - shift).astype(np.float32)
    ang = (gpos[:, None] * inv_freq[None, :]).astype(np.float32)
    cos2 = np.concatenate([np.cos(ang), np.cos(ang)], axis=1).T.astype(np.float32)
    sin2 = np.concatenate([np.sin(ang), np.sin(ang)], axis=1).T.astype(np.float32)
    c['cosk'] = np.ascontiguousarray(cos2)
    c['sink'] = np.ascontiguousarray(sin2)
    own = (np.arange(NO)[:, None] * 256 + 128 + np.arange(128)[None, :]).reshape(-1)
    c['cosq'] = np.ascontiguousarray(cos2[:, own])
    c['sinq'] = np.ascontiguousarray(sin2[:, own])
    ka = np.zeros((5, S), np.float32)
    ka[0] = 1.0
    ka[1] = 1.0
    ka[2] = 128.0 * (L // 128)
    ka[3] = L % 128
    ka[4] = (L < shift).astype(np.float32)
    c['KA'] = ka.astype(NPBF)
    li = np.arange(512)
    cend = 16 * li + 31
    kca = np.zeros((5, 512), np.float32)
    kca[0] = 1.0
    kca[1] = 1.0
    kca[2] = 128.0 * (cend // 128)
    kca[3] = cend % 128
    kca[4] = (16 * li < shift).astype(np.float32)
    c['KCA'] = kca.astype(NPBF)
    slopes = 2.0 ** (-8.0 * np.arange(1, 9) / 8.0)
    qa = np.zeros((8, 5, NO * 128), np.float32)
    for h in range(8):
        cc = slopes[h] / SC_NSA
        qa[h, 0] = -cc * 128.0 * (own // 128)
        qa[h, 1] = -cc * (own % 128)
        qa[h, 2] = cc
        qa[h, 3] = cc
        qa[h, 4] = NEG
    c['QAh'] = qa.astype(NPBF)
    e32 = np.zeros((32, 16, 128), np.float32)
    for v in range(16):
        e32[2 * v, v, 0:64] = 1.0
        e32[2 * v + 1, v, 64:128] = 1.0

    ef = np.zeros((59, S), np.float32)
    lbt = (L // 64)
    for j in range(58):
        ef[j] = (lbt % 58 == j)
    c['EF'] = ef.astype(NPBF)
    cm = np.zeros((128, 8, 128), np.float32)
    for mi in range(8):
        m = 2 * mi + 1
        delta = 128 * m
        valid = (16 * k + 31) <= (delta + q)
        cm[:, mi, :] = np.where(valid, 0.0, NEG)
    c['cmask'] = cm.astype(NPBF)
    lia = np.arange(512)[:, None]
    lb = np.arange(128)[None, :]
    ov = ((16 * lia < 64 * lb + 64) & (16 * lia + 31 >= 64 * lb)).astype(np.float32)
    c['OVL'] = np.ascontiguousarray(ov.reshape(4, 128, 128).transpose(1, 0, 2)).astype(NPBF)
    bon = np.zeros((NO, 128, 128), np.float32)
    blk0 = 2 * (1 - p)
    for i in range(NO):
        t = 128 * (2 * i + 1) + np.arange(128)[:, None]
        cur = t // 64
        lbb = np.arange(128)[None, :]
        valid = (lbb <= cur) & (lbb >= blk0)
        forced = (lbb == blk0) | (lbb >= cur - 1)
        bon[i] = np.where(valid, np.where(forced, 1000.0, 0.0), np.where(lbb > cur, -1e9, -2e9))
    c['bonus'] = bon
    return c


CONST_SHAPES = {
    'ident': ([128, 128], F32), 'identb': ([128, 128], BF16), 'tri': ([128, 128], BF16), 'band': ([128, 128], BF16),
    'cosk': ([32, S], F32), 'sink': ([32, S], F32), 'cosq': ([32, NO * 128], F32), 'sinq': ([32, NO * 128], F32),
    'KA': ([5, S], BF16), 'KCA': ([5, 512], BF16), 'QAh': ([8, 5, NO * 128], BF16), 'EF': ([59, S], BF16),
    'cmask': ([128, 8, 128], BF16), 'OVL': ([128, 4, 128], BF16), 'bonus': ([NO, 128, 128], F32),
}
IN_SHAPES = {
    'xl': [S, D], 'xo': [NO * 128, D], 'c_l': [128, 8], 'w_ada': [D, 6 * D], 'bada_l': [128, 48],
    'gmix_l': [128, 8], 'gmlp_l': [128, 8], 'gcq_l': [128, 3], 'gckv_l': [128, 2], 'gfin_l': [128, 8],
    'w_in': [D, 4024], 'peck_t': [64, 32], 'w_ck1': [2048, 256], 'w_ck2': [256, 64],
    'pecv_t': [64, 32], 'w_cv1': [2048, 256], 'w_cv2': [256, 64],
    'w_uq': [384, 768], 'w_uk': [256, 512], 'w_uv': [256, 512], 'w_o_nsa': [512, D], 'w_o_mla': [512, D],
    'w_out': [D, D], 'w_fc1': [D, 4 * D], 'w_fc2': [4 * D, D],
}


class StopBuild(Exception):
    pass


def build_program(debug=None, stop=None):
    try:
        return _build_program(debug, stop)
    except StopBuild as ex:
        return ex.args[0]


def _build_program(debug=None, stop=None):
    nc = bass.Bass("TRN2", target_bir_lowering=False)
    I = {}
    for name, shp in IN_SHAPES.items():
        I[name] = nc.dram_tensor(name, shp, F32, kind="ExternalInput").ap()
    for name, (shp, dt) in CONST_SHAPES.items():
        I[name] = nc.dram_tensor(name, shp, dt, kind="ExternalInput").ap()
    out = nc.dram_tensor("out", [NO * 128, D], F32, kind="ExternalOutput").ap()
    x1s = nc.dram_tensor("x1s", [NO * 128, D], F32, kind="Internal").ap()
    dbg = {}
    if debug:
        for name, shp in debug.items():
            dbg[name] = nc.dram_tensor("dbg_" + name, shp, F32, kind="ExternalOutput").ap()

    P = Prog(nc)
    rr = {'ev': 0}

    def E(meth, **kw):
        return lambda e: getattr(e, meth)(**kw)

    def SB(name, shape, dt=F32):
        return nc.sbuf_tensor("sb_" + name, shape, dt)

    def kparts(ap, p=128):
        return ap.rearrange("(k p) n -> p k n", p=p)

    def checkpoint(name):
        if stop == name:
            raise StopBuild(nc)

    with ExitStack() as G:
        G.callback(P.emit)

        def sbg(name, shape, dt=F32):
            return G.enter_context(SB(name, shape, dt))
        psT = G.enter_context(nc.psum_tensor("psT", [128, 1024], F32))
        psZ = [G.enter_context(nc.psum_tensor(f"psZ{i}", [128, 512], F32)) for i in range(4)]
        psO = G.enter_context(nc.psum_tensor("psO", [128, 512], F32))
        psB = G.enter_context(nc.psum_tensor("psB", [128, 1024], BF16))
        ident = sbg("ident", [128, 128]); identb = sbg("identb", [128, 128], BF16)
        tri = sbg("tri", [128, 128], BF16); band = sbg("band", [128, 128], BF16)
        onesf = sbg("onesf", [128, 128])
        epsb = sbg("epsb", [128, 1])
        modT = sbg("modT", [128, 48])
        gsA = sbg("gsA", [128, 8]); gsM = sbg("gsM", [128, 8])
        gl = sbg("gl", [128, 8 + 8 + 3 + 2 + 8])
        ssr = sbg("ssr", [128, 16])
        junk = sbg("junk", [128, 1024], BF16)
        gbc = sbg("gbc", [128, 1024])
        dg = sbg("dg", [128, 128])
        OS = ExitStack()
        oTn = OS.enter_context(SB("oTn", [128, 4, NO * 128], BF16))

        def dma(out_ap, in_ap, r=(), w=(), q='sp'):
            return P.dma(q, lambda e: e.dma_start(out=out_ap, in_=in_ap), r=r, w=w)

        def mm(out_ap, lhsT, rhs, start, stop, r=(), w=(), skip=False):
            if skip:
                return P.op('pe', lambda e: e.matmul(out_ap, lhsT=lhsT, rhs=rhs, start=start, stop=stop,
                                                     skip_group_check=True), r=r, w=w)
            return P.op('pe', lambda e: e.matmul(out_ap, lhsT=lhsT, rhs=rhs, start=start, stop=stop), r=r, w=w)

        def act(out_ap, in_ap, func, r=(), w=(), **kw):
            return P.op('act', lambda e: e.activation(out=out_ap, in_=in_ap, func=func, **kw), r=r, w=w)

        def evac(out_ap, in_ap, r=(), w=()):
            rr['ev'] += 1
            if EVAC_ACT_ONLY or rr['ev'] % 2:
                return P.op('act', lambda e: e.activation(out=out_ap, in_=in_ap, func=AF.Copy), r=r, w=w)
            return P.op('dve', lambda e: e.tensor_scalar(out=out_ap, in0=in_ap, scalar1=1.0, scalar2=None, op0=ALU.mult), r=r, w=w)

        def ts(eng, out_ap, in0, s1, s2, op0, op1=None, r=(), w=()):
            if op1 is None:
                return P.op(eng, lambda e: e.tensor_scalar(out=out_ap, in0=in0, scalar1=s1, scalar2=None, op0=op0), r=r, w=w)
            return P.op(eng, lambda e: e.tensor_scalar(out=out_ap, in0=in0, scalar1=s1, scalar2=s2, op0=op0, op1=op1), r=r, w=w)

        def tt(eng, out_ap, in0, in1, op, r=(), w=()):
            return P.op(eng, lambda e: e.tensor_tensor(out=out_ap, in0=in0, in1=in1, op=op), r=r, w=w)

        def stt(eng, out_ap, in0, scalar, in1, op0, op1, r=(), w=()):
            return P.op(eng, lambda e: e.scalar_tensor_tensor(out=out_ap, in0=in0, scalar=scalar, in1=in1, op0=op0, op1=op1), r=r, w=w)

        def memset(eng, ap, val, w=()):
            return P.op(eng, lambda e: e.memset(ap, val), w=w)

        def dump(name, ap, r):
            if name in dbg:
                dma(dbg[name], ap, r=r, w=['dbg_' + name], q='pool')

        dma(ident[:], I['ident'][:, :], w=['ident'])
        dma(identb[:], I['identb'][:, :], w=['identb'])
        dma(tri[:], I['tri'][:, :], w=['tri'])
        dma(band[:], I['band'][:, :], w=['band'])
        dma(gl[:, 0:8], I['gmix_l'][:, :], w=['gl'])
        dma(gl[:, 8:16], I['gmlp_l'][:, :], w=['gl'])
        dma(gl[:, 16:19], I['gcq_l'][:, :], w=['gl'])
        dma(gl[:, 19:21], I['gckv_l'][:, :], w=['gl'])
        dma(gl[:, 21:29], I['gfin_l'][:, :], w=['gl'])
        memset('pool', onesf[:], 1.0, w=['onesf'])
        memset('pool', epsb[:], EPS, w=['epsb'])

        with ExitStack() as es:
            wad = [es.enter_context(SB(f"wad{i}", [128, 8, 512], F32)) for i in range(2)]
            cT = es.enter_context(SB("cT", [128, 8], F32))
            bl = es.enter_context(SB("bl", [128, 48], F32))
            dma(cT[:], I['c_l'][:, :], w=['cT'])
            dma(bl[:], I['bada_l'][:, :], w=['bl'])
            wv = kparts(I['w_ada'])
            for piece in range(12):
                buf = wad[piece % 2]
                bn = f"wad{piece % 2}"
                dma(buf[:], wv[:, :, piece * 512:(piece + 1) * 512], w=[bn])
                for jj in range(4):
                    j = piece * 4 + jj
                    for k in range(8):
                        mm(psZ[0][:, j:j + 1], buf[:, k, jj * 128:(jj + 1) * 128], cT[:, k:k + 1],
                           k == 0, k == 7, r=[bn, 'cT'], w=['psZ0'])
            tt('dve', modT[:], psZ[0][:, 0:48], bl[:], ALU.add, r=['psZ0', 'bl'], w=['modT'])
            stt('dve', gsA[:], modT[:, 8:16], 1.0, gl[:, 0:8], ALU.add, ALU.mult, r=['modT', 'gl'], w=['gsA'])
            stt('dve', gsM[:], modT[:, 32:40], 1.0, gl[:, 8:16], ALU.add, ALU.mult, r=['modT', 'gl'], w=['gsM'])
            dump('modT', modT[:], ['modT'])
        P.barrier()
        checkpoint('p0')
        shA = modT[:, 0:8]
        shM = modT[:, 24:32]

        def make_hT(xin, xin_res, xn, xn_res, hT, hT_res, gs, sh, slot):
            hT_pre(xin, xin_res, xn, xn_res, slot)
            hT_post(hT, hT_res, gs, sh)

        def hT_pre(xin, xin_res, xn, xn_res, slot):
            hT_norm(xin, xin_res, xn, xn_res, slot)
            hT_tr(xn, xn_res)

        def hT_norm(xin, xin_res, xn, xn_res, slot):
            act(junk[:], xin[:], AF.Square, r=[xin_res], w=['junk', f'ss{slot}'], accum_out=ssr[:, slot:slot + 1])
            act(ssr[:, slot + 4:slot + 5], ssr[:, slot:slot + 1], AF.Sqrt, r=[f'ss{slot}', 'epsb'], w=[f'sq{slot}'],
                scale=1.0 / D, bias=epsb[:, 0:1])
            P.op('dve', E('reciprocal', out=ssr[:, slot + 8:slot + 9], in_=ssr[:, slot + 4:slot + 5]),
                 r=[f'sq{slot}'], w=[f'rs{slot}'])
            rstd = ssr[:, slot + 8:slot + 9]
            act(xn[:, 0:512], xin[:, 0:512], AF.Identity, r=[xin_res, f'rs{slot}'], w=[xn_res + 'a'], scale=rstd)
            ts('dve', xn[:, 512:1024], xin[:, 512:1024], rstd, None, ALU.mult, r=[xin_res, f'rs{slot}'], w=[xn_res + 'b'])

        def hT_tr(xn, xn_res):
            for k in range(8):
                hf = 'a' if k < 4 else 'b'
                P.op('pe', E('transpose', out=psT[:, k * 128:(k + 1) * 128], in_=xn[:, k * 128:(k + 1) * 128],
                             identity=ident[:]),
                     r=[xn_res + hf, 'ident'], w=['psT' + hf])

        def hT_post(hT, hT_res, gs, sh):
            for k in range(8):
                hf = 'a' if k < 4 else 'b'
                if k % 2 == 0:
                    ts('dve', hT[:, k, :], psT[:, k * 128:(k + 1) * 128], gs[:, k:k + 1], sh[:, k:k + 1], ALU.mult, ALU.add,
                       r=['psT' + hf, 'gs', 'modT'], w=[hT_res])
                else:
                    act(hT[:, k, :], psT[:, k * 128:(k + 1) * 128], AF.Identity, r=['psT' + hf, 'gs', 'modT'], w=[hT_res],
                        scale=gs[:, k:k + 1], bias=sh[:, k:k + 1])

        dq = []
        LA = 2

        def defer(fn):
            dq.append(fn)
            while len(dq) > LA:
                dq.pop(0)()

        def flush():
            while dq:
                dq.pop(0)()

        zring = {'i': 0, 'n': 4}

        def nextZ():
            zring['i'] = (zring['i'] + 1) % zring['n']
            return psZ[zring['i']], f"psZ{zring['i']}"

        for g in range(2):
            with ExitStack() as NS:
                def sbn(name, shape, dt=F32):
                    return NS.enter_context(SB(f"{name}_g{g}", shape, dt))
                zqT = sbn("zqT", [128, 2, NO * 128], BF16)
                ksA = sbn("ksA", [128, S], BF16)
                kwA = sbn("kwA", [128, S], BF16)
                vsA = sbn("vsA", [128, NT, 66], BF16)
                vwA = sbn("vwA", [128, NT, 66], BF16)
                sg = sbn("sg", [128, NO, 24])
                kcA = sbn("kcA", [128, 512], BF16)
                VCA = sbn("VCA", [128, 4, 194], BF16)
                memset('pool', vsA[:, :, 64:66], 1.0, w=['vsA'])
                memset('pool', vwA[:, :, 64:66], 1.0, w=['vwA'])
                memset('pool', kcA[:], 0.0, w=['kcA'])
                memset('pool', VCA[:, :, 192:194], 1.0, w=['VCA'])
                checkpoint('n_a')
                dma(VCA[:, :, 64:192], I['OVL'][:, :, :], w=['VCA'])
                checkpoint('n_b')
                with ExitStack() as PS:
                    def sbp(name, shape, dt=F32):
                        return PS.enter_context(SB(f"{name}_g{g}", shape, dt))
                    wn = sbp("wn", [128, 8, 664], BF16)
                    xr = [sbp(f"xr{i}", [128, 1024]) for i in range(3)]
                    hTr = [sbp(f"hTr{i}", [128, 8, 128], BF16) for i in range(2)]
                    kcT = sbp("kcT", [128, S], BF16)
                    vcT = sbp("vcT", [128, S], BF16)
                    w1 = sbp("w1", [64, 32, 256], BF16)
                    w2 = sbp("w2", [128, 2, 64], BF16)
                    peT = sbp("peT", [64, 32], BF16)
                    hb = sbp("hb", [128, 2])
                    hid = sbp("hid", [128, 2, 512], BF16)
                    wiv = kparts(I['w_in'])
                    colmap = [(0, C_Q + 256 * g, 256), (256, C_KC + 64 * g, 64), (320, C_VC + 64 * g, 64),
                              (384, C_KS + 64 * g, 64), (448, C_VS + 64 * g, 64), (512, C_KW + 64 * g, 64),
                              (576, C_VW + 64 * g, 64), (640, C_G, 24)]
                    for (d0, s0, n) in colmap:
                        dma(wn[:, :, d0:d0 + n], wiv[:, :, s0:s0 + n], w=['wn'], q='pool')
                    print("sbuf remaining after proj alloc", nc.sbuf_bytes_remaining, flush=True)
                    checkpoint('n_w')
                    for lt in range(NT):
                        if lt == 1:
                            checkpoint('n_t0')
                        if lt == 2:
                            checkpoint('n_t1')
                        sl = lt % 2
                        hT = hTr[sl]
                        hres = f"hT{sl}"

                        def nsa_norm(t2):
                            s3 = t2 % 3
                            dma(xr[s3][:], I['xl'][t2 * 128:(t2 + 1) * 128, :], w=[f"xr{s3}a", f"xr{s3}b", f"xr{s3}"])
                            hT_norm(xr[s3], f"xr{s3}", xr[s3], f"xr{s3}", t2 % 4)

                        def nsa_tr_post(t2):
                            hT_tr(xr[t2 % 3], f"xr{t2 % 3}")
                            hT_post(hTr[t2 % 2], f"hT{t2 % 2}", gsA, shA)
                        if lt == 0:
                            nsa_norm(0)
                            nsa_tr_post(0)
                            nsa_norm(1)
                        tsl = slice(lt * 128, (lt + 1) * 128)
                        pz, pzn = nextZ()
                        for j, c0 in enumerate((256, 320, 384, 512)):
                            for k in range(8):
                                mm(pz[0:64, j * 128:(j + 1) * 128], wn[:, k, c0:c0 + 64], hT[:, k, :], k == 0, k == 7,
                                   r=['wn', hres], w=[pzn])
                        pv, pvn = nextZ()
                        for j, c0 in enumerate((448, 576)):
                            for k in range(8):
                                mm(pv[:, j * 64:(j + 1) * 64], hT[:, k, :], wn[:, k, c0:c0 + 64], k == 0, k == 7,
                                   r=['wn', hres], w=[pvn])
                        own = (lt % 2 == 1)
                        i = lt // 2
                        if own:
                            for n2 in range(2):
                                for k in range(8):
                                    mm(pv[:, 128 + n2 * 128:256 + n2 * 128], wn[:, k, n2 * 128:(n2 + 1) * 128], hT[:, k, :],
                                       k == 0, k == 7, r=['wn', hres], w=[pvn])
                            for k in range(8):
                                mm(pv[:, 384:408], hT[:, k, :], wn[:, k, 640:664], k == 0, k == 7, r=['wn', hres], w=[pvn])
                        if lt + 1 < NT:
                            nsa_tr_post(lt + 1)
                        if lt + 2 < NT:
                            nsa_norm(lt + 2)
                        evac(kcT[0:64, tsl], pz[0:64, 0:128], r=[pzn], w=['kcT'])
                        evac(vcT[0:64, tsl], pz[0:64, 128:256], r=[pzn], w=['vcT'])
                        evac(ksA[0:64, tsl], pz[0:64, 256:384], r=[pzn], w=['ksA'])
                        evac(kwA[0:64, tsl], pz[0:64, 384:512], r=[pzn], w=['kwA'])
                        evac(vsA[:, lt, 0:64], pv[:, 0:64], r=[pvn], w=['vsA'])
                        evac(vwA[:, lt, 0:64], pv[:, 64:128], r=[pvn], w=['vwA'])
                        if own:
                            evac(zqT[:, 0, i * 128:(i + 1) * 128], pv[:, 128:256], r=[pvn], w=['zqT'])
                            evac(zqT[:, 1, i * 128:(i + 1) * 128], pv[:, 256:384], r=[pvn], w=['zqT'])
                            act(sg[:, i, :], pv[:, 384:408], AF.Sigmoid, r=[pvn], w=['sg'])
                    checkpoint('n_tiles')
                    dma(ksA[64:69, :], I['KA'][:, :], w=['ksA'])
                    dma(ksA[69:128, :], I['EF'][:, :], w=['ksA'])
                    dma(kwA[64:69, :], I['KA'][:, :], w=['kwA'])
                    dma(kcA[64:69, :], I['KCA'][:, :], w=['kcA'])
                    checkpoint('n_aug')
                    for which in range(2):
                        src = kcT if which == 0 else vcT
                        srcn = 'kcT' if which == 0 else 'vcT'
                        w1d = I['w_ck1'] if which == 0 else I['w_cv1']
                        w2d = I['w_ck2'] if which == 0 else I['w_cv2']
                        ped = I['peck_t'] if which == 0 else I['pecv_t']
                        dma(w1[:], w1d.rearrange("(l d) h -> d l h", d=64), w=['w1'], q='pool')
                        dma(w2[:], kparts(w2d), w=['w2'], q='pool')
                        dma(peT[:], ped[:, :], w=['peT'], q='pool')
                        pz, pzn = nextZ()
                        for hc in range(2):
                            for l in range(32):
                                mm(pz[:, hc:hc + 1], w1[:, l, hc * 128:(hc + 1) * 128], peT[:, l:l + 1], l == 0, l == 31,
                                   r=['w1', 'peT'], w=[pzn])
                        evac(hb[:], pz[:, 0:2], r=[pzn], w=['hb'])
                        memset('pool', hid[:], 0.0, w=['hid'])
                        for hc in range(2):
                            pz, pzn = nextZ()
                            for l in range(32):
                                mm(pz[:, 0:511], w1[:, l, hc * 128:(hc + 1) * 128], src[0:64, l:l + 16 * 510 + 1:16],
                                   l == 0, l == 31, r=['w1', srcn], w=[pzn])
                            act(hid[:, hc, 0:511], pz[:, 0:511], AF.Silu, r=[pzn, 'hb'], w=['hid'], bias=hb[:, hc:hc + 1])
                        if which == 0:
                            pz, pzn = nextZ()
                            for hc in range(2):
                                mm(pz[0:64, 0:511], w2[:, hc, :], hid[:, hc, 0:511], hc == 0, hc == 1, r=['w2', 'hid'], w=[pzn])
                            evac(kcA[0:64, 0:511], pz[0:64, 0:511], r=[pzn], w=['kcA'])
                        else:
                            pz, pzn = nextZ()
                            for ct in range(4):
                                for hc in range(2):
                                    mm(pz[:, ct * 64:(ct + 1) * 64], hid[:, hc, ct * 128:(ct + 1) * 128], w2[:, hc, :],
                                       hc == 0, hc == 1, r=['w2', 'hid'], w=[pzn])
                            evac(VCA[:, :, 0:64], pz[:, 0:256].rearrange("p (a b) -> p a b", b=64), r=[pzn], w=['VCA'])
                    if g == 0:
                        dump('kcA', kcA[0:64, :], ['kcA'])
                        dump('ksA', ksA[0:64, 0:1024], ['ksA'])
                        dump('zqT', zqT[:, 0, 0:512], ['zqT'])
                P.barrier()
                checkpoint(f'nproj{g}')
                with ExitStack() as AS:
                    def sba(name, shape, dt=F32):
                        return AS.enter_context(SB(f"{name}_g{g}", shape, dt))
                    QA = [sba(f"QA{h}", [128, NO * 128], BF16) for h in range(4)]
                    PT = [sba(f"PT{i}", [128, 512], BF16) for i in range(4)]
                    PTw = [sba(f"PTw{i}", [128, 640], BF16) for i in range(3)]
                    Zm4 = [[sba(f"Zm{q_}_{i}", [128, 128]) for i in range(3)] for q_ in range(4)]
                    late_tr = []
                    RqAll = [[sba(f"Rq{p_}_{i}", [128, 512], BF16) for i in range(3)] for p_ in range(2)]
                    ocmpAll = [sba(f"ocmp{p_}", [128, 4, 4, 64]) for p_ in range(2)]
                    ostage = sba("ostage", [128, NO, 256], BF16)
                    cmask = sba("cmask", [128, 8, 128], BF16)
                    bon = [sba(f"bon{i}", [128, 128]) for i in range(2)]
                    pslc = sba("pslc", [128, 128])
                    scb = sba("scb", [128, 128])
                    mrb = sba("mrb", [128, 128])
                    mx = sba("mx", [128, 16])
                    negm = sba("negm", [128, 128])
                    sm = sba("sm", [128, 64])
                    t1b = sba("t1b", [128, 4, 64])
                    zring['n'] = 3
                    for q_ in range(4):
                        for r_ in range(3):
                            memset('pool', Zm4[q_][r_][:], 0.0, w=[f'Zm{q_}_{r_}'])
                    dma(cmask[:], I['cmask'][:, :, :], w=['cmask'])
                    for hg in range(4):
                        h = 4 * g + hg
                        r0 = (hg % 2) * 64
                        P.op('pool', E('tensor_copy', out=QA[hg][0:64, :], in_=zqT[r0:r0 + 64, hg // 2, :]),
                             r=['zqT'], w=[f'QA{hg}'])
                        dma(QA[hg][64:69, :], I['QAh'][h, :, :], w=[f'QA{hg}'])
                    pti = {'i': 0, 'w': 0}
                    def cmp_topk(c):
                        for qi in range(4):
                            i = 4 * c + qi
                            qt = 2 * i + 1
                            qsl = slice(i * 128, (i + 1) * 128)
                            ctm = (qt - 1) // 16
                            bsl = i % 2
                            dma(bon[bsl][:], I['bonus'][i, :, :], w=[f'bon{bsl}'])
                            for hg in range(4):
                                h = 4 * g + hg
                                pz, pzn = nextZ()
                                for ct in range(ctm + 1):
                                    m = qt - 16 * ct
                                    partial = m < 17
                                    mm(pz[:, ct * 128:(ct + 1) * 128], kcA[0:69, ct * 128:(ct + 1) * 128], QA[hg][0:69, qsl],
                                       True, not partial, r=['kcA', f'QA{hg}'], w=[pzn])
                                    if partial:
                                        mm(pz[:, ct * 128:(ct + 1) * 128], identb[:], cmask[:, (m - 1) // 2, :], False, True,
                                           r=['identb', 'cmask'], w=[pzn])
                                ptc = PT[pti['i'] % 4]
                                ptn = f"PT{pti['i'] % 4}"
                                pti['i'] += 1
                                ncol = (ctm + 1) * 128
                                act(ptc[:, 0:ncol], pz[:, 0:ncol], AF.Exp, r=[pzn], w=[ptn], scale=SC_NSA)
                                def cmpB(hg=hg, h=h, i=i, qi=qi, ctm=ctm, ptc=ptc, ptn=ptn):
                                    ub = 'psTa' if hg < 2 else 'psTb'
                                    uo = hg * 256
                                    for ct in range(ctm + 1):
                                        mm(psT[:, uo:uo + 193], ptc[:, ct * 128:(ct + 1) * 128], VCA[:, ct, 0:193],
                                           ct == 0, ct == ctm, r=[ptn, 'VCA'], w=[ub])
                                    ts('dve', sm[:, hg:hg + 1], psT[:, uo + 192:uo + 193], 1e-30, None, ALU.max, r=[ub], w=[f'sm{hg}'])
                                    P.op('dve', E('reciprocal', out=sm[:, 4 + hg:5 + hg], in_=sm[:, hg:hg + 1]),
                                         r=[f'sm{hg}'], w=[f'smr{hg}'])
                                    if hg == 0:
                                        ts('dve', pslc[:], psT[:, uo + 64:uo + 192], sm[:, 4 + hg:5 + hg], None, ALU.mult,
                                           r=[ub, f'smr{hg}'], w=['pslc'])
                                    else:
                                        stt('dve', pslc[:], psT[:, uo + 64:uo + 192], sm[:, 4 + hg:5 + hg], pslc[:], ALU.mult, ALU.add,
                                            r=[ub, f'smr{hg}', 'pslc'], w=['pslc'])
                                    tt('dve', sm[:, 8 + hg:9 + hg], sm[:, 4 + hg:5 + hg], sg[:, i, h:h + 1], ALU.mult,
                                       r=[f'smr{hg}', 'sg'], w=[f'smg{hg}'])
                                    ts('dve', ocmpAll[c % 2][:, qi, hg, :], psT[:, uo:uo + 64], sm[:, 8 + hg:9 + hg], None, ALU.mult,
                                       r=[ub, f'smg{hg}'], w=[f'ocmp{c % 2}'])
                                defer(cmpB)
                            flush()
                            tt('dve', scb[:], pslc[:], bon[bsl][:], ALU.add, r=['pslc', f'bon{bsl}'], w=['scb'])
                            P.op('dve', E('max', out=mx[:, 0:8], in_=scb[:]), r=['scb'], w=['mx0'])
                            P.op('dve', E('match_replace', out=mrb[:], in_to_replace=mx[:, 0:8], in_values=scb[:],
                                                                  imm_value=-3e9), r=['scb', 'mx0'], w=['mrb'])
                            P.op('dve', E('max', out=mx[:, 8:16], in_=mrb[:]), r=['mrb'], w=['mx1'])
                            for r_ in range((2 * qt + 1) // 58 + 1):
                                nb = min(58, 128 - 58 * r_)
                                ts('dve', Zm4[qi][r_][:, 69:69 + nb], scb[:, 58 * r_:58 * r_ + nb], mx[:, 15:16], NEG, ALU.is_lt, ALU.mult,
                                   r=['scb', 'mx1'], w=[f'Zm{qi}_{r_}'])
                                def trB(r_=r_, qi=qi, c=c, zt=Zm4[qi][r_], ztn=f'Zm{qi}_{r_}'):
                                    pz, pzn = nextZ()
                                    P.op('pe', E('transpose', out=pz[:, 0:128], in_=zt[:], identity=ident[:]),
                                         r=[ztn, 'ident'], w=[pzn])
                                    act(RqAll[c % 2][r_][64:128, qi * 128:(qi + 1) * 128], pz[64:128, 0:128], AF.Copy, r=[pzn],
                                        w=[f'Rq{c % 2}_{r_}'])
                                late_tr.append(trB)
                            if g == 0 and c == 1 and qi == 0:
                                dump('pslc', pslc[:], ['pslc'])
                                dump('negm', Zm4[0][0][:], ['Zm0_0'])
                    def run_late():
                        while late_tr:
                            late_tr.pop(0)()
                    cmp_topk(0)
                    run_late()
                    for c in range(8):
                        for hg in range(4):
                            h = 4 * g + hg
                            if hg == 1 and c + 1 < 8:
                                cmp_topk(c + 1)
                            if hg == 3:
                                run_late()
                            for qi in range(4):
                                i = 4 * c + qi
                                qt = 2 * i + 1
                                qsl = slice(i * 128, (i + 1) * 128)
                                kts = [kt for kt in range(qt - 4, qt + 1) if kt >= 0]
                                pw = PTw[pti['w'] % 3]
                                pwn = f"PTw{pti['w'] % 3}"
                                pti['w'] += 1
                                pzA, pzAn = nextZ()
                                pzB, pzBn = nextZ()
                                for j, kt in enumerate(kts):
                                    ksl = slice(kt * 128, (kt + 1) * 128)
                                    last = (kt == qt)
                                    dst = pzB[:, 0:128] if last else pzA[:, j * 128:(j + 1) * 128]
                                    dn = pzBn if last else pzAn
                                    masked = last or (kt == qt - 4)
                                    mm(dst, kwA[0:69, ksl], QA[hg][0:69, qsl], True, not masked, r=['kwA', f'QA{hg}'], w=[dn])
                                    if masked:
                                        mm(dst, identb[:], tri[:] if last else band[:], False, True, r=['identb', 'tri', 'band'], w=[dn])
                                na = len(kts) - 1
                                if na > 0:
                                    act(pw[:, 0:na * 128], pzA[:, 0:na * 128], AF.Exp, r=[pzAn], w=[pwn + 'a'], scale=SC_NSA)
                                act(pw[:, 512:640], pzB[:, 0:128], AF.Exp, r=[pzBn], w=[pwn + 'b'], scale=SC_NSA)
                                def winB(kts=kts, qt=qt, qi=qi, pw=pw, pwn=pwn):
                                    for j, kt in enumerate(kts):
                                        last = (kt == qt)
                                        src = pw[:, 512:640] if last else pw[:, j * 128:(j + 1) * 128]
                                        mm(psO[:, qi * 128:qi * 128 + 65], src, vwA[:, kt, 0:65], (j == 0 and qi == 0), last,
                                           r=[pwn + 'a', pwn + 'b', 'vwA'], w=['psO'], skip=True)
                                defer(winB)
                            for r_ in range((16 * c + 15) // 58 + 1):
                                P.op('pool', E('tensor_copy', out=RqAll[c % 2][r_][0:69, :], in_=QA[hg][0:69, c * 512:(c + 1) * 512]),
                                     r=[f'QA{hg}'], w=[f'Rq{c % 2}_{r_}'])
                            ktmax = 8 * c + 7
                            for kt in range(ktmax + 1):
                                qmin = max(0, (kt - (8 * c + 1) + 1) // 2)
                                cs = slice(qmin * 128, 512)
                                qs = slice(c * 512 + qmin * 128, (c + 1) * 512)
                                ksl = slice(kt * 128, (kt + 1) * 128)
                                diag = (kt % 2 == 1) and (kt >= 8 * c + 1)
                                pz, pzn = nextZ()
                                if pzn == 'psZ3':
                                    pz, pzn = nextZ()
                                rr_ = (2 * kt) // 58
                                mm(pz[:, cs], ksA[:, ksl], RqAll[c % 2][rr_][:, cs], True, not diag, r=['ksA', f'Rq{c % 2}_{rr_}'], w=[pzn])
                                if diag:
                                    qd = (kt - (8 * c + 1)) // 2
                                    mm(pz[:, qd * 128:(qd + 1) * 128], identb[:], tri[:], False, True, r=['identb', 'tri'], w=[pzn])
                                ptc = PT[pti['i'] % 4]
                                ptn = f"PT{pti['i'] % 4}"
                                pti['i'] += 1
                                act(ptc[:, cs], pz[:, cs], AF.Exp, r=[pzn], w=[ptn], scale=SC_NSA)
                                def selB(kt=kt, qmin=qmin, ptc=ptc, ptn=ptn, c=c):
                                    for qi in range(qmin, 4):
                                        mm(psZ[3][:, qi * 128:qi * 128 + 65], ptc[:, qi * 128:(qi + 1) * 128], vsA[:, kt, 0:65],
                                           (kt == 0 and qi == 0), kt == 8 * c + 1 + 2 * qi, r=[ptn, 'vsA'], w=['psZ3'], skip=True)
                                defer(selB)
                            def finB(c=c, hg=hg, h=h):
                                i0 = 4 * c
                                ts('dve', sm[:, 16:20], psO[:, 64:512:128], 1e-30, None, ALU.max, r=['psO'], w=['smw'])
                                P.op('dve', E('reciprocal', out=sm[:, 20:24], in_=sm[:, 16:20]), r=['smw'], w=['smwr'])
                                tt('dve', sm[:, 24:28], sm[:, 20:24], sg[:, i0:i0 + 4, 16 + h], ALU.mult, r=['smwr', 'sg'], w=['smwm'])
                                ts('dve', sm[:, 28:32], psZ[3][:, 64:512:128], 1e-30, None, ALU.max, r=['psZ3'], w=['sms'])
                                P.op('dve', E('reciprocal', out=sm[:, 32:36], in_=sm[:, 28:32]), r=['sms'], w=['smsr'])
                                tt('dve', sm[:, 36:40], sm[:, 32:36], sg[:, i0:i0 + 4, 8 + h], ALU.mult, r=['smsr', 'sg'], w=['smsm'])
                                for qi in range(4):
                                    stt('dve', t1b[:, qi, :], psO[:, qi * 128:qi * 128 + 64], sm[:, 24 + qi:25 + qi], ocmpAll[c % 2][:, qi, hg, :],
                                        ALU.mult, ALU.add, r=['psO', 'smwm', f'ocmp{c % 2}'], w=['t1b'])
                                    stt('dve', ostage[:, i0 + qi, hg * 64:(hg + 1) * 64], psZ[3][:, qi * 128:qi * 128 + 64],
                                        sm[:, 36 + qi:37 + qi], t1b[:, qi, :], ALU.mult, ALU.add, r=['psZ3', 'smsm', 't1b'], w=['ostage'])
                            defer(finB)
                    flush()
                    if g == 0:
                        dump('ostage', ostage[:, 0:4, :], ['ostage'])
                    zring['n'] = 4
                    for f in range(2):
                        for i8 in range(4):
                            for j in range(8):
                                i = i8 * 8 + j
                                P.op('pe', E('transpose', out=psB[:, j * 128:(j + 1) * 128],
                                                                                in_=ostage[:, i, f * 128:(f + 1) * 128],
                                                                                identity=identb[:]),
                                     r=['ostage', 'identb'], w=['psB'])
                            evac(oTn[:, 2 * g + f, i8 * 1024:(i8 + 1) * 1024], psB[:, :], r=['psB'], w=['oTn'])
                P.barrier()

        checkpoint('nsa')
        oTm = OS.enter_context(SB("oTm", [128, 4, NO * 128], BF16))
        with ExitStack() as MS:
            def sbm(name, shape, dt=F32):
                return MS.enter_context(SB(name, shape, dt))
            ckvT = sbm("ckvT", [128, 2, S], BF16)
            cqT = sbm("cqT", [128, 3, NO * 128], BF16)
            KT = sbm("KT", [128, S], BF16)
            with ExitStack() as PS:
                def sbp(name, shape, dt=F32):
                    return PS.enter_context(SB(name, shape, dt))
                wm = sbp("wm", [128, 8, 640], BF16)
                wkrA = sbp("wkrA", [128, 8, 96], BF16)
                wkrB = sbp("wkrB", [128, 8, 96], BF16)
                xr = [sbp(f"mxr{i}", [128, 1024]) for i in range(3)]
                hTr = [sbp(f"mhTr{i}", [128, 8, 128], BF16) for i in range(2)]
                zf = sbp("zf", [128, 5, 128])
                zs = sbp("zs", [128, 5, 128])
                rcb = sbp("rcb", [128, 2, 128])
                ctab = [sbp(f"ctab{i}", [128, 2, 128]) for i in range(2)]
                rt = sbp("rt", [128, 2, 128])
                wiv = kparts(I['w_in'])
                dma(wm[:, :, 0:384], wiv[:, :, C_QD:C_QD + 384], w=['wm'], q='pool')
                dma(wm[:, :, 384:640], wiv[:, :, C_KVD:C_KVD + 256], w=['wm'], q='pool')
                memset('pool', wkrA[:], 0.0, w=['wkrA'])
                memset('pool', wkrB[:], 0.0, w=['wkrB'])
                dma(wkrA[:, :, 64:96], wiv[:, :, C_KR:C_KR + 32], w=['wkrA'], q='pool')
                dma(wkrB[:, :, 80:96], wiv[:, :, C_KR:C_KR + 16], w=['wkrB'], q='pool')
                dma(wkrB[:, :, 64:80], wiv[:, :, C_KR + 16:C_KR + 32], w=['wkrB'], q='pool')
                ts('pool', wkrB[:, :, 64:80], wkrB[:, :, 64:80], -1.0, None, ALU.mult, r=['wkrB'], w=['wkrB'])
                dma(KT[96:97, :], I['KA'][4:5, :], w=['KT'])
                for lt in range(NT):
                    sl = lt % 2
                    hT = hTr[sl]
                    hres = f"mhT{sl}"

                    def m1_norm(t2):
                        s3 = t2 % 3
                        dma(xr[s3][:], I['xl'][t2 * 128:(t2 + 1) * 128, :], w=[f"mxr{s3}a", f"mxr{s3}b", f"mxr{s3}"])
                        hT_norm(xr[s3], f"mxr{s3}", xr[s3], f"mxr{s3}", t2 % 4)

                    def m1_tr_post(t2):
                        hT_tr(xr[t2 % 3], f"mxr{t2 % 3}")
                        hT_post(hTr[t2 % 2], f"mhT{t2 % 2}", gsA, shA)
                    if lt == 0:
                        m1_norm(0)
                        m1_tr_post(0)
                        m1_norm(1)
                    dma(ctab[sl][64:96, 0, :], I['cosk'][:, lt * 128:(lt + 1) * 128], w=[f'ctab{sl}'])
                    dma(ctab[sl][64:96, 1, :], I['sink'][:, lt * 128:(lt + 1) * 128], w=[f'ctab{sl}'])
                    tsl = slice(lt * 128, (lt + 1) * 128)
                    own = (lt % 2 == 1)
                    i = lt // 2
                    ntl = [(384, 0), (512, 1)] + ([(0, 2), (128, 3), (256, 4)] if own else [])
                    pz, pzn = nextZ()
                    pz2, pz2n = nextZ()
                    for (c0, slot) in ntl:
                        dst = pz[:, slot * 128:(slot + 1) * 128] if slot < 4 else pz2[:, 0:128]
                        dn = pzn if slot < 4 else pz2n
                        for k in range(8):
                            mm(dst, wm[:, k, c0:c0 + 128], hT[:, k, :], k == 0, k == 7, r=['wm', hres], w=[dn])
                    for k in range(8):
                        mm(pz2[0:96, 128:256], wkrA[:, k, :], hT[:, k, :], k == 0, k == 7, r=['wkrA', hres], w=[pz2n])
                    for k in range(8):
                        mm(pz2[0:96, 256:384], wkrB[:, k, :], hT[:, k, :], k == 0, k == 7, r=['wkrB', hres], w=[pz2n])
                    if lt + 1 < NT:
                        m1_tr_post(lt + 1)
                    if lt + 2 < NT:
                        m1_norm(lt + 2)
                    nsl = 5 if own else 2
                    for (c0, slot) in ntl:
                        srcp = pz[:, slot * 128:(slot + 1) * 128] if slot < 4 else pz2[:, 0:128]
                        sn = pzn if slot < 4 else pz2n
                        evac(zf[:, slot, :], srcp, r=[sn], w=[f'zf{slot}'])
                        act(zs[:, slot, :], srcp, AF.Square, r=[sn], w=[f'zs{slot}'])
                    pz3, pz3n = nextZ()
                    for j, slot in enumerate((0, 1)):
                        mm(pz3[:, 0:128], onesf[:], zs[:, slot, :], j == 0, j == 1, r=['onesf', f'zs{slot}'], w=[pz3n])
                    if own:
                        for j, slot in enumerate((2, 3, 4)):
                            mm(pz3[:, 128:256], onesf[:], zs[:, slot, :], j == 0, j == 2, r=['onesf', f'zs{slot}'], w=[pz3n])
                    act(rcb[:, 0, :], pz3[:, 0:128], AF.Sqrt, r=[pz3n, 'epsb'], w=['rcb0'], scale=1.0 / 256, bias=epsb[:, 0:1])
                    P.op('dve', E('reciprocal', out=rcb[:, 0, :], in_=rcb[:, 0, :]), r=['rcb0'], w=['rcb0'])
                    for slot in (0, 1):
                        tt('pool' if slot else 'dve', ckvT[:, slot, tsl], zf[:, slot, :], rcb[:, 0, :], ALU.mult,
                           r=[f'zf{slot}', 'rcb0'], w=['ckvT'])
                    if own:
                        act(rcb[:, 1, :], pz3[:, 128:256], AF.Sqrt, r=[pz3n, 'epsb'], w=['rcb1'], scale=1.0 / 384, bias=epsb[:, 0:1])
                        P.op('dve', E('reciprocal', out=rcb[:, 1, :], in_=rcb[:, 1, :]), r=['rcb1'], w=['rcb1'])
                        for slot in (2, 3, 4):
                            tt('pool' if slot % 2 else 'dve', cqT[:, slot - 2, i * 128:(i + 1) * 128], zf[:, slot, :], rcb[:, 1, :],
                               ALU.mult, r=[f'zf{slot}', 'rcb1'], w=['cqT'])
                    tt('dve', rt[64:96, 0, :], pz2[64:96, 128:256], ctab[sl][64:96, 0, :], ALU.mult, r=[pz2n, f'ctab{sl}'], w=['rt0'])
                    tt('dve', rt[64:96, 1, :], pz2[64:96, 256:384], ctab[sl][64:96, 1, :], ALU.mult, r=[pz2n, f'ctab{sl}'], w=['rt1'])
                    tt('pool', KT[64:96, tsl], rt[64:96, 0, :], rt[64:96, 1, :], ALU.add, r=['rt0', 'rt1'], w=['KT'])
                dump('ckvT', ckvT[:, 0, 0:1024], ['ckvT'])
                dump('cqT', cqT[:, 0, 0:512], ['cqT'])
                dump('krot', KT[64:96, 0:1024], ['KT'])
            P.barrier()
            checkpoint('m1')
            with ExitStack() as AS:
                def sba(name, shape, dt=F32):
                    return AS.enter_context(SB(name, shape, dt))
                wuq = sba("wuq", [128, 3, 768], BF16)
                wuqB = sba("wuqB", [128, 3, 768], BF16)
                wuk = sba("wuk", [128, 2, 512], BF16)
                wuv = sba("wuv", [128, 2, 512], BF16)
                VH = sba("VH", [128, NT, 66], BF16)
                QT = sba("QT", [128, NO * 128], BF16)
                PT = [sba(f"MPT{i}", [128, 512], BF16) for i in range(4)]
                ostage = sba("mostage", [128, NO, 128], BF16)
                qtab = [sba(f"qtab{i}", [128, 2, 512]) for i in range(2)]
                rt = sba("mrt", [128, 2, 512])
                sm = sba("msm", [128, 8])
                dma(wuq[:], kparts(I['w_uq']), w=['wuq'], q='pool')
                dma(wuk[:], kparts(I['w_uk']), w=['wuk'], q='pool')
                dma(wuv[:], kparts(I['w_uv']), w=['wuv'], q='pool')
                for k in range(3):
                    ts('pool', wuq[:, k, :], wuq[:, k, :], gl[:, 16 + k:17 + k], None, ALU.mult, r=['wuq', 'gl'], w=['wuq'])
                for k in range(2):
                    ts('pool', wuk[:, k, :], wuk[:, k, :], gl[:, 19 + k:20 + k], None, ALU.mult, r=['wuk', 'gl'], w=['wuk'])
                    ts('pool', wuv[:, k, :], wuv[:, k, :], gl[:, 19 + k:20 + k], None, ALU.mult, r=['wuv', 'gl'], w=['wuv'])
                memset('pool', wuqB[:], 0.0, w=['wuqB'])
                wq4 = wuq[:].rearrange("p k (h c) -> p k h c", c=96)
                wb4 = wuqB[:].rearrange("p k (h c) -> p k h c", c=96)
                for k in range(3):
                    ts('pool', wb4[:, k, :, 64:80], wq4[:, k, :, 80:96], -1.0, None, ALU.mult, r=['wuq'], w=['wuqB'])
                    P.op('pool', E('tensor_copy', out=wb4[:, k, :, 80:96], in_=wq4[:, k, :, 64:80]), r=['wuq'], w=['wuqB'])
                memset('pool', VH[:, :, 64:66], 1.0, w=['VH'])
                memset('pool', QT[96:97, :], NEG, w=['QT'])
                zring['n'] = 3
                pti = {'i': 0}
                hz = {'i': 0}

                def nextH():
                    hz['i'] = (hz['i'] + 1) % 2
                    return (psT[:, 0:512], 'psTa') if hz['i'] == 0 else (psT[:, 512:1024], 'psTb')

                for h in range(8):
                    for ch in range(16):
                        pz, pzn = nextH()
                        for k in range(2):
                            mm(pz[0:64, :], wuk[:, k, h * 64:(h + 1) * 64], ckvT[:, k, ch * 512:(ch + 1) * 512], k == 0, k == 1,
                               r=['wuk', 'ckvT'], w=[pzn])
                        evac(KT[0:64, ch * 512:(ch + 1) * 512], pz[0:64, :], r=[pzn], w=['KT'])
                    for t8 in range(8):
                        pz, pzn = nextH()
                        for j in range(8):
                            lt = t8 * 8 + j
                            for k in range(2):
                                mm(pz[:, j * 64:(j + 1) * 64], ckvT[:, k, lt * 128:(lt + 1) * 128], wuv[:, k, h * 64:(h + 1) * 64],
                                   k == 0, k == 1, r=['wuv', 'ckvT'], w=[pzn])
                        evac(VH[:, t8 * 8:(t8 + 1) * 8, 0:64], pz.rearrange("p (a b) -> p a b", b=64), r=[pzn], w=['VH'])
                    for c in range(8):
                        csl = slice(c * 512, (c + 1) * 512)
                        qs = c % 2
                        dma(qtab[qs][64:96, 0, :], I['cosq'][:, csl], w=[f'qtab{qs}'])
                        dma(qtab[qs][64:96, 1, :], I['sinq'][:, csl], w=[f'qtab{qs}'])
                        pzA, pzAn = nextH()
                        for k in range(3):
                            mm(pzA[0:96, :], wuq[:, k, h * 96:(h + 1) * 96], cqT[:, k, csl], k == 0, k == 2, r=['wuq', 'cqT'], w=[pzAn])
                        pzB, pzBn = nextH()
                        for k in range(3):
                            mm(pzB[0:96, :], wuqB[:, k, h * 96:(h + 1) * 96], cqT[:, k, csl], k == 0, k == 2, r=['wuqB', 'cqT'], w=[pzBn])
                        act(QT[0:64, csl], pzA[0:64, :], AF.Copy, r=[pzAn], w=['QT'])
                        tt('dve', rt[64:96, 0, :], pzA[64:96, :], qtab[qs][64:96, 0, :], ALU.mult, r=[pzAn, f'qtab{qs}'], w=['mrt0'])
                        tt('dve', rt[64:96, 1, :], pzB[64:96, :], qtab[qs][64:96, 1, :], ALU.mult, r=[pzBn, f'qtab{qs}'], w=['mrt1'])
                        tt('pool', QT[64:96, csl], rt[64:96, 0, :], rt[64:96, 1, :], ALU.add, r=['mrt0', 'mrt1'], w=['QT'])
                    if h == 0:
                        dump('QT', QT[0:96, 0:512], ['QT'])
                        dump('KT', KT[0:96, 0:1024], ['KT'])
                    for c in range(8):
                        ob, obn = (psO, 'psO') if c % 2 == 0 else (psZ[3], 'psZ3')
                        ktmax = 8 * c + 7
                        for kt in range(ktmax + 1):
                            qmin = max(0, (kt - (8 * c + 1) + 1) // 2)
                            cs = slice(qmin * 128, 512)
                            qs_ = slice(c * 512 + qmin * 128, (c + 1) * 512)
                            ksl = slice(kt * 128, (kt + 1) * 128)
                            diag = (kt % 2 == 1) and (kt >= 8 * c + 1)
                            pz, pzn = nextZ()
                            if pzn == 'psZ3':
                                pz, pzn = nextZ()
                            mm(pz[:, cs], KT[0:97, ksl], QT[0:97, qs_], True, not diag, r=['KT', 'QT'], w=[pzn])
                            if diag:
                                qd = (kt - (8 * c + 1)) // 2
                                mm(pz[:, qd * 128:(qd + 1) * 128], identb[:], tri[:], False, True, r=['identb', 'tri'], w=[pzn])
                            ptc = PT[pti['i'] % 4]
                            ptn = f"MPT{pti['i'] % 4}"
                            pti['i'] += 1
                            act(ptc[:, cs], pz[:, cs], AF.Exp, r=[pzn], w=[ptn], scale=SC_MLA)
                            def mlaB(kt=kt, qmin=qmin, ptc=ptc, ptn=ptn, c=c, ob=ob, obn=obn):
                                for qi in range(qmin, 4):
                                    mm(ob[:, qi * 128:qi * 128 + 65], ptc[:, qi * 128:(qi + 1) * 128], VH[:, kt, 0:65],
                                       (kt == 0 and qi == 0), kt == 8 * c + 1 + 2 * qi, r=[ptn, 'VH'], w=[obn], skip=True)
                            defer(mlaB)
                        def mlaF(c=c, h=h, ob=ob, obn=obn):
                            ts('dve', sm[:, 0:4], ob[:, 64:512:128], 1e-30, None, ALU.max, r=[obn], w=['msm0'])
                            P.op('dve', E('reciprocal', out=sm[:, 4:8], in_=sm[:, 0:4]), r=['msm0'], w=['msm1'])
                            for qi in range(4):
                                ts('dve', ostage[:, 4 * c + qi, (h % 2) * 64:(h % 2) * 64 + 64], ob[:, qi * 128:qi * 128 + 64],
                                   sm[:, 4 + qi:5 + qi], None, ALU.mult, r=[obn, 'msm1'], w=['mostage'])
                        defer(mlaF)
                    flush()
                    if h % 2 == 1:
                        if h == 1:
                            dump('mostage', ostage[:, 0:4, :], ['mostage'])
                        for i8 in range(4):
                            for j in range(8):
                                i = i8 * 8 + j
                                P.op('pe', E('transpose', out=psB[:, j * 128:(j + 1) * 128], in_=ostage[:, i, :],
                                                                           identity=identb[:]),
                                     r=['mostage', 'identb'], w=['psB'])
                            evac(oTm[:, h // 2, i8 * 1024:(i8 + 1) * 1024], psB[:, :], r=['psB'], w=['oTm'])
            P.barrier()

        zring['n'] = 4
        checkpoint('mla')
        def bcast_vec(col0):
            for j in range(8):
                pz, pzn = nextZ()
                src = col0(j)
                ts('dve', dg[:], ident[:], src, None, ALU.mult, r=['ident', 'modT', 'gl'], w=['dg'])
                mm(pz[:, 0:128], onesf[:], dg[:], True, True, r=['onesf', 'dg'], w=[pzn])
                evac(gbc[:, j * 128:(j + 1) * 128], pz[:, 0:128], r=[pzn], w=['gbc'])

        with ExitStack() as TS:
            def sbt(name, shape, dt=F32):
                return TS.enter_context(SB(name, shape, dt))
            wg = sbt("wg", [128, 8, 2048], BF16)
            won = sbt("won", [128, 4, 1024], BF16)
            wom = sbt("wom", [128, 4, 1024], BF16)
            wout = sbt("wout", [128, 8, 1024], BF16)
            xr = [sbt(f"txr{i}", [128, 1024]) for i in range(2)]
            xn = sbt("txn", [128, 1024])
            hTr = [sbt(f"thT{i}", [128, 8, 128], BF16) for i in range(2)]
            sgA = sbt("sgA", [128, 1024])
            sgB = sbt("sgB", [128, 1024])
            m1 = sbt("m1", [128, 1024])
            m2 = sbt("m2", [128, 1024])
            yT = sbt("yT", [128, 8, 128], BF16)
            yb = sbt("yb", [128, 1024], BF16)
            x1r = [sbt(f"x1r{i}", [128, 1024]) for i in range(2)]
            wiv = kparts(I['w_in'])
            dma(wg[:, :, 0:1024], wiv[:, :, C_GA:C_GA + 1024], w=['wg'], q='pool')
            dma(wg[:, :, 1024:2048], wiv[:, :, C_GB:C_GB + 1024], w=['wg'], q='pool')
            dma(won[:], kparts(I['w_o_nsa']), w=['won'], q='pool')
            dma(wom[:], kparts(I['w_o_mla']), w=['wom'], q='pool')
            dma(wout[:], kparts(I['w_out']), w=['wout'], q='pool')
            bcast_vec(lambda j: modT[:, 16 + j:17 + j])
            m3 = sbt("m3", [128, 1024])
            yT2 = sbt("yT2", [128, 8, 128], BF16)
            xr3 = sbt("txr2", [128, 1024])
            xr = xr + [xr3]
            yTs = [yT, yT2]

            xr4 = sbt("txr3", [128, 1024])
            xr = xr + [xr4]
            xn2 = sbt("txn2", [128, 1024])
            xnr = [xn, xn2]

            def t1_norm(i2):
                s4 = i2 % 4
                dma(xr[s4][:], I['xo'][i2 * 128:(i2 + 1) * 128, :], w=[f"txr{s4}"])
                hT_norm(xr[s4], f"txr{s4}", xnr[i2 % 2], f"txn{i2 % 2}", i2 % 4)

            def t1_tr_post(i2):
                hT_tr(xnr[i2 % 2], f"txn{i2 % 2}")
                hT_post(hTr[i2 % 2], f"thT{i2 % 2}", gsA, shA)

            def t1_A(i):
                isl = slice(i * 128, (i + 1) * 128)
                hT = hTr[i % 2]
                hres = f"thT{i % 2}"
                for n in range(2):
                    pz, pzn = nextZ()
                    for k in range(8):
                        mm(pz[:, :], hT[:, k, :], wg[:, k, n * 512:(n + 1) * 512], k == 0, k == 7, r=['wg', hres], w=[pzn])
                    act(sgA[:, n * 512:(n + 1) * 512], pz[:, :], AF.Sigmoid, r=[pzn], w=['sgA'])
                for n in range(2):
                    pz, pzn = nextZ()
                    for k in range(4):
                        mm(pz[:, :], oTn[:, k, isl], won[:, k, n * 512:(n + 1) * 512], k == 0, k == 3, r=['won', 'oTn'], w=[pzn])
                    tt('dve', m1[:, n * 512:(n + 1) * 512], pz[:, :], sgA[:, n * 512:(n + 1) * 512], ALU.mult, r=[pzn, 'sgA'], w=['m1'])
                for n in range(2):
                    pz, pzn = nextZ()
                    for k in range(8):
                        mm(pz[:, :], hT[:, k, :], wg[:, k, 1024 + n * 512:1024 + (n + 1) * 512], k == 0, k == 7, r=['wg', hres], w=[pzn])
                    act(sgB[:, n * 512:(n + 1) * 512], pz[:, :], AF.Sigmoid, r=[pzn], w=['sgB'])
                if i + 1 < NO:
                    t1_tr_post(i + 1)
                if i + 2 < NO:
                    t1_norm(i + 2)
                for n in range(2):
                    pz, pzn = nextZ()
                    for k in range(4):
                        mm(pz[:, :], oTm[:, k, isl], wom[:, k, n * 512:(n + 1) * 512], k == 0, k == 3, r=['wom', 'oTm'], w=[pzn])
                    tt('dve', m2[:, n * 512:(n + 1) * 512], pz[:, :], sgB[:, n * 512:(n + 1) * 512], ALU.mult, r=[pzn, 'sgB'], w=['m2'])
                tt('dve', yb[:], m1[:], m2[:], ALU.add, r=['m1', 'm2'], w=['yb'])
                for k in range(8):
                    P.op('pe', E('transpose', out=psB[:, k * 128:(k + 1) * 128], in_=yb[:, k * 128:(k + 1) * 128],
                                 identity=identb[:]), r=['yb', 'identb'], w=['psB'])
                yTc = yTs[i % 2]
                yv = yTc[:].rearrange("p k t -> p (k t)")
                act(yv[:, 0:512], psB[:, 0:512], AF.Copy, r=['psB'], w=[f'yT{i % 2}'])
                ts('dve', yv[:, 512:1024], psB[:, 512:1024], 1.0, None, ALU.mult, r=['psB'], w=[f'yT{i % 2}'])

            def t1_B(i):
                isl = slice(i * 128, (i + 1) * 128)
                yTc = yTs[i % 2]
                xt = xr[i % 4]
                x1 = x1r[i % 2]
                x1n = f"x1r{i % 2}"
                for n in range(2):
                    pz, pzn = nextZ()
                    for k in range(8):
                        mm(pz[:, :], yTc[:, k, :], wout[:, k, n * 512:(n + 1) * 512], k == 0, k == 7, r=['wout', f'yT{i % 2}'], w=[pzn])
                    tt('dve', m3[:, n * 512:(n + 1) * 512], pz[:, :], gbc[:, n * 512:(n + 1) * 512], ALU.mult, r=[pzn, 'gbc'], w=['m3'])
                tt('dve', x1[:], m3[:], xt[:], ALU.add, r=['m3', f"txr{i % 4}"], w=[x1n])
                dma(x1s[isl, :], x1[:], r=[x1n], w=['x1s'])
                if i == 0:
                    dump('x1', x1[:], [x1n])

            t1_norm(0)
            t1_tr_post(0)
            t1_norm(1)
            for i in range(NO):
                t1_A(i)
                if i >= 1:
                    t1_B(i - 1)
            t1_B(NO - 1)
        P.barrier()
        OS.close()
        checkpoint('t1')

        with ExitStack() as TS:
            def sbt(name, shape, dt=F32):
                return TS.enter_context(SB(name, shape, dt))
            wf1 = sbt("wf1", [128, 8, 4096], BF16)
            wf2 = sbt("wf2", [128, 32, 1024], BF16)
            gfb = sbt("gfb", [128, 1024])
            xr = [sbt(f"uxr{i}", [128, 1024]) for i in range(2)]
            xn = sbt("uxn", [128, 1024])
            hTr = [sbt(f"uhT{i}", [128, 8, 128], BF16) for i in range(2)]
            aT = [sbt(f"aT{i}", [128, 32, 128], BF16) for i in range(2)]
            rl = [sbt(f"rl{i}", [128, 512]) for i in range(2)]
            o2 = sbt("o2", [128, 1024])
            tmp = sbt("utmp", [128, 1024])
            res = [sbt(f"ures{i}", [128, 1024]) for i in range(2)]
            fs = sbt("fs", [128, 4])
            wf1v = kparts(I['w_fc1'])
            wf2v = kparts(I['w_fc2'])
            for q4 in range(4):
                dma(wf1[:, :, q4 * 1024:(q4 + 1) * 1024], wf1v[:, :, q4 * 1024:(q4 + 1) * 1024], w=[f'wf1_{q4}'], q='pool')
            for q4 in range(4):
                dma(wf2[:, q4 * 8:(q4 + 1) * 8, :], wf2v[:, q4 * 8:(q4 + 1) * 8, :], w=[f'wf2_{q4}'], q='pool')
            bcast_vec(lambda j: gl[:, 21 + j:22 + j])
            P.op('pool', E('tensor_copy', out=gfb[:], in_=gbc[:]), r=['gbc'], w=['gfb'])
            bcast_vec(lambda j: modT[:, 40 + j:41 + j])
            xr = xr + [sbt("uxr2", [128, 1024])]
            xnr = [xn, sbt("uxn2", [128, 1024])]

            def t2_norm(i2):
                s3 = i2 % 3
                dma(xr[s3][:], x1s[i2 * 128:(i2 + 1) * 128, :], r=['x1s'], w=[f"uxr{s3}"])
                hT_norm(xr[s3], f"uxr{s3}", xnr[i2 % 2], f"uxn{i2 % 2}", i2 % 4)

            t2_norm(0)
            hT_tr(xnr[0], "uxn0")
            hT_post(hTr[0], "uhT0", gsM, shM)
            t2_norm(1)
            for i in range(NO):
                sl = i % 2
                xt = xr[i % 3]
                xres = f"uxr{i % 3}"
                isl = slice(i * 128, (i + 1) * 128)
                hT = hTr[sl]
                hres = f"uhT{sl}"
                a = aT[sl]
                an = f"aT{sl}"
                for jg in range(8):
                    pz, pzn = (psZ[jg % 2], f"psZ{jg % 2}")
                    for jj in range(4):
                        j = jg * 4 + jj
                        for k in range(8):
                            mm(pz[:, jj * 128:(jj + 1) * 128], wf1[:, k, j * 128:(j + 1) * 128], hT[:, k, :], k == 0, k == 7,
                               r=[f'wf1_{jg // 2}', hres], w=[pzn])
                    rb = rl[jg % 2]
                    rbn = f"rl{jg % 2}"
                    act(rb[:], pz[:, :], AF.Relu, r=[pzn], w=[rbn])
                    av = a[:, jg * 4:(jg + 1) * 4, :].rearrange("p a b -> p (a b)")
                    tt('pool', av, rb[:], rb[:], ALU.mult, r=[rbn], w=[an])
                if i + 1 < NO:
                    hT_tr(xnr[(i + 1) % 2], f"uxn{(i + 1) % 2}")
                for n in range(2):
                    pz, pzn = (psZ[2 + n], f"psZ{2 + n}")
                    for j in range(32):
                        mm(pz[:, :], a[:, j, :], wf2[:, j, n * 512:(n + 1) * 512], j == 0, j == 31, r=[f'wf2_{j // 8}', an], w=[pzn])
                if i + 1 < NO:
                    hT_post(hTr[(i + 1) % 2], f"uhT{(i + 1) % 2}", gsM, shM)
                if i + 2 < NO:
                    t2_norm(i + 2)
                for n in range(2):
                    pz, pzn = (psZ[2 + n], f"psZ{2 + n}")
                    tt('dve', tmp[:, n * 512:(n + 1) * 512], pz[:, :], gbc[:, n * 512:(n + 1) * 512], ALU.mult, r=[pzn, 'gbc'], w=['utmp'])
                tt('dve', o2[:], tmp[:], xt[:], ALU.add, r=['utmp', xres], w=['o2'])
                act(junk[:], o2[:], AF.Square, r=['o2'], w=['junk', 'fs0'], accum_out=fs[:, 0:1])
                act(fs[:, 1:2], fs[:, 0:1], AF.Sqrt, r=['fs0', 'epsb'], w=['fs1'], scale=1.0 / D, bias=epsb[:, 0:1])
                P.op('dve', E('reciprocal', out=fs[:, 2:3], in_=fs[:, 1:2]), r=['fs1'], w=['fs2'])
                rs_ = res[sl]
                rn = f"ures{sl}"
                stt('dve', rs_[:], o2[:], fs[:, 2:3], gfb[:], ALU.mult, ALU.mult, r=['o2', 'fs2', 'gfb'], w=[rn])
                dma(out[isl, :], rs_[:], r=[rn], w=['out'])
    return nc


_CACHE = {}


def _lay(v, k):
    return np.ascontiguousarray(np.asarray(v, np.float32).reshape(k, 128).T)


def make_in_maps(inputs, ncores=8):
    x = np.asarray(inputs['x'], np.float32)
    shared = {
        'w_ada': np.ascontiguousarray(inputs['w_ada'][0], dtype=np.float32),
        'bada_l': _lay(inputs['b_ada'][0], 48),
        'gmix_l': _lay(inputs['g_mix'][0], 8), 'gmlp_l': _lay(inputs['g_mlp'][0], 8),
        'gcq_l': _lay(inputs['g_cq'][0], 3), 'gckv_l': _lay(inputs['g_ckv'][0], 2),
        'gfin_l': _lay(inputs['g_final'], 8),
        'w_in': np.ascontiguousarray(inputs['w_in'][0], dtype=np.float32),
        'peck_t': np.ascontiguousarray(np.asarray(inputs['pe_ck'][0], np.float32).T),
        'w_ck1': np.ascontiguousarray(inputs['w_ck1'][0], dtype=np.float32),
        'w_ck2': np.ascontiguousarray(inputs['w_ck2'][0], dtype=np.float32),
        'pecv_t': np.ascontiguousarray(np.asarray(inputs['pe_cv'][0], np.float32).T),
        'w_cv1': np.ascontiguousarray(inputs['w_cv1'][0], dtype=np.float32),
        'w_cv2': np.ascontiguousarray(inputs['w_cv2'][0], dtype=np.float32),
        'w_uq': np.ascontiguousarray(inputs['w_uq'][0], dtype=np.float32),
        'w_uk': np.ascontiguousarray(inputs['w_uk'][0], dtype=np.float32),
        'w_uv': np.ascontiguousarray(inputs['w_uv'][0], dtype=np.float32),
        'w_o_nsa': np.ascontiguousarray(inputs['w_o_nsa'][0], dtype=np.float32),
        'w_o_mla': np.ascontiguousarray(inputs['w_o_mla'][0], dtype=np.float32),
        'w_out': np.ascontiguousarray(inputs['w_out'][0], dtype=np.float32),
        'w_fc1': np.ascontiguousarray(inputs['w_fc1'][0], dtype=np.float32),
        'w_fc2': np.ascontiguousarray(inputs['w_fc2'][0], dtype=np.float32),
    }
    consts = [host_consts(0), host_consts(1)]
    in_maps = []
    for core in range(ncores):
        b, p = core // 2, core % 2
        if p == 1:
            xl = np.ascontiguousarray(x[b])
        else:
            xl = np.concatenate([np.zeros((128, D), np.float32), x[b, :S - 128]], axis=0)
        xo = np.ascontiguousarray(xl.reshape(NT, 128, D)[1::2].reshape(NO * 128, D))
        m = dict(shared)
        m['xl'] = xl
        m['xo'] = xo
        m['c_l'] = _lay(np.asarray(inputs['c'], np.float32)[b], 8)
        m.update(consts[p])
        in_maps.append(m)
    return in_maps


def assemble(results, ncores=8):
    outp = np.zeros((4, S, D), np.float32)
    for core in range(ncores):
        b, p = core // 2, core % 2
        o = np.asarray(results[core]['out'], np.float32).reshape(NO, 128, D)
        outp[b].reshape(NT, 128, D)[p::2] = o
    return outp


def kernel(**inputs):
    if 'nc' not in _CACHE:
        _CACHE['nc'] = build_program()
    nc = _CACHE['nc']
    in_maps = make_in_maps(inputs)
    res = run_bass_kernel_spmd(nc, in_maps, core_ids=list(range(8)))
    return assemble(res.results)
```

```python
import numpy as np
import ml_dtypes
from contextlib import ExitStack
import concourse.bass as bass
import concourse.mybir as mybir
from concourse.bass_utils import run_bass_kernel_spmd

F32 = mybir.dt.float32
BF16 = mybir.dt.bfloat16
AF = mybir.ActivationFunctionType
ALU = mybir.AluOpType
NPBF = ml_dtypes.bfloat16

D = 1024
S = 8192
NT = 64
NO = 32
NEG = -30000.0
EPS = 1e-6
GEN = 8192
EVAC_ACT_ONLY = True
NDSEM = 12
SC_NSA = 0.125
SC_MLA = 96.0 ** -0.5
C_Q, C_KC, C_VC, C_KS, C_VS, C_KW, C_VW, C_G, C_QD, C_KVD, C_KR, C_GA, C_GB = (
    0, 512, 640, 768, 896, 1024, 1152, 1280, 1304, 1688, 1944, 1976, 3000)


class Prog:
    ENGS = ('pe', 'act', 'dve', 'pool', 'sp')

    def __init__(self, nc):
        self.nc = nc
        self.ops = {e: [] for e in self.ENGS}
        self.cnt = {e: 0 for e in self.ENGS}
        self.lastw = {}
        self.readers = {}
        self.dcnt = {}
        self.dnext = {e: 0 for e in self.ENGS}
        self.floor = {}

    def _deps(self, r, w):
        deps = dict(self.floor)

        def add(tok):
            if tok is None:
                return
            k, v = tok
            if deps.get(k, 0) < v:
                deps[k] = v
        for x in r:
            add(self.lastw.get(x))
        for x in w:
            add(self.lastw.get(x))
            for t in self.readers.get(x, ()):
                add(t)
        return deps

    def _commit(self, tok, r, w):
        for x in r:
            self.readers.setdefault(x, []).append(tok)
        for x in w:
            self.lastw[x] = tok
            self.readers[x] = []

    def barrier(self):
        fl = {}
        for e in self.ENGS:
            c = self.cnt[e]
            if c > 0:
                fl[(e, (c - 1) // GEN)] = (c - 1) % GEN + 1
        for key, n in self.dcnt.items():
            fl[key] = 16 * n
        self.floor = fl
        self.lastw = {}
        self.readers = {}

    def op(self, eng, fn, r=(), w=()):
        deps = self._deps(r, w)
        idx = self.cnt[eng]
        self.cnt[eng] += 1
        tok = ((eng, idx // GEN), idx % GEN + 1)
        if eng == 'pe':
            deps = {k: v for k, v in deps.items() if k[0] != 'pe'}
        self.ops[eng].append(('c', fn, deps, tok))
        self._commit(tok, r, w)
        return tok

    def dma(self, q, fn, r=(), w=()):
        deps = self._deps(r, w)
        slot = self.dnext[q] % NDSEM
        self.dnext[q] += 1
        key = ('dma_' + q, slot)
        n = self.dcnt.get(key, 0)
        if n > 0 and deps.get(key, 0) < 16 * n:
            deps[key] = 16 * n
        self.dcnt[key] = n + 1
        tok = (key, 16 * (n + 1))
        self.ops[q].append(('d', fn, deps, tok))
        self._commit(tok, r, w)
        return tok

    def emit(self):
        nc = self.nc
        with ExitStack() as es:
            sems = {}
            for e in self.ENGS:
                for g in range((self.cnt[e] + GEN - 1) // GEN):
                    sems[(e, g)] = es.enter_context(nc.semaphore(f"s_{e}_{g}"))
            for key in self.dcnt:
                sems[key] = es.enter_context(nc.semaphore(f"s_{key[0]}_{key[1]}"))
            block = es.enter_context(nc.Block())
            engobj = {'pe': 'tensor', 'act': 'scalar', 'dve': 'vector', 'pool': 'gpsimd', 'sp': 'sync'}

            def make(ename):
                def body(e):
                    waited = {}
                    for kind, fn, deps, tok in self.ops[ename]:
                        for k, v in deps.items():
                            if waited.get(k, 0) < v:
                                e.wait_ge(sems[k], v)
                                waited[k] = v
                        ins = fn(e)
                        ins.then_inc(sems[tok[0]], 1 if kind == 'c' else 16)
                    if ename == 'sp':
                        fin = {}
                        for e2 in self.ENGS:
                            c = self.cnt[e2]
                            if c > 0:
                                fin[(e2, (c - 1) // GEN)] = (c - 1) % GEN + 1
                        for key, n in self.dcnt.items():
                            fin[key] = 16 * n
                        for k, v in fin.items():
                            if waited.get(k, 0) < v:
                                e.wait_ge(sems[k], v)
                return body
            for ename in self.ENGS:
                getattr(block, engobj[ename])(make(ename))


def host_consts(p):
    shift = 128 * (1 - p)
    c = {}
    c['ident'] = np.eye(128, dtype=np.float32)
    c['identb'] = np.eye(128, dtype=np.float32).astype(NPBF)
    k = np.arange(128)[:, None]
    q = np.arange(128)[None, :]
    c['tri'] = np.where(k > q, NEG, 0.0).astype(NPBF)
    c['band'] = np.where(k <= q, NEG, 0.0).astype(NPBF)
    half = 16
    inv_freq = (10000.0 ** (-np.arange(half, dtype=np.float32) / half)).astype(np.float32)
    L = np.arange(S)
    gpos = (L - shift).astype(np.float32)
    ang = (gpos[:, None] * inv_freq[None, :]).astype(np.float32)
    cos2 = np.concatenate([np.cos(ang), np.cos(ang)], axis=1).T.astype(np.float32)
    sin2 = np.concatenate([np.sin(ang), np.sin(ang)], axis=1).T.astype(np.float32)
    c['cosk'] = np.ascontiguousarray(cos2)
    c['sink'] = np.ascontiguousarray(sin2)
    own = (np.arange(NO)[:, None] * 256 + 128 + np.arange(128)[None, :]).reshape(-1)
    c['cosq'] = np.ascontiguousarray(cos2[:, own])
    c['sinq'] = np.ascontiguousarray(sin2[:, own])
    ka = np.zeros((5, S), np.float32)
    ka[0] = 1.0
    ka[1] = 1.0
    ka[2] = 128.0 * (L // 128)
    ka[3] = L % 128
    ka[4] = (L < shift).astype(np.float32)
    c['KA'] = ka.astype(NPBF)
    li = np.arange(512)
    cend = 16 * li + 31
    kca = np.zeros((5, 512), np.float32)
    kca[0] = 1.0
    kca[1] = 1.0
    kca[2] = 128.0 * (cend // 128)
    kca[3] = cend % 128
    kca[4] = (16 * li < shift).astype(np.float32)
    c['KCA'] = kca.astype(NPBF)
    slopes = 2.0 ** (-8.0 * np.arange(1, 9) / 8.0)
    qa = np.zeros((8, 5, NO * 128), np.float32)
    for h in range(8):
        cc = slopes[h] / SC_NSA
        qa[h, 0] = -cc * 128.0 * (own // 128)
        qa[h, 1] = -cc * (own % 128)
        qa[h, 2] = cc
        qa[h, 3] = cc
        qa[h, 4] = NEG
    c['QAh'] = qa.astype(NPBF)
    e32 = np.zeros((32, 16, 128), np.float32)
    for v in range(16):
        e32[2 * v, v, 0:64] = 1.0
        e32[2 * v + 1, v, 64:128] = 1.0

    ef = np.zeros((59, S), np.float32)
    lbt = (L // 64)
    for j in range(58):
        ef[j] = (lbt % 58 == j)
    c['EF'] = ef.astype(NPBF)
    cm = np.zeros((128, 8, 128), np.float32)
    for mi in range(8):
        m = 2 * mi + 1
        delta = 128 * m
        valid = (16 * k + 31) <= (delta + q)
        cm[:, mi, :] = np.where(valid, 0.0, NEG)
    c['cmask'] = cm.astype(NPBF)
    lia = np.arange(512)[:, None]
    lb = np.arange(128)[None, :]
    ov = ((16 * lia < 64 * lb + 64) & (16 * lia + 31 >= 64 * lb)).astype(np.float32)
    c['OVL'] = np.ascontiguousarray(ov.reshape(4, 128, 128).transpose(1, 0, 2)).astype(NPBF)
    bon = np.zeros((NO, 128, 128), np.float32)
    blk0 = 2 * (1 - p)
    for i in range(NO):
        t = 128 * (2 * i + 1) + np.arange(128)[:, None]
        cur = t // 64
        lbb = np.arange(128)[None, :]
        valid = (lbb <= cur) & (lbb >= blk0)
        forced = (lbb == blk0) | (lbb >= cur - 1)
        bon[i] = np.where(valid, np.where(forced, 1000.0, 0.0), np.where(lbb > cur, -1e9, -2e9))
    c['bonus'] = bon
    return c


CONST_SHAPES = {
    'ident': ([128, 128], F32), 'identb': ([128, 128], BF16), 'tri': ([128, 128], BF16), 'band': ([128, 128], BF16),
    'cosk': ([32, S], F32), 'sink': ([32, S], F32), 'cosq': ([32, NO * 128], F32), 'sinq': ([32, NO * 128], F32),
    'KA': ([5, S], BF16), 'KCA': ([5, 512], BF16), 'QAh': ([8, 5, NO * 128], BF16), 'EF': ([59, S], BF16),
    'cmask': ([128, 8, 128], BF16), 'OVL': ([128, 4, 128], BF16), 'bonus': ([NO, 128, 128], F32),
}
IN_SHAPES = {
    'xl': [S, D], 'xo': [NO * 128, D], 'c_l': [128, 8], 'w_ada': [D, 6 * D], 'bada_l': [128, 48],
    'gmix_l': [128, 8], 'gmlp_l': [128, 8], 'gcq_l': [128, 3], 'gckv_l': [128, 2], 'gfin_l': [128, 8],
    'w_in': [D, 4024], 'peck_t': [64, 32], 'w_ck1': [2048, 256], 'w_ck2': [256, 64],
    'pecv_t': [64, 32], 'w_cv1': [2048, 256], 'w_cv2': [256, 64],
    'w_uq': [384, 768], 'w_uk': [256, 512], 'w_uv': [256, 512], 'w_o_nsa': [512, D], 'w_o_mla': [512, D],
    'w_out': [D, D], 'w_fc1': [D, 4 * D], 'w_fc2': [4 * D, D],
}


class StopBuild(Exception):
    pass


def build_program(debug=None, stop=None):
    try:
        return _build_program(debug, stop)
    except StopBuild as ex:
        return ex.args[0]


def _build_program(debug=None, stop=None):
    nc = bass.Bass("TRN2", target_bir_lowering=False)
    I = {}
    for name, shp in IN_SHAPES.items():
        I[name] = nc.dram_tensor(name, shp, F32, kind="ExternalInput").ap()
    for name, (shp, dt) in CONST_SHAPES.items():
        I[name] = nc.dram_tensor(name, shp, dt, kind="ExternalInput").ap()
    out = nc.dram_tensor("out", [NO * 128, D], F32, kind="ExternalOutput").ap()
    x1s = nc.dram_tensor("x1s", [NO * 128, D], F32, kind="Internal").ap()
    dbg = {}
    if debug:
        for name, shp in debug.items():
            dbg[name] = nc.dram_tensor("dbg_" + name, shp, F32, kind="ExternalOutput").ap()

    P = Prog(nc)
    rr = {'ev': 0}

    def E(meth, **kw):
        return lambda e: getattr(e, meth)(**kw)

    def SB(name, shape, dt=F32):
        return nc.sbuf_tensor("sb_" + name, shape, dt)

    def kparts(ap, p=128):
        return ap.rearrange("(k p) n -> p k n", p=p)

    def checkpoint(name):
        if stop == name:
            raise StopBuild(nc)

    with ExitStack() as G:
        G.callback(P.emit)

        def sbg(name, shape, dt=F32):
            return G.enter_context(SB(name, shape, dt))
        psT = G.enter_context(nc.psum_tensor("psT", [128, 1024], F32))
        psZ = [G.enter_context(nc.psum_tensor(f"psZ{i}", [128, 512], F32)) for i in range(4)]
        psO = G.enter_context(nc.psum_tensor("psO", [128, 512], F32))
        psB = G.enter_context(nc.psum_tensor("psB", [128, 1024], BF16))
        ident = sbg("ident", [128, 128]); identb = sbg("identb", [128, 128], BF16)
        tri = sbg("tri", [128, 128], BF16); band = sbg("band", [128, 128], BF16)
        onesf = sbg("onesf", [128, 128])
        epsb = sbg("epsb", [128, 1])
        modT = sbg("modT", [128, 48])
        gsA = sbg("gsA", [128, 8]); gsM = sbg("gsM", [128, 8])
        gl = sbg("gl", [128, 8 + 8 + 3 + 2 + 8])
        ssr = sbg("ssr", [128, 16])
        junk = sbg("junk", [128, 1024], BF16)
        gbc = sbg("gbc", [128, 1024])
        dg = sbg("dg", [128, 128])
        OS = ExitStack()
        oTn = OS.enter_context(SB("oTn", [128, 4, NO * 128], BF16))

        def dma(out_ap, in_ap, r=(), w=(), q='sp'):
            return P.dma(q, lambda e: e.dma_start(out=out_ap, in_=in_ap), r=r, w=w)

        def mm(out_ap, lhsT, rhs, start, stop, r=(), w=(), skip=False):
            if skip:
                return P.op('pe', lambda e: e.matmul(out_ap, lhsT=lhsT, rhs=rhs, start=start, stop=stop,
                                                     skip_group_check=True), r=r, w=w)
            return P.op('pe', lambda e: e.matmul(out_ap, lhsT=lhsT, rhs=rhs, start=start, stop=stop), r=r, w=w)

        def act(out_ap, in_ap, func, r=(), w=(), **kw):
            return P.op('act', lambda e: e.activation(out=out_ap, in_=in_ap, func=func, **kw), r=r, w=w)

        def evac(out_ap, in_ap, r=(), w=()):
            rr['ev'] += 1
            if EVAC_ACT_ONLY or rr['ev'] % 2:
                return P.op('act', lambda e: e.activation(out=out_ap, in_=in_ap, func=AF.Copy), r=r, w=w)
            return P.op('dve', lambda e: e.tensor_scalar(out=out_ap, in0=in_ap, scalar1=1.0, scalar2=None, op0=ALU.mult), r=r, w=w)

        def ts(eng, out_ap, in0, s1, s2, op0, op1=None, r=(), w=()):
            if op1 is None:
                return P.op(eng, lambda e: e.tensor_scalar(out=out_ap, in0=in0, scalar1=s1, scalar2=None, op0=op0), r=r, w=w)
            return P.op(eng, lambda e: e.tensor_scalar(out=out_ap, in0=in0, scalar1=s1, scalar2=s2, op0=op0, op1=op1), r=r, w=w)

        def tt(eng, out_ap, in0, in1, op, r=(), w=()):
            return P.op(eng, lambda e: e.tensor_tensor(out=out_ap, in0=in0, in1=in1, op=op), r=r, w=w)

        def stt(eng, out_ap, in0, scalar, in1, op0, op1, r=(), w=()):
            return P.op(eng, lambda e: e.scalar_tensor_tensor(out=out_ap, in0=in0, scalar=scalar, in1=in1, op0=op0, op1=op1), r=r, w=w)

        def memset(eng, ap, val, w=()):
            return P.op(eng, lambda e: e.memset(ap, val), w=w)

        def dump(name, ap, r):
            if name in dbg:
                dma(dbg[name], ap, r=r, w=['dbg_' + name], q='pool')

        dma(ident[:], I['ident'][:, :], w=['ident'])
        dma(identb[:], I['identb'][:, :], w=['identb'])
        dma(tri[:], I['tri'][:, :], w=['tri'])
        dma(band[:], I['band'][:, :], w=['band'])
        dma(gl[:, 0:8], I['gmix_l'][:, :], w=['gl'])
        dma(gl[:, 8:16], I['gmlp_l'][:, :], w=['gl'])
        dma(gl[:, 16:19], I['gcq_l'][:, :], w=['gl'])
        dma(gl[:, 19:21], I['gckv_l'][:, :], w=['gl'])
        dma(gl[:, 21:29], I['gfin_l'][:, :], w=['gl'])
        memset('pool', onesf[:], 1.0, w=['onesf'])
        memset('pool', epsb[:], EPS, w=['epsb'])

        with ExitStack() as es:
            wad = [es.enter_context(SB(f"wad{i}", [128, 8, 512], F32)) for i in range(2)]
            cT = es.enter_context(SB("cT", [128, 8], F32))
            bl = es.enter_context(SB("bl", [128, 48], F32))
            dma(cT[:], I['c_l'][:, :], w=['cT'])
            dma(bl[:], I['bada_l'][:, :], w=['bl'])
            wv = kparts(I['w_ada'])
            for piece in range(12):
                buf = wad[piece % 2]
                bn = f"wad{piece % 2}"
                dma(buf[:], wv[:, :, piece * 512:(piece + 1) * 512], w=[bn])
                for jj in range(4):
                    j = piece * 4 + jj
                    for k in range(8):
                        mm(psZ[0][:, j:j + 1], buf[:, k, jj * 128:(jj + 1) * 128], cT[:, k:k + 1],
                           k == 0, k == 7, r=[bn, 'cT'], w=['psZ0'])
            tt('dve', modT[:], psZ[0][:, 0:48], bl[:], ALU.add, r=['psZ0', 'bl'], w=['modT'])
            stt('dve', gsA[:], modT[:, 8:16], 1.0, gl[:, 0:8], ALU.add, ALU.mult, r=['modT', 'gl'], w=['gsA'])
            stt('dve', gsM[:], modT[:, 32:40], 1.0, gl[:, 8:16], ALU.add, ALU.mult, r=['modT', 'gl'], w=['gsM'])
            dump('modT', modT[:], ['modT'])
        P.barrier()
        checkpoint('p0')
        shA = modT[:, 0:8]
        shM = modT[:, 24:32]

        def make_hT(xin, xin_res, xn, xn_res, hT, hT_res, gs, sh, slot):
            hT_pre(xin, xin_res, xn, xn_res, slot)
            hT_post(hT, hT_res, gs, sh)

        def hT_pre(xin, xin_res, xn, xn_res, slot):
            hT_norm(xin, xin_res, xn, xn_res, slot)
            hT_tr(xn, xn_res)

        def hT_norm(xin, xin_res, xn, xn_res, slot):
            act(junk[:], xin[:], AF.Square, r=[xin_res], w=['junk', f'ss{slot}'], accum_out=ssr[:, slot:slot + 1])
            act(ssr[:, slot + 4:slot + 5], ssr[:, slot:slot + 1], AF.Sqrt, r=[f'ss{slot}', 'epsb'], w=[f'sq{slot}'],
                scale=1.0 / D, bias=epsb[:, 0:1])
            P.op('dve', E('reciprocal', out=ssr[:, slot + 8:slot + 9], in_=ssr[:, slot + 4:slot + 5]),
                 r=[f'sq{slot}'], w=[f'rs{slot}'])
            rstd = ssr[:, slot + 8:slot + 9]
            act(xn[:, 0:512], xin[:, 0:512], AF.Identity, r=[xin_res, f'rs{slot}'], w=[xn_res + 'a'], scale=rstd)
            ts('dve', xn[:, 512:1024], xin[:, 512:1024], rstd, None, ALU.mult, r=[xin_res, f'rs{slot}'], w=[xn_res + 'b'])

        def hT_tr(xn, xn_res):
            for k in range(8):
                hf = 'a' if k < 4 else 'b'
                P.op('pe', E('transpose', out=psT[:, k * 128:(k + 1) * 128], in_=xn[:, k * 128:(k + 1) * 128],
                             identity=ident[:]),
                     r=[xn_res + hf, 'ident'], w=['psT' + hf])

        def hT_post(hT, hT_res, gs, sh, dve_share=2):
            for k in range(8):
                hf = 'a' if k < 4 else 'b'
                if k % dve_share == 0:
                    ts('dve', hT[:, k, :], psT[:, k * 128:(k + 1) * 128], gs[:, k:k + 1], sh[:, k:k + 1], ALU.mult, ALU.add,
                       r=['psT' + hf, 'gs', 'modT'], w=[hT_res])
                else:
                    act(hT[:, k, :], psT[:, k * 128:(k + 1) * 128], AF.Identity, r=['psT' + hf, 'gs', 'modT'], w=[hT_res],
                        scale=gs[:, k:k + 1], bias=sh[:, k:k + 1])

        dq = []
        LA = 2

        def defer(fn):
            dq.append(fn)
            while len(dq) > LA:
                dq.pop(0)()

        def flush():
            while dq:
                dq.pop(0)()

        zring = {'i': 0, 'n': 4}

        def nextZ():
            zring['i'] = (zring['i'] + 1) % zring['n']
            return psZ[zring['i']], f"psZ{zring['i']}"

        for g in range(2):
            with ExitStack() as NS:
                def sbn(name, shape, dt=F32):
                    return NS.enter_context(SB(f"{name}_g{g}", shape, dt))
                zqT = sbn("zqT", [128, 2, NO * 128], BF16)
                ksA = sbn("ksA", [128, S], BF16)
                kwA = sbn("kwA", [128, S], BF16)
                vsA = sbn("vsA", [128, NT, 66], BF16)
                vwA = sbn("vwA", [128, NT, 66], BF16)
                sg = sbn("sg", [128, NO, 24])
                kcA = sbn("kcA", [128, 512], BF16)
                VCA = sbn("VCA", [128, 4, 194], BF16)
                memset('pool', vsA[:, :, 64:66], 1.0, w=['vsA'])
                memset('pool', vwA[:, :, 64:66], 1.0, w=['vwA'])
                memset('pool', kcA[:], 0.0, w=['kcA'])
                memset('pool', VCA[:, :, 192:194], 1.0, w=['VCA'])
                checkpoint('n_a')
                dma(VCA[:, :, 64:192], I['OVL'][:, :, :], w=['VCA'])
                checkpoint('n_b')
                with ExitStack() as PS:
                    def sbp(name, shape, dt=F32):
                        return PS.enter_context(SB(f"{name}_g{g}", shape, dt))
                    wn = sbp("wn", [128, 8, 664], BF16)
                    xr = [sbp(f"xr{i}", [128, 1024]) for i in range(3)]
                    hTr = [sbp(f"hTr{i}", [128, 8, 128], BF16) for i in range(2)]
                    kcT = sbp("kcT", [128, S], BF16)
                    vcT = sbp("vcT", [128, S], BF16)
                    w1 = sbp("w1", [64, 32, 256], BF16)
                    w2 = sbp("w2", [128, 2, 64], BF16)
                    peT = sbp("peT", [64, 32], BF16)
                    hb = sbp("hb", [128, 2])
                    hid = sbp("hid", [128, 2, 512], BF16)
                    wiv = kparts(I['w_in'])
                    colmap = [(0, C_Q + 256 * g, 256), (256, C_KC + 64 * g, 64), (320, C_VC + 64 * g, 64),
                              (384, C_KS + 64 * g, 64), (448, C_VS + 64 * g, 64), (512, C_KW + 64 * g, 64),
                              (576, C_VW + 64 * g, 64), (640, C_G, 24)]
                    for (d0, s0, n) in colmap:
                        dma(wn[:, :, d0:d0 + n], wiv[:, :, s0:s0 + n], w=['wn'], q='pool')
                    print("sbuf remaining after proj alloc", nc.sbuf_bytes_remaining, flush=True)
                    checkpoint('n_w')
                    for lt in range(NT):
                        if lt == 1:
                            checkpoint('n_t0')
                        if lt == 2:
                            checkpoint('n_t1')
                        sl = lt % 2
                        hT = hTr[sl]
                        hres = f"hT{sl}"

                        def nsa_norm(t2):
                            s3 = t2 % 3
                            dma(xr[s3][:], I['xl'][t2 * 128:(t2 + 1) * 128, :], w=[f"xr{s3}a", f"xr{s3}b", f"xr{s3}"])
                            hT_norm(xr[s3], f"xr{s3}", xr[s3], f"xr{s3}", t2 % 4)

                        def nsa_tr_post(t2):
                            hT_tr(xr[t2 % 3], f"xr{t2 % 3}")
                            hT_post(hTr[t2 % 2], f"hT{t2 % 2}", gsA, shA, dve_share=1)
                        if lt == 0:
                            nsa_norm(0)
                            nsa_tr_post(0)
                            nsa_norm(1)
                        if lt + 1 < NT:
                            nsa_tr_post(lt + 1)
                        tsl = slice(lt * 128, (lt + 1) * 128)
                        pz, pzn = nextZ()
                        for j, c0 in enumerate((256, 320, 384, 512)):
                            for k in range(8):
                                mm(pz[0:64, j * 128:(j + 1) * 128], wn[:, k, c0:c0 + 64], hT[:, k, :], k == 0, k == 7,
                                   r=['wn', hres], w=[pzn])
                        pv, pvn = nextZ()
                        for j, c0 in enumerate((448, 576)):
                            for k in range(8):
                                mm(pv[:, j * 64:(j + 1) * 64], hT[:, k, :], wn[:, k, c0:c0 + 64], k == 0, k == 7,
                                   r=['wn', hres], w=[pvn])
                        own = (lt % 2 == 1)
                        i = lt // 2
                        if own:
                            for n2 in range(2):
                                for k in range(8):
                                    mm(pv[:, 128 + n2 * 128:256 + n2 * 128], wn[:, k, n2 * 128:(n2 + 1) * 128], hT[:, k, :],
                                       k == 0, k == 7, r=['wn', hres], w=[pvn])
                            for k in range(8):
                                mm(pv[:, 384:408], hT[:, k, :], wn[:, k, 640:664], k == 0, k == 7, r=['wn', hres], w=[pvn])
                        if lt + 2 < NT:
                            nsa_norm(lt + 2)
                        evac(kcT[0:64, tsl], pz[0:64, 0:128], r=[pzn], w=['kcT'])
                        evac(vcT[0:64, tsl], pz[0:64, 128:256], r=[pzn], w=['vcT'])
                        evac(ksA[0:64, tsl], pz[0:64, 256:384], r=[pzn], w=['ksA'])
                        evac(kwA[0:64, tsl], pz[0:64, 384:512], r=[pzn], w=['kwA'])
                        evac(vsA[:, lt, 0:64], pv[:, 0:64], r=[pvn], w=['vsA'])
                        evac(vwA[:, lt, 0:64], pv[:, 64:128], r=[pvn], w=['vwA'])
                        if own:
                            evac(zqT[:, 0, i * 128:(i + 1) * 128], pv[:, 128:256], r=[pvn], w=['zqT'])
                            evac(zqT[:, 1, i * 128:(i + 1) * 128], pv[:, 256:384], r=[pvn], w=['zqT'])
                            act(sg[:, i, :], pv[:, 384:408], AF.Sigmoid, r=[pvn], w=['sg'])
                    checkpoint('n_tiles')
                    dma(ksA[64:69, :], I['KA'][:, :], w=['ksA'])
                    dma(ksA[69:128, :], I['EF'][:, :], w=['ksA'])
                    dma(kwA[64:69, :], I['KA'][:, :], w=['kwA'])
                    dma(kcA[64:69, :], I['KCA'][:, :], w=['kcA'])
                    checkpoint('n_aug')
                    for which in range(2):
                        src = kcT if which == 0 else vcT
                        srcn = 'kcT' if which == 0 else 'vcT'
                        w1d = I['w_ck1'] if which == 0 else I['w_cv1']
                        w2d = I['w_ck2'] if which == 0 else I['w_cv2']
                        ped = I['peck_t'] if which == 0 else I['pecv_t']
                        dma(w1[:], w1d.rearrange("(l d) h -> d l h", d=64), w=['w1'], q='pool')
                        dma(w2[:], kparts(w2d), w=['w2'], q='pool')
                        dma(peT[:], ped[:, :], w=['peT'], q='pool')
                        pz, pzn = nextZ()
                        for hc in range(2):
                            for l in range(32):
                                mm(pz[:, hc:hc + 1], w1[:, l, hc * 128:(hc + 1) * 128], peT[:, l:l + 1], l == 0, l == 31,
                                   r=['w1', 'peT'], w=[pzn])
                        evac(hb[:], pz[:, 0:2], r=[pzn], w=['hb'])
                        memset('pool', hid[:], 0.0, w=['hid'])
                        for hc in range(2):
                            pz, pzn = nextZ()
                            for l in range(32):
                                mm(pz[:, 0:511], w1[:, l, hc * 128:(hc + 1) * 128], src[0:64, l:l + 16 * 510 + 1:16],
                                   l == 0, l == 31, r=['w1', srcn], w=[pzn])
                            act(hid[:, hc, 0:511], pz[:, 0:511], AF.Silu, r=[pzn, 'hb'], w=['hid'], bias=hb[:, hc:hc + 1])
                        if which == 0:
                            pz, pzn = nextZ()
                            for hc in range(2):
                                mm(pz[0:64, 0:511], w2[:, hc, :], hid[:, hc, 0:511], hc == 0, hc == 1, r=['w2', 'hid'], w=[pzn])
                            evac(kcA[0:64, 0:511], pz[0:64, 0:511], r=[pzn], w=['kcA'])
                        else:
                            pz, pzn = nextZ()
                            for ct in range(4):
                                for hc in range(2):
                                    mm(pz[:, ct * 64:(ct + 1) * 64], hid[:, hc, ct * 128:(ct + 1) * 128], w2[:, hc, :],
                                       hc == 0, hc == 1, r=['w2', 'hid'], w=[pzn])
                            evac(VCA[:, :, 0:64], pz[:, 0:256].rearrange("p (a b) -> p a b", b=64), r=[pzn], w=['VCA'])
                    if g == 0:
                        dump('kcA', kcA[0:64, :], ['kcA'])
                        dump('ksA', ksA[0:64, 0:1024], ['ksA'])
                        dump('zqT', zqT[:, 0, 0:512], ['zqT'])
                P.barrier()
                checkpoint(f'nproj{g}')
                with ExitStack() as AS:
                    def sba(name, shape, dt=F32):
                        return AS.enter_context(SB(f"{name}_g{g}", shape, dt))
                    QA = [sba(f"QA{h}", [128, NO * 128], BF16) for h in range(4)]
                    PT = [sba(f"PT{i}", [128, 512], BF16) for i in range(4)]
                    PTw = [sba(f"PTw{i}", [128, 640], BF16) for i in range(3)]
                    Zm4 = [[sba(f"Zm{q_}_{i}", [128, 128]) for i in range(3)] for q_ in range(4)]
                    late_tr = []
                    RqAll = [[sba(f"Rq{p_}_{i}", [128, 512], BF16) for i in range(3)] for p_ in range(2)]
                    ocmpAll = [sba(f"ocmp{p_}", [128, 4, 4, 64]) for p_ in range(2)]
                    ostage = sba("ostage", [128, NO, 256], BF16)
                    cmask = sba("cmask", [128, 8, 128], BF16)
                    bon = [sba(f"bon{i}", [128, 128]) for i in range(2)]
                    pslc = sba("pslc", [128, 128])
                    scb = sba("scb", [128, 128])
                    mrb = sba("mrb", [128, 128])
                    mx = sba("mx", [128, 16])
                    negm = sba("negm", [128, 128])
                    sm = sba("sm", [128, 64])
                    t1b = sba("t1b", [128, 4, 64])
                    zring['n'] = 3
                    for q_ in range(4):
                        for r_ in range(3):
                            memset('pool', Zm4[q_][r_][:], 0.0, w=[f'Zm{q_}_{r_}'])
                    dma(cmask[:], I['cmask'][:, :, :], w=['cmask'])
                    for hg in range(4):
                        h = 4 * g + hg
                        r0 = (hg % 2) * 64
                        P.op('pool', E('tensor_copy', out=QA[hg][0:64, :], in_=zqT[r0:r0 + 64, hg // 2, :]),
                             r=['zqT'], w=[f'QA{hg}'])
                        dma(QA[hg][64:69, :], I['QAh'][h, :, :], w=[f'QA{hg}'])
                    pti = {'i': 0, 'w': 0}
                    def cmp_topk(c):
                        for qi in range(4):
                            i = 4 * c + qi
                            qt = 2 * i + 1
                            qsl = slice(i * 128, (i + 1) * 128)
                            ctm = (qt - 1) // 16
                            bsl = i % 2
                            dma(bon[bsl][:], I['bonus'][i, :, :], w=[f'bon{bsl}'])
                            for hg in range(4):
                                h = 4 * g + hg
                                pz, pzn = nextZ()
                                for ct in range(ctm + 1):
                                    m = qt - 16 * ct
                                    partial = m < 17
                                    mm(pz[:, ct * 128:(ct + 1) * 128], kcA[0:69, ct * 128:(ct + 1) * 128], QA[hg][0:69, qsl],
                                       True, not partial, r=['kcA', f'QA{hg}'], w=[pzn])
                                    if partial:
                                        mm(pz[:, ct * 128:(ct + 1) * 128], identb[:], cmask[:, (m - 1) // 2, :], False, True,
                                           r=['identb', 'cmask'], w=[pzn])
                                ptc = PT[pti['i'] % 4]
                                ptn = f"PT{pti['i'] % 4}"
                                pti['i'] += 1
                                ncol = (ctm + 1) * 128
                                act(ptc[:, 0:ncol], pz[:, 0:ncol], AF.Exp, r=[pzn], w=[ptn], scale=SC_NSA)
                                def cmpB(hg=hg, h=h, i=i, qi=qi, ctm=ctm, ptc=ptc, ptn=ptn):
                                    ub = 'psTa' if hg < 2 else 'psTb'
                                    uo = hg * 256
                                    for ct in range(ctm + 1):
                                        mm(psT[:, uo:uo + 193], ptc[:, ct * 128:(ct + 1) * 128], VCA[:, ct, 0:193],
                                           ct == 0, ct == ctm, r=[ptn, 'VCA'], w=[ub])
                                    ts('dve', sm[:, hg:hg + 1], psT[:, uo + 192:uo + 193], 1e-30, None, ALU.max, r=[ub], w=[f'sm{hg}'])
                                    P.op('dve', E('reciprocal', out=sm[:, 4 + hg:5 + hg], in_=sm[:, hg:hg + 1]),
                                         r=[f'sm{hg}'], w=[f'smr{hg}'])
                                    if hg == 0:
                                        ts('dve', pslc[:], psT[:, uo + 64:uo + 192], sm[:, 4 + hg:5 + hg], None, ALU.mult,
                                           r=[ub, f'smr{hg}'], w=['pslc'])
                                    else:
                                        stt('dve', pslc[:], psT[:, uo + 64:uo + 192], sm[:, 4 + hg:5 + hg], pslc[:], ALU.mult, ALU.add,
                                            r=[ub, f'smr{hg}', 'pslc'], w=['pslc'])
                                    tt('dve', sm[:, 8 + hg:9 + hg], sm[:, 4 + hg:5 + hg], sg[:, i, h:h + 1], ALU.mult,
                                       r=[f'smr{hg}', 'sg'], w=[f'smg{hg}'])
                                    ts('dve', ocmpAll[c % 2][:, qi, hg, :], psT[:, uo:uo + 64], sm[:, 8 + hg:9 + hg], None, ALU.mult,
                                       r=[ub, f'smg{hg}'], w=[f'ocmp{c % 2}'])
                                defer(cmpB)
                            flush()
                            tt('dve', scb[:], pslc[:], bon[bsl][:], ALU.add, r=['pslc', f'bon{bsl}'], w=['scb'])
                            P.op('dve', E('max', out=mx[:, 0:8], in_=scb[:]), r=['scb'], w=['mx0'])
                            P.op('dve', E('match_replace', out=mrb[:], in_to_replace=mx[:, 0:8], in_values=scb[:],
                                                                  imm_value=-3e9), r=['scb', 'mx0'], w=['mrb'])
                            P.op('dve', E('max', out=mx[:, 8:16], in_=mrb[:]), r=['mrb'], w=['mx1'])
                            for r_ in range((2 * qt + 1) // 58 + 1):
                                nb = min(58, 128 - 58 * r_)
                                ts('dve', Zm4[qi][r_][:, 69:69 + nb], scb[:, 58 * r_:58 * r_ + nb], mx[:, 15:16], NEG, ALU.is_lt, ALU.mult,
                                   r=['scb', 'mx1'], w=[f'Zm{qi}_{r_}'])
                                def trB(r_=r_, qi=qi, c=c, zt=Zm4[qi][r_], ztn=f'Zm{qi}_{r_}'):
                                    pz, pzn = nextZ()
                                    P.op('pe', E('transpose', out=pz[:, 0:128], in_=zt[:], identity=ident[:]),
                                         r=[ztn, 'ident'], w=[pzn])
                                    act(RqAll[c % 2][r_][64:128, qi * 128:(qi + 1) * 128], pz[64:128, 0:128], AF.Copy, r=[pzn],
                                        w=[f'Rq{c % 2}_{r_}'])
                                late_tr.append(trB)
                            if g == 0 and c == 1 and qi == 0:
                                dump('pslc', pslc[:], ['pslc'])
                                dump('negm', Zm4[0][0][:], ['Zm0_0'])
                    def run_late():
                        while late_tr:
                            late_tr.pop(0)()
                    cmp_topk(0)
                    run_late()
                    for c in range(8):
                        for hg in range(4):
                            h = 4 * g + hg
                            if hg == 1 and c + 1 < 8:
                                cmp_topk(c + 1)
                            if hg == 3:
                                run_late()
                            for qi in range(4):
                                i = 4 * c + qi
                                qt = 2 * i + 1
                                qsl = slice(i * 128, (i + 1) * 128)
                                kts = [kt for kt in range(qt - 4, qt + 1) if kt >= 0]
                                pw = PTw[pti['w'] % 3]
                                pwn = f"PTw{pti['w'] % 3}"
                                pti['w'] += 1
                                pzA, pzAn = nextZ()
                                pzB, pzBn = nextZ()
                                for j, kt in enumerate(kts):
                                    ksl = slice(kt * 128, (kt + 1) * 128)
                                    last = (kt == qt)
                                    dst = pzB[:, 0:128] if last else pzA[:, j * 128:(j + 1) * 128]
                                    dn = pzBn if last else pzAn
                                    masked = last or (kt == qt - 4)
                                    mm(dst, kwA[0:69, ksl], QA[hg][0:69, qsl], True, not masked, r=['kwA', f'QA{hg}'], w=[dn])
                                    if masked:
                                        mm(dst, identb[:], tri[:] if last else band[:], False, True, r=['identb', 'tri', 'band'], w=[dn])
                                na = len(kts) - 1
                                if na > 0:
                                    act(pw[:, 0:na * 128], pzA[:, 0:na * 128], AF.Exp, r=[pzAn], w=[pwn + 'a'], scale=SC_NSA)
                                act(pw[:, 512:640], pzB[:, 0:128], AF.Exp, r=[pzBn], w=[pwn + 'b'], scale=SC_NSA)
                                def winB(kts=kts, qt=qt, qi=qi, pw=pw, pwn=pwn):
                                    for j, kt in enumerate(kts):
                                        last = (kt == qt)
                                        src = pw[:, 512:640] if last else pw[:, j * 128:(j + 1) * 128]
                                        mm(psO[:, qi * 128:qi * 128 + 65], src, vwA[:, kt, 0:65], (j == 0 and qi == 0), last,
                                           r=[pwn + 'a', pwn + 'b', 'vwA'], w=['psO'], skip=True)
                                defer(winB)
                            for r_ in range((16 * c + 15) // 58 + 1):
                                P.op('pool', E('tensor_copy', out=RqAll[c % 2][r_][0:69, :], in_=QA[hg][0:69, c * 512:(c + 1) * 512]),
                                     r=[f'QA{hg}'], w=[f'Rq{c % 2}_{r_}'])
                            ktmax = 8 * c + 7
                            for kt in range(ktmax + 1):
                                qmin = max(0, (kt - (8 * c + 1) + 1) // 2)
                                cs = slice(qmin * 128, 512)
                                qs = slice(c * 512 + qmin * 128, (c + 1) * 512)
                                ksl = slice(kt * 128, (kt + 1) * 128)
                                diag = (kt % 2 == 1) and (kt >= 8 * c + 1)
                                pz, pzn = nextZ()
                                if pzn == 'psZ3':
                                    pz, pzn = nextZ()
                                rr_ = (2 * kt) // 58
                                mm(pz[:, cs], ksA[:, ksl], RqAll[c % 2][rr_][:, cs], True, not diag, r=['ksA', f'Rq{c % 2}_{rr_}'], w=[pzn])
                                if diag:
                                    qd = (kt - (8 * c + 1)) // 2
                                    mm(pz[:, qd * 128:(qd + 1) * 128], identb[:], tri[:], False, True, r=['identb', 'tri'], w=[pzn])
                                ptc = PT[pti['i'] % 4]
                                ptn = f"PT{pti['i'] % 4}"
                                pti['i'] += 1
                                act(ptc[:, cs], pz[:, cs], AF.Exp, r=[pzn], w=[ptn], scale=SC_NSA)
                                def selB(kt=kt, qmin=qmin, ptc=ptc, ptn=ptn, c=c):
                                    for qi in range(qmin, 4):
                                        mm(psZ[3][:, qi * 128:qi * 128 + 65], ptc[:, qi * 128:(qi + 1) * 128], vsA[:, kt, 0:65],
                                           (kt == 0 and qi == 0), kt == 8 * c + 1 + 2 * qi, r=[ptn, 'vsA'], w=['psZ3'], skip=True)
                                defer(selB)
                            def finB(c=c, hg=hg, h=h):
                                i0 = 4 * c
                                ts('dve', sm[:, 16:20], psO[:, 64:512:128], 1e-30, None, ALU.max, r=['psO'], w=['smw'])
                                P.op('dve', E('reciprocal', out=sm[:, 20:24], in_=sm[:, 16:20]), r=['smw'], w=['smwr'])
                                tt('dve', sm[:, 24:28], sm[:, 20:24], sg[:, i0:i0 + 4, 16 + h], ALU.mult, r=['smwr', 'sg'], w=['smwm'])
                                ts('dve', sm[:, 28:32], psZ[3][:, 64:512:128], 1e-30, None, ALU.max, r=['psZ3'], w=['sms'])
                                P.op('dve', E('reciprocal', out=sm[:, 32:36], in_=sm[:, 28:32]), r=['sms'], w=['smsr'])
                                tt('dve', sm[:, 36:40], sm[:, 32:36], sg[:, i0:i0 + 4, 8 + h], ALU.mult, r=['smsr', 'sg'], w=['smsm'])
                                for qi in range(4):
                                    stt('dve', t1b[:, qi, :], psO[:, qi * 128:qi * 128 + 64], sm[:, 24 + qi:25 + qi], ocmpAll[c % 2][:, qi, hg, :],
                                        ALU.mult, ALU.add, r=['psO', 'smwm', f'ocmp{c % 2}'], w=['t1b'])
                                    stt('dve', ostage[:, i0 + qi, hg * 64:(hg + 1) * 64], psZ[3][:, qi * 128:qi * 128 + 64],
                                        sm[:, 36 + qi:37 + qi], t1b[:, qi, :], ALU.mult, ALU.add, r=['psZ3', 'smsm', 't1b'], w=['ostage'])
                            defer(finB)
                    flush()
                    if g == 0:
                        dump('ostage', ostage[:, 0:4, :], ['ostage'])
                    zring['n'] = 4
                    for f in range(2):
                        for i8 in range(4):
                            for j in range(8):
                                i = i8 * 8 + j
                                P.op('pe', E('transpose', out=psB[:, j * 128:(j + 1) * 128],
                                                                                in_=ostage[:, i, f * 128:(f + 1) * 128],
                                                                                identity=identb[:]),
                                     r=['ostage', 'identb'], w=['psB'])
                            evac(oTn[:, 2 * g + f, i8 * 1024:(i8 + 1) * 1024], psB[:, :], r=['psB'], w=['oTn'])
                P.barrier()

        checkpoint('nsa')
        oTm = OS.enter_context(SB("oTm", [128, 4, NO * 128], BF16))
        with ExitStack() as MS:
            def sbm(name, shape, dt=F32):
                return MS.enter_context(SB(name, shape, dt))
            ckvT = sbm("ckvT", [128, 2, S], BF16)
            cqT = sbm("cqT", [128, 3, NO * 128], BF16)
            KT = sbm("KT", [128, S], BF16)
            with ExitStack() as PS:
                def sbp(name, shape, dt=F32):
                    return PS.enter_context(SB(name, shape, dt))
                wm = sbp("wm", [128, 8, 640], BF16)
                wkrA = sbp("wkrA", [128, 8, 96], BF16)
                wkrB = sbp("wkrB", [128, 8, 96], BF16)
                xr = [sbp(f"mxr{i}", [128, 1024]) for i in range(3)]
                hTr = [sbp(f"mhTr{i}", [128, 8, 128], BF16) for i in range(2)]
                zf = sbp("zf", [128, 5, 128])
                zs = sbp("zs", [128, 5, 128])
                rcb = sbp("rcb", [128, 2, 128])
                ctab = [sbp(f"ctab{i}", [128, 2, 128]) for i in range(2)]
                rt = sbp("rt", [128, 2, 128])
                wiv = kparts(I['w_in'])
                dma(wm[:, :, 0:384], wiv[:, :, C_QD:C_QD + 384], w=['wm'], q='pool')
                dma(wm[:, :, 384:640], wiv[:, :, C_KVD:C_KVD + 256], w=['wm'], q='pool')
                memset('pool', wkrA[:], 0.0, w=['wkrA'])
                memset('pool', wkrB[:], 0.0, w=['wkrB'])
                dma(wkrA[:, :, 64:96], wiv[:, :, C_KR:C_KR + 32], w=['wkrA'], q='pool')
                dma(wkrB[:, :, 80:96], wiv[:, :, C_KR:C_KR + 16], w=['wkrB'], q='pool')
                dma(wkrB[:, :, 64:80], wiv[:, :, C_KR + 16:C_KR + 32], w=['wkrB'], q='pool')
                ts('pool', wkrB[:, :, 64:80], wkrB[:, :, 64:80], -1.0, None, ALU.mult, r=['wkrB'], w=['wkrB'])
                dma(KT[96:97, :], I['KA'][4:5, :], w=['KT'])
                for lt in range(NT):
                    sl = lt % 2
                    hT = hTr[sl]
                    hres = f"mhT{sl}"

                    def m1_norm(t2):
                        s3 = t2 % 3
                        dma(xr[s3][:], I['xl'][t2 * 128:(t2 + 1) * 128, :], w=[f"mxr{s3}a", f"mxr{s3}b", f"mxr{s3}"])
                        hT_norm(xr[s3], f"mxr{s3}", xr[s3], f"mxr{s3}", t2 % 4)

                    def m1_tr_post(t2):
                        hT_tr(xr[t2 % 3], f"mxr{t2 % 3}")
                        hT_post(hTr[t2 % 2], f"mhT{t2 % 2}", gsA, shA, dve_share=1)
                    if lt == 0:
                        m1_norm(0)
                        m1_tr_post(0)
                        m1_norm(1)
                    dma(ctab[sl][64:96, 0, :], I['cosk'][:, lt * 128:(lt + 1) * 128], w=[f'ctab{sl}'])
                    dma(ctab[sl][64:96, 1, :], I['sink'][:, lt * 128:(lt + 1) * 128], w=[f'ctab{sl}'])
                    tsl = slice(lt * 128, (lt + 1) * 128)
                    own = (lt % 2 == 1)
                    i = lt // 2
                    if lt + 1 < NT:
                        m1_tr_post(lt + 1)
                    ntl = [(384, 0), (512, 1)] + ([(0, 2), (128, 3), (256, 4)] if own else [])
                    pz, pzn = nextZ()
                    pz2, pz2n = nextZ()
                    for (c0, slot) in ntl:
                        dst = pz[:, slot * 128:(slot + 1) * 128] if slot < 4 else pz2[:, 0:128]
                        dn = pzn if slot < 4 else pz2n
                        for k in range(8):
                            mm(dst, wm[:, k, c0:c0 + 128], hT[:, k, :], k == 0, k == 7, r=['wm', hres], w=[dn])
                    for k in range(8):
                        mm(pz2[0:96, 128:256], wkrA[:, k, :], hT[:, k, :], k == 0, k == 7, r=['wkrA', hres], w=[pz2n])
                    for k in range(8):
                        mm(pz2[0:96, 256:384], wkrB[:, k, :], hT[:, k, :], k == 0, k == 7, r=['wkrB', hres], w=[pz2n])
                    if lt + 2 < NT:
                        m1_norm(lt + 2)
                    nsl = 5 if own else 2
                    for (c0, slot) in ntl:
                        srcp = pz[:, slot * 128:(slot + 1) * 128] if slot < 4 else pz2[:, 0:128]
                        sn = pzn if slot < 4 else pz2n
                        evac(zf[:, slot, :], srcp, r=[sn], w=[f'zf{slot}'])
                        act(zs[:, slot, :], srcp, AF.Square, r=[sn], w=[f'zs{slot}'])
                    pz3, pz3n = nextZ()
                    for j, slot in enumerate((0, 1)):
                        mm(pz3[:, 0:128], onesf[:], zs[:, slot, :], j == 0, j == 1, r=['onesf', f'zs{slot}'], w=[pz3n])
                    if own:
                        for j, slot in enumerate((2, 3, 4)):
                            mm(pz3[:, 128:256], onesf[:], zs[:, slot, :], j == 0, j == 2, r=['onesf', f'zs{slot}'], w=[pz3n])
                    act(rcb[:, 0, :], pz3[:, 0:128], AF.Sqrt, r=[pz3n, 'epsb'], w=['rcb0'], scale=1.0 / 256, bias=epsb[:, 0:1])
                    P.op('dve', E('reciprocal', out=rcb[:, 0, :], in_=rcb[:, 0, :]), r=['rcb0'], w=['rcb0'])
                    for slot in (0, 1):
                        tt('pool' if slot else 'dve', ckvT[:, slot, tsl], zf[:, slot, :], rcb[:, 0, :], ALU.mult,
                           r=[f'zf{slot}', 'rcb0'], w=['ckvT'])
                    if own:
                        act(rcb[:, 1, :], pz3[:, 128:256], AF.Sqrt, r=[pz3n, 'epsb'], w=['rcb1'], scale=1.0 / 384, bias=epsb[:, 0:1])
                        P.op('dve', E('reciprocal', out=rcb[:, 1, :], in_=rcb[:, 1, :]), r=['rcb1'], w=['rcb1'])
                        for slot in (2, 3, 4):
                            tt('pool' if slot % 2 else 'dve', cqT[:, slot - 2, i * 128:(i + 1) * 128], zf[:, slot, :], rcb[:, 1, :],
                               ALU.mult, r=[f'zf{slot}', 'rcb1'], w=['cqT'])
                    tt('dve', rt[64:96, 0, :], pz2[64:96, 128:256], ctab[sl][64:96, 0, :], ALU.mult, r=[pz2n, f'ctab{sl}'], w=['rt0'])
                    tt('dve', rt[64:96, 1, :], pz2[64:96, 256:384], ctab[sl][64:96, 1, :], ALU.mult, r=[pz2n, f'ctab{sl}'], w=['rt1'])
                    tt('pool', KT[64:96, tsl], rt[64:96, 0, :], rt[64:96, 1, :], ALU.add, r=['rt0', 'rt1'], w=['KT'])
                dump('ckvT', ckvT[:, 0, 0:1024], ['ckvT'])
                dump('cqT', cqT[:, 0, 0:512], ['cqT'])
                dump('krot', KT[64:96, 0:1024], ['KT'])
            P.barrier()
            checkpoint('m1')
            with ExitStack() as AS:
                def sba(name, shape, dt=F32):
                    return AS.enter_context(SB(name, shape, dt))
                wuq = sba("wuq", [128, 3, 768], BF16)
                wuqB = sba("wuqB", [128, 3, 768], BF16)
                wuk = sba("wuk", [128, 2, 512], BF16)
                wuv = sba("wuv", [128, 2, 512], BF16)
                VH = sba("VH", [128, NT, 66], BF16)
                QT = sba("QT", [128, NO * 128], BF16)
                PT = [sba(f"MPT{i}", [128, 512], BF16) for i in range(4)]
                ostage = sba("mostage", [128, NO, 128], BF16)
                qtab = [sba(f"qtab{i}", [128, 2, 512]) for i in range(2)]
                rt = sba("mrt", [128, 2, 512])
                sm = sba("msm", [128, 8])
                dma(wuq[:], kparts(I['w_uq']), w=['wuq'], q='pool')
                dma(wuk[:], kparts(I['w_uk']), w=['wuk'], q='pool')
                dma(wuv[:], kparts(I['w_uv']), w=['wuv'], q='pool')
                for k in range(3):
                    ts('pool', wuq[:, k, :], wuq[:, k, :], gl[:, 16 + k:17 + k], None, ALU.mult, r=['wuq', 'gl'], w=['wuq'])
                for k in range(2):
                    ts('pool', wuk[:, k, :], wuk[:, k, :], gl[:, 19 + k:20 + k], None, ALU.mult, r=['wuk', 'gl'], w=['wuk'])
                    ts('pool', wuv[:, k, :], wuv[:, k, :], gl[:, 19 + k:20 + k], None, ALU.mult, r=['wuv', 'gl'], w=['wuv'])
                memset('pool', wuqB[:], 0.0, w=['wuqB'])
                wq4 = wuq[:].rearrange("p k (h c) -> p k h c", c=96)
                wb4 = wuqB[:].rearrange("p k (h c) -> p k h c", c=96)
                for k in range(3):
                    ts('pool', wb4[:, k, :, 64:80], wq4[:, k, :, 80:96], -1.0, None, ALU.mult, r=['wuq'], w=['wuqB'])
                    P.op('pool', E('tensor_copy', out=wb4[:, k, :, 80:96], in_=wq4[:, k, :, 64:80]), r=['wuq'], w=['wuqB'])
                memset('pool', VH[:, :, 64:66], 1.0, w=['VH'])
                memset('pool', QT[96:97, :], NEG, w=['QT'])
                zring['n'] = 3
                pti = {'i': 0}
                hz = {'i': 0}

                def nextH():
                    hz['i'] = (hz['i'] + 1) % 2
                    return (psT[:, 0:512], 'psTa') if hz['i'] == 0 else (psT[:, 512:1024], 'psTb')

                for h in range(8):
                    for ch in range(16):
                        pz, pzn = nextH()
                        for k in range(2):
                            mm(pz[0:64, :], wuk[:, k, h * 64:(h + 1) * 64], ckvT[:, k, ch * 512:(ch + 1) * 512], k == 0, k == 1,
                               r=['wuk', 'ckvT'], w=[pzn])
                        evac(KT[0:64, ch * 512:(ch + 1) * 512], pz[0:64, :], r=[pzn], w=['KT'])
                    for t8 in range(8):
                        pz, pzn = nextH()
                        for j in range(8):
                            lt = t8 * 8 + j
                            for k in range(2):
                                mm(pz[:, j * 64:(j + 1) * 64], ckvT[:, k, lt * 128:(lt + 1) * 128], wuv[:, k, h * 64:(h + 1) * 64],
                                   k == 0, k == 1, r=['wuv', 'ckvT'], w=[pzn])
                        evac(VH[:, t8 * 8:(t8 + 1) * 8, 0:64], pz.rearrange("p (a b) -> p a b", b=64), r=[pzn], w=['VH'])
                    for c in range(8):
                        csl = slice(c * 512, (c + 1) * 512)
                        qs = c % 2
                        dma(qtab[qs][64:96, 0, :], I['cosq'][:, csl], w=[f'qtab{qs}'])
                        dma(qtab[qs][64:96, 1, :], I['sinq'][:, csl], w=[f'qtab{qs}'])
                        pzA, pzAn = nextH()
                        for k in range(3):
                            mm(pzA[0:96, :], wuq[:, k, h * 96:(h + 1) * 96], cqT[:, k, csl], k == 0, k == 2, r=['wuq', 'cqT'], w=[pzAn])
                        pzB, pzBn = nextH()
                        for k in range(3):
                            mm(pzB[0:96, :], wuqB[:, k, h * 96:(h + 1) * 96], cqT[:, k, csl], k == 0, k == 2, r=['wuqB', 'cqT'], w=[pzBn])
                        act(QT[0:64, csl], pzA[0:64, :], AF.Copy, r=[pzAn], w=['QT'])
                        tt('dve', rt[64:96, 0, :], pzA[64:96, :], qtab[qs][64:96, 0, :], ALU.mult, r=[pzAn, f'qtab{qs}'], w=['mrt0'])
                        tt('dve', rt[64:96, 1, :], pzB[64:96, :], qtab[qs][64:96, 1, :], ALU.mult, r=[pzBn, f'qtab{qs}'], w=['mrt1'])
                        tt('pool', QT[64:96, csl], rt[64:96, 0, :], rt[64:96, 1, :], ALU.add, r=['mrt0', 'mrt1'], w=['QT'])
                    if h == 0:
                        dump('QT', QT[0:96, 0:512], ['QT'])
                        dump('KT', KT[0:96, 0:1024], ['KT'])
                    for c in range(8):
                        ob, obn = (psO, 'psO') if c % 2 == 0 else (psZ[3], 'psZ3')
                        ktmax = 8 * c + 7
                        for kt in range(ktmax + 1):
                            qmin = max(0, (kt - (8 * c + 1) + 1) // 2)
                            cs = slice(qmin * 128, 512)
                            qs_ = slice(c * 512 + qmin * 128, (c + 1) * 512)
                            ksl = slice(kt * 128, (kt + 1) * 128)
                            diag = (kt % 2 == 1) and (kt >= 8 * c + 1)
                            pz, pzn = nextZ()
                            if pzn == 'psZ3':
                                pz, pzn = nextZ()
                            mm(pz[:, cs], KT[0:97, ksl], QT[0:97, qs_], True, not diag, r=['KT', 'QT'], w=[pzn])
                            if diag:
                                qd = (kt - (8 * c + 1)) // 2
                                mm(pz[:, qd * 128:(qd + 1) * 128], identb[:], tri[:], False, True, r=['identb', 'tri'], w=[pzn])
                            ptc = PT[pti['i'] % 4]
                            ptn = f"MPT{pti['i'] % 4}"
                            pti['i'] += 1
                            act(ptc[:, cs], pz[:, cs], AF.Exp, r=[pzn], w=[ptn], scale=SC_MLA)
                            def mlaB(kt=kt, qmin=qmin, ptc=ptc, ptn=ptn, c=c, ob=ob, obn=obn):
                                for qi in range(qmin, 4):
                                    mm(ob[:, qi * 128:qi * 128 + 65], ptc[:, qi * 128:(qi + 1) * 128], VH[:, kt, 0:65],
                                       (kt == 0 and qi == 0), kt == 8 * c + 1 + 2 * qi, r=[ptn, 'VH'], w=[obn], skip=True)
                            defer(mlaB)
                        def mlaF(c=c, h=h, ob=ob, obn=obn):
                            ts('dve', sm[:, 0:4], ob[:, 64:512:128], 1e-30, None, ALU.max, r=[obn], w=['msm0'])
                            P.op('dve', E('reciprocal', out=sm[:, 4:8], in_=sm[:, 0:4]), r=['msm0'], w=['msm1'])
                            for qi in range(4):
                                ts('dve', ostage[:, 4 * c + qi, (h % 2) * 64:(h % 2) * 64 + 64], ob[:, qi * 128:qi * 128 + 64],
                                   sm[:, 4 + qi:5 + qi], None, ALU.mult, r=[obn, 'msm1'], w=['mostage'])
                        defer(mlaF)
                    flush()
                    if h % 2 == 1:
                        if h == 1:
                            dump('mostage', ostage[:, 0:4, :], ['mostage'])
                        for i8 in range(4):
                            for j in range(8):
                                i = i8 * 8 + j
                                P.op('pe', E('transpose', out=psB[:, j * 128:(j + 1) * 128], in_=ostage[:, i, :],
                                                                           identity=identb[:]),
                                     r=['mostage', 'identb'], w=['psB'])
                            evac(oTm[:, h // 2, i8 * 1024:(i8 + 1) * 1024], psB[:, :], r=['psB'], w=['oTm'])
            P.barrier()

        zring['n'] = 4
        checkpoint('mla')
        def bcast_vec(col0):
            for j in range(8):
                pz, pzn = nextZ()
                src = col0(j)
                ts('dve', dg[:], ident[:], src, None, ALU.mult, r=['ident', 'modT', 'gl'], w=['dg'])
                mm(pz[:, 0:128], onesf[:], dg[:], True, True, r=['onesf', 'dg'], w=[pzn])
                evac(gbc[:, j * 128:(j + 1) * 128], pz[:, 0:128], r=[pzn], w=['gbc'])

        with ExitStack() as TS:
            def sbt(name, shape, dt=F32):
                return TS.enter_context(SB(name, shape, dt))
            wg = sbt("wg", [128, 8, 2048], BF16)
            won = sbt("won", [128, 4, 1024], BF16)
            wom = sbt("wom", [128, 4, 1024], BF16)
            wout = sbt("wout", [128, 8, 1024], BF16)
            xr = [sbt(f"txr{i}", [128, 1024]) for i in range(2)]
            xn = sbt("txn", [128, 1024])
            hTr = [sbt(f"thT{i}", [128, 8, 128], BF16) for i in range(2)]
            sgA = sbt("sgA", [128, 1024])
            sgB = sbt("sgB", [128, 1024])
            m1 = sbt("m1", [128, 1024])
            m2 = sbt("m2", [128, 1024])
            yT = sbt("yT", [128, 8, 128], BF16)
            yb = sbt("yb", [128, 1024], BF16)
            x1r = [sbt(f"x1r{i}", [128, 1024]) for i in range(2)]
            wiv = kparts(I['w_in'])
            dma(wg[:, :, 0:1024], wiv[:, :, C_GA:C_GA + 1024], w=['wg'], q='pool')
            dma(wg[:, :, 1024:2048], wiv[:, :, C_GB:C_GB + 1024], w=['wg'], q='pool')
            dma(won[:], kparts(I['w_o_nsa']), w=['won'], q='pool')
            dma(wom[:], kparts(I['w_o_mla']), w=['wom'], q='pool')
            dma(wout[:], kparts(I['w_out']), w=['wout'], q='pool')
            bcast_vec(lambda j: modT[:, 16 + j:17 + j])
            m3 = sbt("m3", [128, 1024])
            yT2 = sbt("yT2", [128, 8, 128], BF16)
            xr3 = sbt("txr2", [128, 1024])
            xr = xr + [xr3]
            yTs = [yT, yT2]

            xr4 = sbt("txr3", [128, 1024])
            xr = xr + [xr4]
            xn2 = sbt("txn2", [128, 1024])
            xnr = [xn, xn2]

            def t1_norm(i2):
                s4 = i2 % 4
                dma(xr[s4][:], I['xo'][i2 * 128:(i2 + 1) * 128, :], w=[f"txr{s4}"])
                hT_norm(xr[s4], f"txr{s4}", xnr[i2 % 2], f"txn{i2 % 2}", i2 % 4)

            def t1_tr_post(i2):
                hT_tr(xnr[i2 % 2], f"txn{i2 % 2}")
                hT_post(hTr[i2 % 2], f"thT{i2 % 2}", gsA, shA)

            def t1_A(i):
                isl = slice(i * 128, (i + 1) * 128)
                hT = hTr[i % 2]
                hres = f"thT{i % 2}"
                for n in range(2):
                    pz, pzn = nextZ()
                    for k in range(8):
                        mm(pz[:, :], hT[:, k, :], wg[:, k, n * 512:(n + 1) * 512], k == 0, k == 7, r=['wg', hres], w=[pzn])
                    act(sgA[:, n * 512:(n + 1) * 512], pz[:, :], AF.Sigmoid, r=[pzn], w=['sgA'])
                for n in range(2):
                    pz, pzn = nextZ()
                    for k in range(4):
                        mm(pz[:, :], oTn[:, k, isl], won[:, k, n * 512:(n + 1) * 512], k == 0, k == 3, r=['won', 'oTn'], w=[pzn])
                    tt('dve', m1[:, n * 512:(n + 1) * 512], pz[:, :], sgA[:, n * 512:(n + 1) * 512], ALU.mult, r=[pzn, 'sgA'], w=['m1'])
                for n in range(2):
                    pz, pzn = nextZ()
                    for k in range(8):
                        mm(pz[:, :], hT[:, k, :], wg[:, k, 1024 + n * 512:1024 + (n + 1) * 512], k == 0, k == 7, r=['wg', hres], w=[pzn])
                    act(sgB[:, n * 512:(n + 1) * 512], pz[:, :], AF.Sigmoid, r=[pzn], w=['sgB'])
                if i + 1 < NO:
                    t1_tr_post(i + 1)
                if i + 2 < NO:
                    t1_norm(i + 2)
                for n in range(2):
                    pz, pzn = nextZ()
                    for k in range(4):
                        mm(pz[:, :], oTm[:, k, isl], wom[:, k, n * 512:(n + 1) * 512], k == 0, k == 3, r=['wom', 'oTm'], w=[pzn])
                    tt('dve', m2[:, n * 512:(n + 1) * 512], pz[:, :], sgB[:, n * 512:(n + 1) * 512], ALU.mult, r=[pzn, 'sgB'], w=['m2'])
                tt('dve', yb[:], m1[:], m2[:], ALU.add, r=['m1', 'm2'], w=['yb'])
                for k in range(8):
                    P.op('pe', E('transpose', out=psB[:, k * 128:(k + 1) * 128], in_=yb[:, k * 128:(k + 1) * 128],
                                 identity=identb[:]), r=['yb', 'identb'], w=['psB'])
                yTc = yTs[i % 2]
                yv = yTc[:].rearrange("p k t -> p (k t)")
                act(yv[:, 0:512], psB[:, 0:512], AF.Copy, r=['psB'], w=[f'yT{i % 2}'])
                ts('dve', yv[:, 512:1024], psB[:, 512:1024], 1.0, None, ALU.mult, r=['psB'], w=[f'yT{i % 2}'])

            def t1_B(i):
                isl = slice(i * 128, (i + 1) * 128)
                yTc = yTs[i % 2]
                xt = xr[i % 4]
                x1 = x1r[i % 2]
                x1n = f"x1r{i % 2}"
                for n in range(2):
                    pz, pzn = nextZ()
                    for k in range(8):
                        mm(pz[:, :], yTc[:, k, :], wout[:, k, n * 512:(n + 1) * 512], k == 0, k == 7, r=['wout', f'yT{i % 2}'], w=[pzn])
                    tt('dve', m3[:, n * 512:(n + 1) * 512], pz[:, :], gbc[:, n * 512:(n + 1) * 512], ALU.mult, r=[pzn, 'gbc'], w=['m3'])
                tt('dve', x1[:], m3[:], xt[:], ALU.add, r=['m3', f"txr{i % 4}"], w=[x1n])
                dma(x1s[isl, :], x1[:], r=[x1n], w=['x1s'])
                if i == 0:
                    dump('x1', x1[:], [x1n])

            t1_norm(0)
            t1_tr_post(0)
            t1_norm(1)
            for i in range(NO):
                t1_A(i)
                if i >= 1:
                    t1_B(i - 1)
            t1_B(NO - 1)
        P.barrier()
        OS.close()
        checkpoint('t1')

        with ExitStack() as TS:
            def sbt(name, shape, dt=F32):
                return TS.enter_context(SB(name, shape, dt))
            wf1 = sbt("wf1", [128, 8, 4096], BF16)
            wf2 = sbt("wf2", [128, 32, 1024], BF16)
            gfb = sbt("gfb", [128, 1024])
            xr = [sbt(f"uxr{i}", [128, 1024]) for i in range(2)]
            xn = sbt("uxn", [128, 1024])
            hTr = [sbt(f"uhT{i}", [128, 8, 128], BF16) for i in range(2)]
            aT = [sbt(f"aT{i}", [128, 32, 128], BF16) for i in range(2)]
            rl = [sbt(f"rl{i}", [128, 512]) for i in range(2)]
            o2 = sbt("o2", [128, 1024])
            tmp = sbt("utmp", [128, 1024])
            res = [sbt(f"ures{i}", [128, 1024]) for i in range(2)]
            fs = sbt("fs", [128, 4])
            wf1v = kparts(I['w_fc1'])
            wf2v = kparts(I['w_fc2'])
            for q4 in range(4):
                dma(wf1[:, :, q4 * 1024:(q4 + 1) * 1024], wf1v[:, :, q4 * 1024:(q4 + 1) * 1024], w=[f'wf1_{q4}'], q='pool')
            for q4 in range(4):
                dma(wf2[:, q4 * 8:(q4 + 1) * 8, :], wf2v[:, q4 * 8:(q4 + 1) * 8, :], w=[f'wf2_{q4}'], q='pool')
            bcast_vec(lambda j: gl[:, 21 + j:22 + j])
            P.op('pool', E('tensor_copy', out=gfb[:], in_=gbc[:]), r=['gbc'], w=['gfb'])
            bcast_vec(lambda j: modT[:, 40 + j:41 + j])
            xr = xr + [sbt("uxr2", [128, 1024])]
            xnr = [xn, sbt("uxn2", [128, 1024])]

            def t2_norm(i2):
                s3 = i2 % 3
                dma(xr[s3][:], x1s[i2 * 128:(i2 + 1) * 128, :], r=['x1s'], w=[f"uxr{s3}"])
                hT_norm(xr[s3], f"uxr{s3}", xnr[i2 % 2], f"uxn{i2 % 2}", i2 % 4)

            t2_norm(0)
            hT_tr(xnr[0], "uxn0")
            hT_post(hTr[0], "uhT0", gsM, shM)
            t2_norm(1)
            for i in range(NO):
                sl = i % 2
                xt = xr[i % 3]
                xres = f"uxr{i % 3}"
                isl = slice(i * 128, (i + 1) * 128)
                hT = hTr[sl]
                hres = f"uhT{sl}"
                a = aT[sl]
                an = f"aT{sl}"
                for jg in range(8):
                    pz, pzn = (psZ[jg % 2], f"psZ{jg % 2}")
                    for jj in range(4):
                        j = jg * 4 + jj
                        for k in range(8):
                            mm(pz[:, jj * 128:(jj + 1) * 128], wf1[:, k, j * 128:(j + 1) * 128], hT[:, k, :], k == 0, k == 7,
                               r=[f'wf1_{jg // 2}', hres], w=[pzn])
                    rb = rl[jg % 2]
                    rbn = f"rl{jg % 2}"
                    act(rb[:], pz[:, :], AF.Relu, r=[pzn], w=[rbn])
                    av = a[:, jg * 4:(jg + 1) * 4, :].rearrange("p a b -> p (a b)")
                    tt('pool', av, rb[:], rb[:], ALU.mult, r=[rbn], w=[an])
                if i + 1 < NO:
                    hT_tr(xnr[(i + 1) % 2], f"uxn{(i + 1) % 2}")
                for n in range(2):
                    pz, pzn = (psZ[2 + n], f"psZ{2 + n}")
                    for j in range(32):
                        mm(pz[:, :], a[:, j, :], wf2[:, j, n * 512:(n + 1) * 512], j == 0, j == 31, r=[f'wf2_{j // 8}', an], w=[pzn])
                if i + 1 < NO:
                    hT_post(hTr[(i + 1) % 2], f"uhT{(i + 1) % 2}", gsM, shM)
                if i + 2 < NO:
                    t2_norm(i + 2)
                for n in range(2):
                    pz, pzn = (psZ[2 + n], f"psZ{2 + n}")
                    tt('dve', tmp[:, n * 512:(n + 1) * 512], pz[:, :], gbc[:, n * 512:(n + 1) * 512], ALU.mult, r=[pzn, 'gbc'], w=['utmp'])
                tt('dve', o2[:], tmp[:], xt[:], ALU.add, r=['utmp', xres], w=['o2'])
                act(junk[:], o2[:], AF.Square, r=['o2'], w=['junk', 'fs0'], accum_out=fs[:, 0:1])
                act(fs[:, 1:2], fs[:, 0:1], AF.Sqrt, r=['fs0', 'epsb'], w=['fs1'], scale=1.0 / D, bias=epsb[:, 0:1])
                P.op('dve', E('reciprocal', out=fs[:, 2:3], in_=fs[:, 1:2]), r=['fs1'], w=['fs2'])
                rs_ = res[sl]
                rn = f"ures{sl}"
                stt('dve', rs_[:], o2[:], fs[:, 2:3], gfb[:], ALU.mult, ALU.mult, r=['o2', 'fs2', 'gfb'], w=[rn])
                dma(out[isl, :], rs_[:], r=[rn], w=['out'])
    return nc


_CACHE = {}


def _lay(v, k):
    return np.ascontiguousarray(np.asarray(v, np.float32).reshape(k, 128).T)


def make_in_maps(inputs, ncores=8):
    x = np.asarray(inputs['x'], np.float32)
    shared = {
        'w_ada': np.ascontiguousarray(inputs['w_ada'][0], dtype=np.float32),
        'bada_l': _lay(inputs['b_ada'][0], 48),
        'gmix_l': _lay(inputs['g_mix'][0], 8), 'gmlp_l': _lay(inputs['g_mlp'][0], 8),
        'gcq_l': _lay(inputs['g_cq'][0], 3), 'gckv_l': _lay(inputs['g_ckv'][0], 2),
        'gfin_l': _lay(inputs['g_final'], 8),
        'w_in': np.ascontiguousarray(inputs['w_in'][0], dtype=np.float32),
        'peck_t': np.ascontiguousarray(np.asarray(inputs['pe_ck'][0], np.float32).T),
        'w_ck1': np.ascontiguousarray(inputs['w_ck1'][0], dtype=np.float32),
        'w_ck2': np.ascontiguousarray(inputs['w_ck2'][0], dtype=np.float32),
        'pecv_t': np.ascontiguousarray(np.asarray(inputs['pe_cv'][0], np.float32).T),
        'w_cv1': np.ascontiguousarray(inputs['w_cv1'][0], dtype=np.float32),
        'w_cv2': np.ascontiguousarray(inputs['w_cv2'][0], dtype=np.float32),
        'w_uq': np.ascontiguousarray(inputs['w_uq'][0], dtype=np.float32),
        'w_uk': np.ascontiguousarray(inputs['w_uk'][0], dtype=np.float32),
        'w_uv': np.ascontiguousarray(inputs['w_uv'][0], dtype=np.float32),
        'w_o_nsa': np.ascontiguousarray(inputs['w_o_nsa'][0], dtype=np.float32),
        'w_o_mla': np.ascontiguousarray(inputs['w_o_mla'][0], dtype=np.float32),
        'w_out': np.ascontiguousarray(inputs['w_out'][0], dtype=np.float32),
        'w_fc1': np.ascontiguousarray(inputs['w_fc1'][0], dtype=np.float32),
        'w_fc2': np.ascontiguousarray(inputs['w_fc2'][0], dtype=np.float32),
    }
    consts = [host_consts(0), host_consts(1)]
    in_maps = []
    for core in range(ncores):
        b, p = core // 2, core % 2
        if p == 1:
            xl = np.ascontiguousarray(x[b])
        else:
            xl = np.concatenate([np.zeros((128, D), np.float32), x[b, :S - 128]], axis=0)
        xo = np.ascontiguousarray(xl.reshape(NT, 128, D)[1::2].reshape(NO * 128, D))
        m = dict(shared)
        m['xl'] = xl
        m['xo'] = xo
        m['c_l'] = _lay(np.asarray(inputs['c'], np.float32)[b], 8)
        m.update(consts[p])
        in_maps.append(m)
    return in_maps


def assemble(results, ncores=8):
    outp = np.zeros((4, S, D), np.float32)
    for core in range(ncores):
        b, p = core // 2, core % 2
        o = np.asarray(results[core]['out'], np.float32).reshape(NO, 128, D)
        outp[b].reshape(NT, 128, D)[p::2] = o
    return outp


def kernel(**inputs):
    if 'nc' not in _CACHE:
        _CACHE['nc'] = build_program()
    nc = _CACHE['nc']
    in_maps = make_in_maps(inputs)
    res = run_bass_kernel_spmd(nc, in_maps, core_ids=list(range(8)))
    return assemble(res.results)
```

```python
import numpy as np
import ml_dtypes
from contextlib import ExitStack
import concourse.bass as bass
import concourse.mybir as mybir
from concourse.bass_utils import run_bass_kernel_spmd

F32 = mybir.dt.float32
BF16 = mybir.dt.bfloat16
AF = mybir.ActivationFunctionType
ALU = mybir.AluOpType
NPBF = ml_dtypes.bfloat16

D = 1024
S = 8192
NT = 64
NO = 32
NEG = -30000.0
EPS = 1e-6
GEN = 8192
EVAC_ACT_ONLY = True
NDSEM = 12
SC_NSA = 0.125
SC_MLA = 96.0 ** -0.5
C_Q, C_KC, C_VC, C_KS, C_VS, C_KW, C_VW, C_G, C_QD, C_KVD, C_KR, C_GA, C_GB = (
    0, 512, 640, 768, 896, 1024, 1152, 1280, 1304, 1688, 1944, 1976, 3000)


class Prog:
    ENGS = ('pe', 'act', 'dve', 'pool', 'sp')

    def __init__(self, nc):
        self.nc = nc
        self.ops = {e: [] for e in self.ENGS}
        self.cnt = {e: 0 for e in self.ENGS}
        self.lastw = {}
        self.readers = {}
        self.dcnt = {}
        self.dnext = {e: 0 for e in self.ENGS}
        self.floor = {}

    def _deps(self, r, w):
        deps = dict(self.floor)

        def add(tok):
            if tok is None:
                return
            k, v = tok
            if deps.get(k, 0) < v:
                deps[k] = v
        for x in r:
            add(self.lastw.get(x))
        for x in w:
            add(self.lastw.get(x))
            for t in self.readers.get(x, ()):
                add(t)
        return deps

    def _commit(self, tok, r, w):
        for x in r:
            self.readers.setdefault(x, []).append(tok)
        for x in w:
            self.lastw[x] = tok
            self.readers[x] = []

    def barrier(self):
        fl = {}
        for e in self.ENGS:
            c = self.cnt[e]
            if c > 0:
                fl[(e, (c - 1) // GEN)] = (c - 1) % GEN + 1
        for key, n in self.dcnt.items():
            fl[key] = 16 * n
        self.floor = fl
        self.lastw = {}
        self.readers = {}

    def op(self, eng, fn, r=(), w=()):
        deps = self._deps(r, w)
        idx = self.cnt[eng]
        self.cnt[eng] += 1
        tok = ((eng, idx // GEN), idx % GEN + 1)
        if eng == 'pe':
            deps = {k: v for k, v in deps.items() if k[0] != 'pe'}
        self.ops[eng].append(('c', fn, deps, tok))
        self._commit(tok, r, w)
        return tok

    def dma(self, q, fn, r=(), w=()):
        deps = self._deps(r, w)
        slot = self.dnext[q] % NDSEM
        self.dnext[q] += 1
        key = ('dma_' + q, slot)
        n = self.dcnt.get(key, 0)
        if n > 0 and deps.get(key, 0) < 16 * n:
            deps[key] = 16 * n
        self.dcnt[key] = n + 1
        tok = (key, 16 * (n + 1))
        self.ops[q].append(('d', fn, deps, tok))
        self._commit(tok, r, w)
        return tok

    def emit(self):
        nc = self.nc
        with ExitStack() as es:
            sems = {}
            for e in self.ENGS:
                for g in range((self.cnt[e] + GEN - 1) // GEN):
                    sems[(e, g)] = es.enter_context(nc.semaphore(f"s_{e}_{g}"))
            for key in self.dcnt:
                sems[key] = es.enter_context(nc.semaphore(f"s_{key[0]}_{key[1]}"))
            block = es.enter_context(nc.Block())
            engobj = {'pe': 'tensor', 'act': 'scalar', 'dve': 'vector', 'pool': 'gpsimd', 'sp': 'sync'}

            def make(ename):
                def body(e):
                    waited = {}
                    for kind, fn, deps, tok in self.ops[ename]:
                        for k, v in deps.items():
                            if waited.get(k, 0) < v:
                                e.wait_ge(sems[k], v)
                                waited[k] = v
                        ins = fn(e)
                        ins.then_inc(sems[tok[0]], 1 if kind == 'c' else 16)
                    if ename == 'sp':
                        fin = {}
                        for e2 in self.ENGS:
                            c = self.cnt[e2]
                            if c > 0:
                                fin[(e2, (c - 1) // GEN)] = (c - 1) % GEN + 1
                        for key, n in self.dcnt.items():
                            fin[key] = 16 * n
                        for k, v in fin.items():
                            if waited.get(k, 0) < v:
                                e.wait_ge(sems[k], v)
                return body
            for ename in self.ENGS:
                getattr(block, engobj[ename])(make(ename))


def host_consts(p):
    shift = 128 * (1 - p)
    c = {}
    c['ident'] = np.eye(128, dtype=np.float32)
    c['identb'] = np.eye(128, dtype=np.float32).astype(NPBF)
    k = np.arange(128)[:, None]
    q = np.arange(128)[None, :]
    c['tri'] = np.where(k > q, NEG, 0.0).astype(NPBF)
    c['band'] = np.where(k <= q, NEG, 0.0).astype(NPBF)
    half = 16
    inv_freq = (10000.0 ** (-np.arange(half, dtype=np.float32) / half)).astype(np.float32)
    L = np.arange(S)
    gpos = (L - shift).astype(np.float32)
    ang = (gpos[:, None] * inv_freq[None, :]).astype(np.float32)
    cos2 = np.concatenate([np.cos(ang), np.cos(ang)], axis=1).T.astype(np.float32)
    sin2 = np.concatenate([np.sin(ang), np.sin(ang)], axis=1).T.astype(np.float32)
    c['cosk'] = np.ascontiguousarray(cos2)
    c['sink'] = np.ascontiguousarray(sin2)
    own = (np.arange(NO)[:, None] * 256 + 128 + np.arange(128)[None, :]).reshape(-1)
    c['cosq'] = np.ascontiguousarray(cos2[:, own])
    c['sinq'] = np.ascontiguousarray(sin2[:, own])
    ka = np.zeros((5, S), np.float32)
    ka[0] = 1.0
    ka[1] = 1.0
    ka[2] = 128.0 * (L // 128)
    ka[3] = L % 128
    ka[4] = (L < shift).astype(np.float32)
    c['KA'] = ka.astype(NPBF)
    li = np.arange(512)
    cend = 16 * li + 31
    kca = np.zeros((5, 512), np.float32)
    kca[0] = 1.0
    kca[1] = 1.0
    kca[2] = 128.0 * (cend // 128)
    kca[3] = cend % 128
    kca[4] = (16 * li < shift).astype(np.float32)
    c['KCA'] = kca.astype(NPBF)
    slopes = 2.0 ** (-8.0 * np.arange(1, 9) / 8.0)
    qa = np.zeros((8, 5, NO * 128), np.float32)
    for h in range(8):
        cc = slopes[h] / SC_NSA
        qa[h, 0] = -cc * 128.0 * (own // 128)
        qa[h, 1] = -cc * (own % 128)
        qa[h, 2] = cc
        qa[h, 3] = cc
        qa[h, 4] = NEG
    c['QAh'] = qa.astype(NPBF)
    e32 = np.zeros((32, 16, 128), np.float32)
    for v in range(16):
        e32[2 * v, v, 0:64] = 1.0
        e32[2 * v + 1, v, 64:128] = 1.0

    ef = np.zeros((59, S), np.float32)
    lbt = (L // 64)
    for j in range(58):
        ef[j] = (lbt % 58 == j)
    c['EF'] = ef.astype(NPBF)
    cm = np.zeros((128, 8, 128), np.float32)
    for mi in range(8):
        m = 2 * mi + 1
        delta = 128 * m
        valid = (16 * k + 31) <= (delta + q)
        cm[:, mi, :] = np.where(valid, 0.0, NEG)
    c['cmask'] = cm.astype(NPBF)
    lia = np.arange(512)[:, None]
    lb = np.arange(128)[None, :]
    ov = ((16 * lia < 64 * lb + 64) & (16 * lia + 31 >= 64 * lb)).astype(np.float32)
    c['OVL'] = np.ascontiguousarray(ov.reshape(4, 128, 128).transpose(1, 0, 2)).astype(NPBF)
    bon = np.zeros((NO, 128, 128), np.float32)
    blk0 = 2 * (1 - p)
    for i in range(NO):
        t = 128 * (2 * i + 1) + np.arange(128)[:, None]
        cur = t // 64
        lbb = np.arange(128)[None, :]
        valid = (lbb <= cur) & (lbb >= blk0)
        forced = (lbb == blk0) | (lbb >= cur - 1)
        bon[i] = np.where(valid, np.where(forced, 1000.0, 0.0), np.where(lbb > cur, -1e9, -2e9))
    c['bonus'] = bon
    return c


CONST_SHAPES = {
    'ident': ([128, 128], F32), 'identb': ([128, 128], BF16), 'tri': ([128, 128], BF16), 'band': ([128, 128], BF16),
    'cosk': ([32, S], F32), 'sink': ([32, S], F32), 'cosq': ([32, NO * 128], F32), 'sinq': ([32, NO * 128], F32),
    'KA': ([5, S], BF16), 'KCA': ([5, 512], BF16), 'QAh': ([8, 5, NO * 128], BF16), 'EF': ([59, S], BF16),
    'cmask': ([128, 8, 128], BF16), 'OVL': ([128, 4, 128], BF16), 'bonus': ([NO, 128, 128], F32),
}
IN_SHAPES = {
    'xl': [S, D], 'xo': [NO * 128, D], 'c_l': [128, 8], 'w_ada': [D, 6 * D], 'bada_l': [128, 48],
    'gmix_l': [128, 8], 'gmlp_l': [128, 8], 'gcq_l': [128, 3], 'gckv_l': [128, 2], 'gfin_l': [128, 8],
    'w_in': [D, 4024], 'peck_t': [64, 32], 'w_ck1': [2048, 256], 'w_ck2': [256, 64],
    'pecv_t': [64, 32], 'w_cv1': [2048, 256], 'w_cv2': [256, 64],
    'w_uq': [384, 768], 'w_uk': [256, 512], 'w_uv': [256, 512], 'w_o_nsa': [512, D], 'w_o_mla': [512, D],
    'w_out': [D, D], 'w_fc1': [D, 4 * D], 'w_fc2': [4 * D, D],
}


class StopBuild(Exception):
    pass


def build_program(debug=None, stop=None):
    try:
        return _build_program(debug, stop)
    except StopBuild as ex:
        return ex.args[0]


def _build_program(debug=None, stop=None):
    nc = bass.Bass("TRN2", target_bir_lowering=False)
    I = {}
    for name, shp in IN_SHAPES.items():
        I[name] = nc.dram_tensor(name, shp, F32, kind="ExternalInput").ap()
    for name, (shp, dt) in CONST_SHAPES.items():
        I[name] = nc.dram_tensor(name, shp, dt, kind="ExternalInput").ap()
    out = nc.dram_tensor("out", [NO * 128, D], F32, kind="ExternalOutput").ap()
    x1s = nc.dram_tensor("x1s", [NO * 128, D], F32, kind="Internal").ap()
    dbg = {}
    if debug:
        for name, shp in debug.items():
            dbg[name] = nc.dram_tensor("dbg_" + name, shp, F32, kind="ExternalOutput").ap()

    P = Prog(nc)
    rr = {'ev': 0}

    def E(meth, **kw):
        return lambda e: getattr(e, meth)(**kw)

    def SB(name, shape, dt=F32):
        return nc.sbuf_tensor("sb_" + name, shape, dt)

    def kparts(ap, p=128):
        return ap.rearrange("(k p) n -> p k n", p=p)

    def checkpoint(name):
        if stop == name:
            raise StopBuild(nc)

    with ExitStack() as G:
        G.callback(P.emit)

        def sbg(name, shape, dt=F32):
            return G.enter_context(SB(name, shape, dt))
        psT = G.enter_context(nc.psum_tensor("psT", [128, 1024], F32))
        psZ = [G.enter_context(nc.psum_tensor(f"psZ{i}", [128, 512], F32)) for i in range(4)]
        psO = G.enter_context(nc.psum_tensor("psO", [128, 512], F32))
        psB = G.enter_context(nc.psum_tensor("psB", [128, 1024], BF16))
        ident = sbg("ident", [128, 128]); identb = sbg("identb", [128, 128], BF16)
        tri = sbg("tri", [128, 128], BF16); band = sbg("band", [128, 128], BF16)
        onesf = sbg("onesf", [128, 128])
        epsb = sbg("epsb", [128, 1])
        modT = sbg("modT", [128, 48])
        gsA = sbg("gsA", [128, 8]); gsM = sbg("gsM", [128, 8])
        gl = sbg("gl", [128, 8 + 8 + 3 + 2 + 8])
        ssr = sbg("ssr", [128, 16])
        junk = sbg("junk", [128, 1024], BF16)
        gbc = sbg("gbc", [128, 1024])
        dg = sbg("dg", [128, 128])
        OS = ExitStack()
        oTn = OS.enter_context(SB("oTn", [128, 4, NO * 128], BF16))

        def dma(out_ap, in_ap, r=(), w=(), q='sp'):
            return P.dma(q, lambda e: e.dma_start(out=out_ap, in_=in_ap), r=r, w=w)

        def mm(out_ap, lhsT, rhs, start, stop, r=(), w=(), skip=False):
            if skip:
                return P.op('pe', lambda e: e.matmul(out_ap, lhsT=lhsT, rhs=rhs, start=start, stop=stop,
                                                     skip_group_check=True), r=r, w=w)
            return P.op('pe', lambda e: e.matmul(out_ap, lhsT=lhsT, rhs=rhs, start=start, stop=stop), r=r, w=w)

        def act(out_ap, in_ap, func, r=(), w=(), **kw):
            return P.op('act', lambda e: e.activation(out=out_ap, in_=in_ap, func=func, **kw), r=r, w=w)

        def evac(out_ap, in_ap, r=(), w=()):
            rr['ev'] += 1
            if EVAC_ACT_ONLY or rr['ev'] % 2:
                return P.op('act', lambda e: e.activation(out=out_ap, in_=in_ap, func=AF.Copy), r=r, w=w)
            return P.op('dve', lambda e: e.tensor_scalar(out=out_ap, in0=in_ap, scalar1=1.0, scalar2=None, op0=ALU.mult), r=r, w=w)

        def ts(eng, out_ap, in0, s1, s2, op0, op1=None, r=(), w=()):
            if op1 is None:
                return P.op(eng, lambda e: e.tensor_scalar(out=out_ap, in0=in0, scalar1=s1, scalar2=None, op0=op0), r=r, w=w)
            return P.op(eng, lambda e: e.tensor_scalar(out=out_ap, in0=in0, scalar1=s1, scalar2=s2, op0=op0, op1=op1), r=r, w=w)

        def tt(eng, out_ap, in0, in1, op, r=(), w=()):
            return P.op(eng, lambda e: e.tensor_tensor(out=out_ap, in0=in0, in1=in1, op=op), r=r, w=w)

        def stt(eng, out_ap, in0, scalar, in1, op0, op1, r=(), w=()):
            return P.op(eng, lambda e: e.scalar_tensor_tensor(out=out_ap, in0=in0, scalar=scalar, in1=in1, op0=op0, op1=op1), r=r, w=w)

        def memset(eng, ap, val, w=()):
            return P.op(eng, lambda e: e.memset(ap, val), w=w)

        def dump(name, ap, r):
            if name in dbg:
                dma(dbg[name], ap, r=r, w=['dbg_' + name], q='pool')

        dma(ident[:], I['ident'][:, :], w=['ident'])
        dma(identb[:], I['identb'][:, :], w=['identb'])
        dma(tri[:], I['tri'][:, :], w=['tri'])
        dma(band[:], I['band'][:, :], w=['band'])
        dma(gl[:, 0:8], I['gmix_l'][:, :], w=['gl'])
        dma(gl[:, 8:16], I['gmlp_l'][:, :], w=['gl'])
        dma(gl[:, 16:19], I['gcq_l'][:, :], w=['gl'])
        dma(gl[:, 19:21], I['gckv_l'][:, :], w=['gl'])
        dma(gl[:, 21:29], I['gfin_l'][:, :], w=['gl'])
        memset('pool', onesf[:], 1.0, w=['onesf'])
        memset('pool', epsb[:], EPS, w=['epsb'])

        with ExitStack() as es:
            wad = [es.enter_context(SB(f"wad{i}", [128, 8, 512], F32)) for i in range(2)]
            cT = es.enter_context(SB("cT", [128, 8], F32))
            bl = es.enter_context(SB("bl", [128, 48], F32))
            dma(cT[:], I['c_l'][:, :], w=['cT'])
            dma(bl[:], I['bada_l'][:, :], w=['bl'])
            wv = kparts(I['w_ada'])
            for piece in range(12):
                buf = wad[piece % 2]
                bn = f"wad{piece % 2}"
                dma(buf[:], wv[:, :, piece * 512:(piece + 1) * 512], w=[bn])
                for jj in range(4):
                    j = piece * 4 + jj
                    for k in range(8):
                        mm(psZ[0][:, j:j + 1], buf[:, k, jj * 128:(jj + 1) * 128], cT[:, k:k + 1],
                           k == 0, k == 7, r=[bn, 'cT'], w=['psZ0'])
            tt('dve', modT[:], psZ[0][:, 0:48], bl[:], ALU.add, r=['psZ0', 'bl'], w=['modT'])
            stt('dve', gsA[:], modT[:, 8:16], 1.0, gl[:, 0:8], ALU.add, ALU.mult, r=['modT', 'gl'], w=['gsA'])
            stt('dve', gsM[:], modT[:, 32:40], 1.0, gl[:, 8:16], ALU.add, ALU.mult, r=['modT', 'gl'], w=['gsM'])
            dump('modT', modT[:], ['modT'])
        P.barrier()
        checkpoint('p0')
        shA = modT[:, 0:8]
        shM = modT[:, 24:32]

        def make_hT(xin, xin_res, xn, xn_res, hT, hT_res, gs, sh, slot):
            hT_pre(xin, xin_res, xn, xn_res, slot)
            hT_post(hT, hT_res, gs, sh)

        def hT_pre(xin, xin_res, xn, xn_res, slot):
            hT_norm(xin, xin_res, xn, xn_res, slot)
            hT_tr(xn, xn_res)

        def hT_norm(xin, xin_res, xn, xn_res, slot):
            act(junk[:], xin[:], AF.Square, r=[xin_res], w=['junk', f'ss{slot}'], accum_out=ssr[:, slot:slot + 1])
            act(ssr[:, slot + 4:slot + 5], ssr[:, slot:slot + 1], AF.Sqrt, r=[f'ss{slot}', 'epsb'], w=[f'sq{slot}'],
                scale=1.0 / D, bias=epsb[:, 0:1])
            P.op('dve', E('reciprocal', out=ssr[:, slot + 8:slot + 9], in_=ssr[:, slot + 4:slot + 5]),
                 r=[f'sq{slot}'], w=[f'rs{slot}'])
            rstd = ssr[:, slot + 8:slot + 9]
            act(xn[:, 0:512], xin[:, 0:512], AF.Identity, r=[xin_res, f'rs{slot}'], w=[xn_res + 'a'], scale=rstd)
            ts('dve', xn[:, 512:1024], xin[:, 512:1024], rstd, None, ALU.mult, r=[xin_res, f'rs{slot}'], w=[xn_res + 'b'])

        def hT_tr(xn, xn_res):
            for k in range(8):
                hf = 'a' if k < 4 else 'b'
                P.op('pe', E('transpose', out=psT[:, k * 128:(k + 1) * 128], in_=xn[:, k * 128:(k + 1) * 128],
                             identity=ident[:]),
                     r=[xn_res + hf, 'ident'], w=['psT' + hf])

        def hT_post(hT, hT_res, gs, sh, dve_share=2):
            for k in range(8):
                hf = 'a' if k < 4 else 'b'
                if k % dve_share == 0:
                    ts('dve', hT[:, k, :], psT[:, k * 128:(k + 1) * 128], gs[:, k:k + 1], sh[:, k:k + 1], ALU.mult, ALU.add,
                       r=['psT' + hf, 'gs', 'modT'], w=[hT_res])
                else:
                    act(hT[:, k, :], psT[:, k * 128:(k + 1) * 128], AF.Identity, r=['psT' + hf, 'gs', 'modT'], w=[hT_res],
                        scale=gs[:, k:k + 1], bias=sh[:, k:k + 1])

        dq = []
        LA = 2

        def defer(fn):
            dq.append(fn)
            while len(dq) > LA:
                dq.pop(0)()

        def flush():
            while dq:
                dq.pop(0)()

        zring = {'i': 0, 'n': 4}

        def nextZ():
            zring['i'] = (zring['i'] + 1) % zring['n']
            return psZ[zring['i']], f"psZ{zring['i']}"

        for g in range(2):
            with ExitStack() as NS:
                def sbn(name, shape, dt=F32):
                    return NS.enter_context(SB(f"{name}_g{g}", shape, dt))
                zqT = sbn("zqT", [128, 2, NO * 128], BF16)
                ksA = sbn("ksA", [128, S], BF16)
                kwA = sbn("kwA", [128, S], BF16)
                vsA = sbn("vsA", [128, NT, 66], BF16)
                vwA = sbn("vwA", [128, NT, 66], BF16)
                sg = sbn("sg", [128, NO, 24])
                kcA = sbn("kcA", [128, 512], BF16)
                VCA = sbn("VCA", [128, 4, 194], BF16)
                memset('pool', vsA[:, :, 64:66], 1.0, w=['vsA'])
                memset('pool', vwA[:, :, 64:66], 1.0, w=['vwA'])
                memset('pool', kcA[:], 0.0, w=['kcA'])
                memset('pool', VCA[:, :, 192:194], 1.0, w=['VCA'])
                checkpoint('n_a')
                dma(VCA[:, :, 64:192], I['OVL'][:, :, :], w=['VCA'])
                checkpoint('n_b')
                with ExitStack() as PS:
                    def sbp(name, shape, dt=F32):
                        return PS.enter_context(SB(f"{name}_g{g}", shape, dt))
                    wn = sbp("wn", [128, 8, 664], BF16)
                    xr = [sbp(f"xr{i}", [128, 1024]) for i in range(3)]
                    hTr = [sbp(f"hTr{i}", [128, 8, 128], BF16) for i in range(2)]
                    kcT = sbp("kcT", [128, S], BF16)
                    vcT = sbp("vcT", [128, S], BF16)
                    w1 = sbp("w1", [64, 32, 256], BF16)
                    w2 = sbp("w2", [128, 2, 64], BF16)
                    peT = sbp("peT", [64, 32], BF16)
                    hb = sbp("hb", [128, 2])
                    hid = sbp("hid", [128, 2, 512], BF16)
                    wiv = kparts(I['w_in'])
                    colmap = [(0, C_Q + 256 * g, 256), (256, C_KC + 64 * g, 64), (320, C_VC + 64 * g, 64),
                              (384, C_KS + 64 * g, 64), (448, C_VS + 64 * g, 64), (512, C_KW + 64 * g, 64),
                              (576, C_VW + 64 * g, 64), (640, C_G, 24)]
                    for (d0, s0, n) in colmap:
                        dma(wn[:, :, d0:d0 + n], wiv[:, :, s0:s0 + n], w=['wn'], q='pool')
                    print("sbuf remaining after proj alloc", nc.sbuf_bytes_remaining, flush=True)
                    checkpoint('n_w')
                    for lt in range(NT):
                        if lt == 1:
                            checkpoint('n_t0')
                        if lt == 2:
                            checkpoint('n_t1')
                        sl = lt % 2
                        hT = hTr[sl]
                        hres = f"hT{sl}"

                        def nsa_norm(t2):
                            s3 = t2 % 3
                            dma(xr[s3][:], I['xl'][t2 * 128:(t2 + 1) * 128, :], w=[f"xr{s3}a", f"xr{s3}b", f"xr{s3}"])
                            hT_norm(xr[s3], f"xr{s3}", xr[s3], f"xr{s3}", t2 % 4)

                        def nsa_tr_post(t2):
                            hT_tr(xr[t2 % 3], f"xr{t2 % 3}")
                            hT_post(hTr[t2 % 2], f"hT{t2 % 2}", gsA, shA, dve_share=1)
                        if lt == 0:
                            nsa_norm(0)
                            nsa_tr_post(0)
                            nsa_norm(1)
                        if lt + 1 < NT:
                            nsa_tr_post(lt + 1)
                        tsl = slice(lt * 128, (lt + 1) * 128)
                        pz, pzn = nextZ()
                        for j, c0 in enumerate((256, 320, 384, 512)):
                            for k in range(8):
                                mm(pz[0:64, j * 128:(j + 1) * 128], wn[:, k, c0:c0 + 64], hT[:, k, :], k == 0, k == 7,
                                   r=['wn', hres], w=[pzn])
                        pv, pvn = nextZ()
                        for j, c0 in enumerate((448, 576)):
                            for k in range(8):
                                mm(pv[:, j * 64:(j + 1) * 64], hT[:, k, :], wn[:, k, c0:c0 + 64], k == 0, k == 7,
                                   r=['wn', hres], w=[pvn])
                        own = (lt % 2 == 1)
                        i = lt // 2
                        if own:
                            for n2 in range(2):
                                for k in range(8):
                                    mm(pv[:, 128 + n2 * 128:256 + n2 * 128], wn[:, k, n2 * 128:(n2 + 1) * 128], hT[:, k, :],
                                       k == 0, k == 7, r=['wn', hres], w=[pvn])
                            for k in range(8):
                                mm(pv[:, 384:408], hT[:, k, :], wn[:, k, 640:664], k == 0, k == 7, r=['wn', hres], w=[pvn])
                        if lt + 2 < NT:
                            nsa_norm(lt + 2)
                        evac(kcT[0:64, tsl], pz[0:64, 0:128], r=[pzn], w=['kcT'])
                        evac(vcT[0:64, tsl], pz[0:64, 128:256], r=[pzn], w=['vcT'])
                        evac(ksA[0:64, tsl], pz[0:64, 256:384], r=[pzn], w=['ksA'])
                        evac(kwA[0:64, tsl], pz[0:64, 384:512], r=[pzn], w=['kwA'])
                        evac(vsA[:, lt, 0:64], pv[:, 0:64], r=[pvn], w=['vsA'])
                        evac(vwA[:, lt, 0:64], pv[:, 64:128], r=[pvn], w=['vwA'])
                        if own:
                            evac(zqT[:, 0, i * 128:(i + 1) * 128], pv[:, 128:256], r=[pvn], w=['zqT'])
                            evac(zqT[:, 1, i * 128:(i + 1) * 128], pv[:, 256:384], r=[pvn], w=['zqT'])
                            act(sg[:, i, :], pv[:, 384:408], AF.Sigmoid, r=[pvn], w=['sg'])
                    checkpoint('n_tiles')
                    dma(ksA[64:69, :], I['KA'][:, :], w=['ksA'])
                    dma(ksA[69:128, :], I['EF'][:, :], w=['ksA'])
                    dma(kwA[64:69, :], I['KA'][:, :], w=['kwA'])
                    dma(kcA[64:69, :], I['KCA'][:, :], w=['kcA'])
                    checkpoint('n_aug')
                    for which in range(2):
                        src = kcT if which == 0 else vcT
                        srcn = 'kcT' if which == 0 else 'vcT'
                        w1d = I['w_ck1'] if which == 0 else I['w_cv1']
                        w2d = I['w_ck2'] if which == 0 else I['w_cv2']
                        ped = I['peck_t'] if which == 0 else I['pecv_t']
                        dma(w1[:], w1d.rearrange("(l d) h -> d l h", d=64), w=['w1'], q='pool')
                        dma(w2[:], kparts(w2d), w=['w2'], q='pool')
                        dma(peT[:], ped[:, :], w=['peT'], q='pool')
                        pz, pzn = nextZ()
                        for hc in range(2):
                            for l in range(32):
                                mm(pz[:, hc:hc + 1], w1[:, l, hc * 128:(hc + 1) * 128], peT[:, l:l + 1], l == 0, l == 31,
                                   r=['w1', 'peT'], w=[pzn])
                        evac(hb[:], pz[:, 0:2], r=[pzn], w=['hb'])
                        memset('pool', hid[:], 0.0, w=['hid'])
                        for hc in range(2):
                            pz, pzn = nextZ()
                            for l in range(32):
                                mm(pz[:, 0:511], w1[:, l, hc * 128:(hc + 1) * 128], src[0:64, l:l + 16 * 510 + 1:16],
                                   l == 0, l == 31, r=['w1', srcn], w=[pzn])
                            act(hid[:, hc, 0:511], pz[:, 0:511], AF.Silu, r=[pzn, 'hb'], w=['hid'], bias=hb[:, hc:hc + 1])
                        if which == 0:
                            pz, pzn = nextZ()
                            for hc in range(2):
                                mm(pz[0:64, 0:511], w2[:, hc, :], hid[:, hc, 0:511], hc == 0, hc == 1, r=['w2', 'hid'], w=[pzn])
                            evac(kcA[0:64, 0:511], pz[0:64, 0:511], r=[pzn], w=['kcA'])
                        else:
                            pz, pzn = nextZ()
                            for ct in range(4):
                                for hc in range(2):
                                    mm(pz[:, ct * 64:(ct + 1) * 64], hid[:, hc, ct * 128:(ct + 1) * 128], w2[:, hc, :],
                                       hc == 0, hc == 1, r=['w2', 'hid'], w=[pzn])
                            evac(VCA[:, :, 0:64], pz[:, 0:256].rearrange("p (a b) -> p a b", b=64), r=[pzn], w=['VCA'])
                    if g == 0:
                        dump('kcA', kcA[0:64, :], ['kcA'])
                        dump('ksA', ksA[0:64, 0:1024], ['ksA'])
                        dump('zqT', zqT[:, 0, 0:512], ['zqT'])
                P.barrier()
                checkpoint(f'nproj{g}')
                with ExitStack() as AS:
                    def sba(name, shape, dt=F32):
                        return AS.enter_context(SB(f"{name}_g{g}", shape, dt))
                    QA = [sba(f"QA{h}", [128, NO * 128], BF16) for h in range(4)]
                    PT = [sba(f"PT{i}", [128, 512], BF16) for i in range(4)]
                    PTw = [sba(f"PTw{i}", [128, 640], BF16) for i in range(3)]
                    Zm4 = [[sba(f"Zm{q_}_{i}", [128, 128]) for i in range(3)] for q_ in range(4)]
                    late_tr = []
                    RqAll = [[sba(f"Rq{p_}_{i}", [128, 512], BF16) for i in range(3)] for p_ in range(2)]
                    ocmpAll = [sba(f"ocmp{p_}", [128, 4, 4, 64]) for p_ in range(2)]
                    ostage = sba("ostage", [128, NO, 256], BF16)
                    cmask = sba("cmask", [128, 8, 128], BF16)
                    bon = [sba(f"bon{i}", [128, 128]) for i in range(2)]
                    pslc = sba("pslc", [128, 128])
                    scb = sba("scb", [128, 128])
                    mrb = sba("mrb", [128, 128])
                    mx = sba("mx", [128, 16])
                    negm = sba("negm", [128, 128])
                    sm = sba("sm", [128, 64])
                    t1b = sba("t1b", [128, 4, 64])
                    zring['n'] = 3
                    for q_ in range(4):
                        for r_ in range(3):
                            memset('pool', Zm4[q_][r_][:], 0.0, w=[f'Zm{q_}_{r_}'])
                    dma(cmask[:], I['cmask'][:, :, :], w=['cmask'])
                    for hg in range(4):
                        h = 4 * g + hg
                        r0 = (hg % 2) * 64
                        if hg == 0:
                            act(QA[hg][0:64, :], zqT[r0:r0 + 64, hg // 2, :], AF.Copy, r=['zqT'], w=[f'QA{hg}'])
                        elif hg == 1:
                            P.op('dve', E('tensor_copy', out=QA[hg][0:64, :], in_=zqT[r0:r0 + 64, hg // 2, :]),
                                 r=['zqT'], w=[f'QA{hg}'])
                        else:
                            P.op('pool', E('tensor_copy', out=QA[hg][0:64, :], in_=zqT[r0:r0 + 64, hg // 2, :]),
                                 r=['zqT'], w=[f'QA{hg}'])
                        dma(QA[hg][64:69, :], I['QAh'][h, :, :], w=[f'QA{hg}'])
                    pti = {'i': 0, 'w': 0}
                    def cmp_topk(c):
                        for qi in range(4):
                            i = 4 * c + qi
                            qt = 2 * i + 1
                            qsl = slice(i * 128, (i + 1) * 128)
                            ctm = (qt - 1) // 16
                            bsl = i % 2
                            dma(bon[bsl][:], I['bonus'][i, :, :], w=[f'bon{bsl}'])
                            for hg in range(4):
                                h = 4 * g + hg
                                pz, pzn = nextZ()
                                for ct in range(ctm + 1):
                                    m = qt - 16 * ct
                                    partial = m < 17
                                    mm(pz[:, ct * 128:(ct + 1) * 128], kcA[0:69, ct * 128:(ct + 1) * 128], QA[hg][0:69, qsl],
                                       True, not partial, r=['kcA', f'QA{hg}'], w=[pzn])
                                    if partial:
                                        mm(pz[:, ct * 128:(ct + 1) * 128], identb[:], cmask[:, (m - 1) // 2, :], False, True,
                                           r=['identb', 'cmask'], w=[pzn])
                                ptc = PT[pti['i'] % 4]
                                ptn = f"PT{pti['i'] % 4}"
                                pti['i'] += 1
                                ncol = (ctm + 1) * 128
                                act(ptc[:, 0:ncol], pz[:, 0:ncol], AF.Exp, r=[pzn], w=[ptn], scale=SC_NSA)
                                def cmpB(hg=hg, h=h, i=i, qi=qi, ctm=ctm, ptc=ptc, ptn=ptn):
                                    ub = 'psTa' if hg < 2 else 'psTb'
                                    uo = hg * 256
                                    for ct in range(ctm + 1):
                                        mm(psT[:, uo:uo + 193], ptc[:, ct * 128:(ct + 1) * 128], VCA[:, ct, 0:193],
                                           ct == 0, ct == ctm, r=[ptn, 'VCA'], w=[ub])
                                    ts('dve', sm[:, hg:hg + 1], psT[:, uo + 192:uo + 193], 1e-30, None, ALU.max, r=[ub], w=[f'sm{hg}'])
                                    P.op('dve', E('reciprocal', out=sm[:, 4 + hg:5 + hg], in_=sm[:, hg:hg + 1]),
                                         r=[f'sm{hg}'], w=[f'smr{hg}'])
                                    if hg == 0:
                                        ts('dve', pslc[:], psT[:, uo + 64:uo + 192], sm[:, 4 + hg:5 + hg], None, ALU.mult,
                                           r=[ub, f'smr{hg}'], w=['pslc'])
                                    else:
                                        stt('dve', pslc[:], psT[:, uo + 64:uo + 192], sm[:, 4 + hg:5 + hg], pslc[:], ALU.mult, ALU.add,
                                            r=[ub, f'smr{hg}', 'pslc'], w=['pslc'])
                                    tt('dve', sm[:, 8 + hg:9 + hg], sm[:, 4 + hg:5 + hg], sg[:, i, h:h + 1], ALU.mult,
                                       r=[f'smr{hg}', 'sg'], w=[f'smg{hg}'])
                                    ts('dve', ocmpAll[c % 2][:, qi, hg, :], psT[:, uo:uo + 64], sm[:, 8 + hg:9 + hg], None, ALU.mult,
                                       r=[ub, f'smg{hg}'], w=[f'ocmp{c % 2}'])
                                defer(cmpB)
                            flush()
                            tt('dve', scb[:], pslc[:], bon[bsl][:], ALU.add, r=['pslc', f'bon{bsl}'], w=['scb'])
                            P.op('dve', E('max', out=mx[:, 0:8], in_=scb[:]), r=['scb'], w=['mx0'])
                            P.op('dve', E('match_replace', out=mrb[:], in_to_replace=mx[:, 0:8], in_values=scb[:],
                                                                  imm_value=-3e9), r=['scb', 'mx0'], w=['mrb'])
                            P.op('dve', E('max', out=mx[:, 8:16], in_=mrb[:]), r=['mrb'], w=['mx1'])
                            for r_ in range((2 * qt + 1) // 58 + 1):
                                nb = min(58, 128 - 58 * r_)
                                ts('dve', Zm4[qi][r_][:, 69:69 + nb], scb[:, 58 * r_:58 * r_ + nb], mx[:, 15:16], NEG, ALU.is_lt, ALU.mult,
                                   r=['scb', 'mx1'], w=[f'Zm{qi}_{r_}'])
                                def trB(r_=r_, qi=qi, c=c, zt=Zm4[qi][r_], ztn=f'Zm{qi}_{r_}'):
                                    pz, pzn = nextZ()
                                    P.op('pe', E('transpose', out=pz[:, 0:128], in_=zt[:], identity=ident[:]),
                                         r=[ztn, 'ident'], w=[pzn])
                                    act(RqAll[c % 2][r_][64:128, qi * 128:(qi + 1) * 128], pz[64:128, 0:128], AF.Copy, r=[pzn],
                                        w=[f'Rq{c % 2}_{r_}'])
                                late_tr.append(trB)
                            if g == 0 and c == 1 and qi == 0:
                                dump('pslc', pslc[:], ['pslc'])
                                dump('negm', Zm4[0][0][:], ['Zm0_0'])
                    def run_late():
                        while late_tr:
                            late_tr.pop(0)()
                    cmp_topk(0)
                    run_late()
                    for c in range(8):
                        for hg in range(4):
                            h = 4 * g + hg
                            if hg == 1 and c + 1 < 8:
                                cmp_topk(c + 1)
                            if hg == 3:
                                run_late()
                            for qi in range(4):
                                i = 4 * c + qi
                                qt = 2 * i + 1
                                qsl = slice(i * 128, (i + 1) * 128)
                                kts = [kt for kt in range(qt - 4, qt + 1) if kt >= 0]
                                pw = PTw[pti['w'] % 3]
                                pwn = f"PTw{pti['w'] % 3}"
                                pti['w'] += 1
                                pzA, pzAn = nextZ()
                                pzB, pzBn = nextZ()
                                for j, kt in enumerate(kts):
                                    ksl = slice(kt * 128, (kt + 1) * 128)
                                    last = (kt == qt)
                                    dst = pzB[:, 0:128] if last else pzA[:, j * 128:(j + 1) * 128]
                                    dn = pzBn if last else pzAn
                                    masked = last or (kt == qt - 4)
                                    mm(dst, kwA[0:69, ksl], QA[hg][0:69, qsl], True, not masked, r=['kwA', f'QA{hg}'], w=[dn])
                                    if masked:
                                        mm(dst, identb[:], tri[:] if last else band[:], False, True, r=['identb', 'tri', 'band'], w=[dn])
                                na = len(kts) - 1
                                if na > 0:
                                    act(pw[:, 0:na * 128], pzA[:, 0:na * 128], AF.Exp, r=[pzAn], w=[pwn + 'a'], scale=SC_NSA)
                                act(pw[:, 512:640], pzB[:, 0:128], AF.Exp, r=[pzBn], w=[pwn + 'b'], scale=SC_NSA)
                                def winB(kts=kts, qt=qt, qi=qi, pw=pw, pwn=pwn):
                                    for j, kt in enumerate(kts):
                                        last = (kt == qt)
                                        src = pw[:, 512:640] if last else pw[:, j * 128:(j + 1) * 128]
                                        mm(psO[:, qi * 128:qi * 128 + 65], src, vwA[:, kt, 0:65], (j == 0 and qi == 0), last,
                                           r=[pwn + 'a', pwn + 'b', 'vwA'], w=['psO'], skip=True)
                                defer(winB)
                            for r_ in range((16 * c + 15) // 58 + 1):
                                P.op('pool', E('tensor_copy', out=RqAll[c % 2][r_][0:69, :], in_=QA[hg][0:69, c * 512:(c + 1) * 512]),
                                     r=[f'QA{hg}'], w=[f'Rq{c % 2}_{r_}'])
                            ktmax = 8 * c + 7
                            for kt in range(ktmax + 1):
                                qmin = max(0, (kt - (8 * c + 1) + 1) // 2)
                                cs = slice(qmin * 128, 512)
                                qs = slice(c * 512 + qmin * 128, (c + 1) * 512)
                                ksl = slice(kt * 128, (kt + 1) * 128)
                                diag = (kt % 2 == 1) and (kt >= 8 * c + 1)
                                pz, pzn = nextZ()
                                if pzn == 'psZ3':
                                    pz, pzn = nextZ()
                                rr_ = (2 * kt) // 58
                                mm(pz[:, cs], ksA[:, ksl], RqAll[c % 2][rr_][:, cs], True, not diag, r=['ksA', f'Rq{c % 2}_{rr_}'], w=[pzn])
                                if diag:
                                    qd = (kt - (8 * c + 1)) // 2
                                    mm(pz[:, qd * 128:(qd + 1) * 128], identb[:], tri[:], False, True, r=['identb', 'tri'], w=[pzn])
                                ptc = PT[pti['i'] % 4]
                                ptn = f"PT{pti['i'] % 4}"
                                pti['i'] += 1
                                act(ptc[:, cs], pz[:, cs], AF.Exp, r=[pzn], w=[ptn], scale=SC_NSA)
                                def selB(kt=kt, qmin=qmin, ptc=ptc, ptn=ptn, c=c):
                                    for qi in range(qmin, 4):
                                        mm(psZ[3][:, qi * 128:qi * 128 + 65], ptc[:, qi * 128:(qi + 1) * 128], vsA[:, kt, 0:65],
                                           (kt == 0 and qi == 0), kt == 8 * c + 1 + 2 * qi, r=[ptn, 'vsA'], w=['psZ3'], skip=True)
                                defer(selB)
                            def finB(c=c, hg=hg, h=h):
                                i0 = 4 * c
                                ts('dve', sm[:, 16:20], psO[:, 64:512:128], 1e-30, None, ALU.max, r=['psO'], w=['smw'])
                                P.op('dve', E('reciprocal', out=sm[:, 20:24], in_=sm[:, 16:20]), r=['smw'], w=['smwr'])
                                tt('dve', sm[:, 24:28], sm[:, 20:24], sg[:, i0:i0 + 4, 16 + h], ALU.mult, r=['smwr', 'sg'], w=['smwm'])
                                ts('dve', sm[:, 28:32], psZ[3][:, 64:512:128], 1e-30, None, ALU.max, r=['psZ3'], w=['sms'])
                                P.op('dve', E('reciprocal', out=sm[:, 32:36], in_=sm[:, 28:32]), r=['sms'], w=['smsr'])
                                tt('dve', sm[:, 36:40], sm[:, 32:36], sg[:, i0:i0 + 4, 8 + h], ALU.mult, r=['smsr', 'sg'], w=['smsm'])
                                for qi in range(4):
                                    stt('dve', t1b[:, qi, :], psO[:, qi * 128:qi * 128 + 64], sm[:, 24 + qi:25 + qi], ocmpAll[c % 2][:, qi, hg, :],
                                        ALU.mult, ALU.add, r=['psO', 'smwm', f'ocmp{c % 2}'], w=['t1b'])
                                    stt('dve', ostage[:, i0 + qi, hg * 64:(hg + 1) * 64], psZ[3][:, qi * 128:qi * 128 + 64],
                                        sm[:, 36 + qi:37 + qi], t1b[:, qi, :], ALU.mult, ALU.add, r=['psZ3', 'smsm', 't1b'], w=['ostage'])
                            defer(finB)
                    flush()
                    if g == 0:
                        dump('ostage', ostage[:, 0:4, :], ['ostage'])
                    zring['n'] = 4
                    for f in range(2):
                        for i8 in range(4):
                            for j in range(8):
                                i = i8 * 8 + j
                                P.op('pe', E('transpose', out=psB[:, j * 128:(j + 1) * 128],
                                                                                in_=ostage[:, i, f * 128:(f + 1) * 128],
                                                                                identity=identb[:]),
                                     r=['ostage', 'identb'], w=['psB'])
                            evac(oTn[:, 2 * g + f, i8 * 1024:(i8 + 1) * 1024], psB[:, :], r=['psB'], w=['oTn'])
                P.barrier()

        checkpoint('nsa')
        oTm = OS.enter_context(SB("oTm", [128, 4, NO * 128], BF16))
        with ExitStack() as MS:
            def sbm(name, shape, dt=F32):
                return MS.enter_context(SB(name, shape, dt))
            ckvT = sbm("ckvT", [128, 2, S], BF16)
            cqT = sbm("cqT", [128, 3, NO * 128], BF16)
            KT = sbm("KT", [128, S], BF16)
            with ExitStack() as PS:
                def sbp(name, shape, dt=F32):
                    return PS.enter_context(SB(name, shape, dt))
                wm = sbp("wm", [128, 8, 640], BF16)
                wkrA = sbp("wkrA", [128, 8, 96], BF16)
                wkrB = sbp("wkrB", [128, 8, 96], BF16)
                xr = [sbp(f"mxr{i}", [128, 1024]) for i in range(3)]
                hTr = [sbp(f"mhTr{i}", [128, 8, 128], BF16) for i in range(2)]
                zf = sbp("zf", [128, 5, 128])
                zs = sbp("zs", [128, 5, 128])
                rcb = sbp("rcb", [128, 2, 128])
                ctab = [sbp(f"ctab{i}", [128, 2, 128]) for i in range(2)]
                rt = sbp("rt", [128, 2, 128])
                wiv = kparts(I['w_in'])
                dma(wm[:, :, 0:384], wiv[:, :, C_QD:C_QD + 384], w=['wm'], q='pool')
                dma(wm[:, :, 384:640], wiv[:, :, C_KVD:C_KVD + 256], w=['wm'], q='pool')
                memset('pool', wkrA[:], 0.0, w=['wkrA'])
                memset('pool', wkrB[:], 0.0, w=['wkrB'])
                dma(wkrA[:, :, 64:96], wiv[:, :, C_KR:C_KR + 32], w=['wkrA'], q='pool')
                dma(wkrB[:, :, 80:96], wiv[:, :, C_KR:C_KR + 16], w=['wkrB'], q='pool')
                dma(wkrB[:, :, 64:80], wiv[:, :, C_KR + 16:C_KR + 32], w=['wkrB'], q='pool')
                ts('pool', wkrB[:, :, 64:80], wkrB[:, :, 64:80], -1.0, None, ALU.mult, r=['wkrB'], w=['wkrB'])
                dma(KT[96:97, :], I['KA'][4:5, :], w=['KT'])
                for lt in range(NT):
                    sl = lt % 2
                    hT = hTr[sl]
                    hres = f"mhT{sl}"

                    def m1_norm(t2):
                        s3 = t2 % 3
                        dma(xr[s3][:], I['xl'][t2 * 128:(t2 + 1) * 128, :], w=[f"mxr{s3}a", f"mxr{s3}b", f"mxr{s3}"])
                        hT_norm(xr[s3], f"mxr{s3}", xr[s3], f"mxr{s3}", t2 % 4)

                    def m1_tr_post(t2):
                        hT_tr(xr[t2 % 3], f"mxr{t2 % 3}")
                        hT_post(hTr[t2 % 2], f"mhT{t2 % 2}", gsA, shA, dve_share=1)
                    if lt == 0:
                        m1_norm(0)
                        m1_tr_post(0)
                        m1_norm(1)
                    dma(ctab[sl][64:96, 0, :], I['cosk'][:, lt * 128:(lt + 1) * 128], w=[f'ctab{sl}'])
                    dma(ctab[sl][64:96, 1, :], I['sink'][:, lt * 128:(lt + 1) * 128], w=[f'ctab{sl}'])
                    tsl = slice(lt * 128, (lt + 1) * 128)
                    own = (lt % 2 == 1)
                    i = lt // 2
                    if lt + 1 < NT:
                        m1_tr_post(lt + 1)
                    ntl = [(384, 0), (512, 1)] + ([(0, 2), (128, 3), (256, 4)] if own else [])
                    pz, pzn = nextZ()
                    pz2, pz2n = nextZ()
                    for (c0, slot) in ntl:
                        dst = pz[:, slot * 128:(slot + 1) * 128] if slot < 4 else pz2[:, 0:128]
                        dn = pzn if slot < 4 else pz2n
                        for k in range(8):
                            mm(dst, wm[:, k, c0:c0 + 128], hT[:, k, :], k == 0, k == 7, r=['wm', hres], w=[dn])
                    for k in range(8):
                        mm(pz2[0:96, 128:256], wkrA[:, k, :], hT[:, k, :], k == 0, k == 7, r=['wkrA', hres], w=[pz2n])
                    for k in range(8):
                        mm(pz2[0:96, 256:384], wkrB[:, k, :], hT[:, k, :], k == 0, k == 7, r=['wkrB', hres], w=[pz2n])
                    if lt + 2 < NT:
                        m1_norm(lt + 2)
                    nsl = 5 if own else 2
                    for (c0, slot) in ntl:
                        srcp = pz[:, slot * 128:(slot + 1) * 128] if slot < 4 else pz2[:, 0:128]
                        sn = pzn if slot < 4 else pz2n
                        evac(zf[:, slot, :], srcp, r=[sn], w=[f'zf{slot}'])
                        act(zs[:, slot, :], srcp, AF.Square, r=[sn], w=[f'zs{slot}'])
                    pz3, pz3n = nextZ()
                    for j, slot in enumerate((0, 1)):
                        mm(pz3[:, 0:128], onesf[:], zs[:, slot, :], j == 0, j == 1, r=['onesf', f'zs{slot}'], w=[pz3n])
                    if own:
                        for j, slot in enumerate((2, 3, 4)):
                            mm(pz3[:, 128:256], onesf[:], zs[:, slot, :], j == 0, j == 2, r=['onesf', f'zs{slot}'], w=[pz3n])
                    act(rcb[:, 0, :], pz3[:, 0:128], AF.Sqrt, r=[pz3n, 'epsb'], w=['rcb0'], scale=1.0 / 256, bias=epsb[:, 0:1])
                    P.op('dve', E('reciprocal', out=rcb[:, 0, :], in_=rcb[:, 0, :]), r=['rcb0'], w=['rcb0'])
                    for slot in (0, 1):
                        tt('pool' if slot else 'dve', ckvT[:, slot, tsl], zf[:, slot, :], rcb[:, 0, :], ALU.mult,
                           r=[f'zf{slot}', 'rcb0'], w=['ckvT'])
                    if own:
                        act(rcb[:, 1, :], pz3[:, 128:256], AF.Sqrt, r=[pz3n, 'epsb'], w=['rcb1'], scale=1.0 / 384, bias=epsb[:, 0:1])
                        P.op('dve', E('reciprocal', out=rcb[:, 1, :], in_=rcb[:, 1, :]), r=['rcb1'], w=['rcb1'])
                        for slot in (2, 3, 4):
                            tt('pool' if slot % 2 else 'dve', cqT[:, slot - 2, i * 128:(i + 1) * 128], zf[:, slot, :], rcb[:, 1, :],
                               ALU.mult, r=[f'zf{slot}', 'rcb1'], w=['cqT'])
                    tt('dve', rt[64:96, 0, :], pz2[64:96, 128:256], ctab[sl][64:96, 0, :], ALU.mult, r=[pz2n, f'ctab{sl}'], w=['rt0'])
                    tt('dve', rt[64:96, 1, :], pz2[64:96, 256:384], ctab[sl][64:96, 1, :], ALU.mult, r=[pz2n, f'ctab{sl}'], w=['rt1'])
                    tt('pool', KT[64:96, tsl], rt[64:96, 0, :], rt[64:96, 1, :], ALU.add, r=['rt0', 'rt1'], w=['KT'])
                dump('ckvT', ckvT[:, 0, 0:1024], ['ckvT'])
                dump('cqT', cqT[:, 0, 0:512], ['cqT'])
                dump('krot', KT[64:96, 0:1024], ['KT'])
            P.barrier()
            checkpoint('m1')
            with ExitStack() as AS:
                def sba(name, shape, dt=F32):
                    return AS.enter_context(SB(name, shape, dt))
                wuq = sba("wuq", [128, 3, 768], BF16)
                wuqB = sba("wuqB", [128, 3, 768], BF16)
                wuk = sba("wuk", [128, 2, 512], BF16)
                wuv = sba("wuv", [128, 2, 512], BF16)
                VH = sba("VH", [128, NT, 66], BF16)
                QT = sba("QT", [128, NO * 128], BF16)
                PT = [sba(f"MPT{i}", [128, 512], BF16) for i in range(4)]
                ostage = sba("mostage", [128, NO, 128], BF16)
                qtab = [sba(f"qtab{i}", [128, 2, 512]) for i in range(2)]
                rt = sba("mrt", [128, 2, 512])
                sm = sba("msm", [128, 8])
                dma(wuq[:], kparts(I['w_uq']), w=['wuq'], q='pool')
                dma(wuk[:], kparts(I['w_uk']), w=['wuk'], q='pool')
                dma(wuv[:], kparts(I['w_uv']), w=['wuv'], q='pool')
                for k in range(3):
                    ts('pool', wuq[:, k, :], wuq[:, k, :], gl[:, 16 + k:17 + k], None, ALU.mult, r=['wuq', 'gl'], w=['wuq'])
                for k in range(2):
                    ts('pool', wuk[:, k, :], wuk[:, k, :], gl[:, 19 + k:20 + k], None, ALU.mult, r=['wuk', 'gl'], w=['wuk'])
                    ts('pool', wuv[:, k, :], wuv[:, k, :], gl[:, 19 + k:20 + k], None, ALU.mult, r=['wuv', 'gl'], w=['wuv'])
                memset('pool', wuqB[:], 0.0, w=['wuqB'])
                wq4 = wuq[:].rearrange("p k (h c) -> p k h c", c=96)
                wb4 = wuqB[:].rearrange("p k (h c) -> p k h c", c=96)
                for k in range(3):
                    ts('pool', wb4[:, k, :, 64:80], wq4[:, k, :, 80:96], -1.0, None, ALU.mult, r=['wuq'], w=['wuqB'])
                    P.op('pool', E('tensor_copy', out=wb4[:, k, :, 80:96], in_=wq4[:, k, :, 64:80]), r=['wuq'], w=['wuqB'])
                memset('pool', VH[:, :, 64:66], 1.0, w=['VH'])
                memset('pool', QT[96:97, :], NEG, w=['QT'])
                zring['n'] = 3
                pti = {'i': 0}
                hz = {'i': 0}

                def nextH():
                    hz['i'] = (hz['i'] + 1) % 2
                    return (psT[:, 0:512], 'psTa') if hz['i'] == 0 else (psT[:, 512:1024], 'psTb')

                for h in range(8):
                    for ch in range(16):
                        pz, pzn = nextH()
                        for k in range(2):
                            mm(pz[0:64, :], wuk[:, k, h * 64:(h + 1) * 64], ckvT[:, k, ch * 512:(ch + 1) * 512], k == 0, k == 1,
                               r=['wuk', 'ckvT'], w=[pzn])
                        evac(KT[0:64, ch * 512:(ch + 1) * 512], pz[0:64, :], r=[pzn], w=['KT'])
                    for t8 in range(8):
                        pz, pzn = nextH()
                        for j in range(8):
                            lt = t8 * 8 + j
                            for k in range(2):
                                mm(pz[:, j * 64:(j + 1) * 64], ckvT[:, k, lt * 128:(lt + 1) * 128], wuv[:, k, h * 64:(h + 1) * 64],
                                   k == 0, k == 1, r=['wuv', 'ckvT'], w=[pzn])
                        evac(VH[:, t8 * 8:(t8 + 1) * 8, 0:64], pz.rearrange("p (a b) -> p a b", b=64), r=[pzn], w=['VH'])
                    for c in range(8):
                        csl = slice(c * 512, (c + 1) * 512)
                        qs = c % 2
                        dma(qtab[qs][64:96, 0, :], I['cosq'][:, csl], w=[f'qtab{qs}'])
                        dma(qtab[qs][64:96, 1, :], I['sinq'][:, csl], w=[f'qtab{qs}'])
                        pzA, pzAn = nextH()
                        for k in range(3):
                            mm(pzA[0:96, :], wuq[:, k, h * 96:(h + 1) * 96], cqT[:, k, csl], k == 0, k == 2, r=['wuq', 'cqT'], w=[pzAn])
                        pzB, pzBn = nextH()
                        for k in range(3):
                            mm(pzB[0:96, :], wuqB[:, k, h * 96:(h + 1) * 96], cqT[:, k, csl], k == 0, k == 2, r=['wuqB', 'cqT'], w=[pzBn])
                        act(QT[0:64, csl], pzA[0:64, :], AF.Copy, r=[pzAn], w=['QT'])
                        tt('dve', rt[64:96, 0, :], pzA[64:96, :], qtab[qs][64:96, 0, :], ALU.mult, r=[pzAn, f'qtab{qs}'], w=['mrt0'])
                        tt('dve', rt[64:96, 1, :], pzB[64:96, :], qtab[qs][64:96, 1, :], ALU.mult, r=[pzBn, f'qtab{qs}'], w=['mrt1'])
                        tt('pool', QT[64:96, csl], rt[64:96, 0, :], rt[64:96, 1, :], ALU.add, r=['mrt0', 'mrt1'], w=['QT'])
                    if h == 0:
                        dump('QT', QT[0:96, 0:512], ['QT'])
                        dump('KT', KT[0:96, 0:1024], ['KT'])
                    for c in range(8):
                        ob, obn = (psO, 'psO') if c % 2 == 0 else (psZ[3], 'psZ3')
                        ktmax = 8 * c + 7
                        for kt in range(ktmax + 1):
                            qmin = max(0, (kt - (8 * c + 1) + 1) // 2)
                            cs = slice(qmin * 128, 512)
                            qs_ = slice(c * 512 + qmin * 128, (c + 1) * 512)
                            ksl = slice(kt * 128, (kt + 1) * 128)
                            diag = (kt % 2 == 1) and (kt >= 8 * c + 1)
                            pz, pzn = nextZ()
                            if pzn == 'psZ3':
                                pz, pzn = nextZ()
                            mm(pz[:, cs], KT[0:97, ksl], QT[0:97, qs_], True, not diag, r=['KT', 'QT'], w=[pzn])
                            if diag:
                                qd = (kt - (8 * c + 1)) // 2
                                mm(pz[:, qd * 128:(qd + 1) * 128], identb[:], tri[:], False, True, r=['identb', 'tri'], w=[pzn])
                            ptc = PT[pti['i'] % 4]
                            ptn = f"MPT{pti['i'] % 4}"
                            pti['i'] += 1
                            act(ptc[:, cs], pz[:, cs], AF.Exp, r=[pzn], w=[ptn], scale=SC_MLA)
                            def mlaB(kt=kt, qmin=qmin, ptc=ptc, ptn=ptn, c=c, ob=ob, obn=obn):
                                for qi in range(qmin, 4):
                                    mm(ob[:, qi * 128:qi * 128 + 65], ptc[:, qi * 128:(qi + 1) * 128], VH[:, kt, 0:65],
                                       (kt == 0 and qi == 0), kt == 8 * c + 1 + 2 * qi, r=[ptn, 'VH'], w=[obn], skip=True)
                            defer(mlaB)
                        def mlaF(c=c, h=h, ob=ob, obn=obn):
                            ts('dve', sm[:, 0:4], ob[:, 64:512:128], 1e-30, None, ALU.max, r=[obn], w=['msm0'])
                            P.op('dve', E('reciprocal', out=sm[:, 4:8], in_=sm[:, 0:4]), r=['msm0'], w=['msm1'])
                            for qi in range(4):
                                ts('dve', ostage[:, 4 * c + qi, (h % 2) * 64:(h % 2) * 64 + 64], ob[:, qi * 128:qi * 128 + 64],
                                   sm[:, 4 + qi:5 + qi], None, ALU.mult, r=[obn, 'msm1'], w=['mostage'])
                        defer(mlaF)
                    flush()
                    if h % 2 == 1:
                        if h == 1:
                            dump('mostage', ostage[:, 0:4, :], ['mostage'])
                        for i8 in range(4):
                            for j in range(8):
                                i = i8 * 8 + j
                                P.op('pe', E('transpose', out=psB[:, j * 128:(j + 1) * 128], in_=ostage[:, i, :],
                                                                           identity=identb[:]),
                                     r=['mostage', 'identb'], w=['psB'])
                            evac(oTm[:, h // 2, i8 * 1024:(i8 + 1) * 1024], psB[:, :], r=['psB'], w=['oTm'])
            P.barrier()

        zring['n'] = 4
        checkpoint('mla')
        def bcast_vec(col0):
            for j in range(8):
                pz, pzn = nextZ()
                src = col0(j)
                ts('dve', dg[:], ident[:], src, None, ALU.mult, r=['ident', 'modT', 'gl'], w=['dg'])
                mm(pz[:, 0:128], onesf[:], dg[:], True, True, r=['onesf', 'dg'], w=[pzn])
                evac(gbc[:, j * 128:(j + 1) * 128], pz[:, 0:128], r=[pzn], w=['gbc'])

        with ExitStack() as TS:
            def sbt(name, shape, dt=F32):
                return TS.enter_context(SB(name, shape, dt))
            wg = sbt("wg", [128, 8, 2048], BF16)
            won = sbt("won", [128, 4, 1024], BF16)
            wom = sbt("wom", [128, 4, 1024], BF16)
            wout = sbt("wout", [128, 8, 1024], BF16)
            xr = [sbt(f"txr{i}", [128, 1024]) for i in range(2)]
            xn = sbt("txn", [128, 1024])
            hTr = [sbt(f"thT{i}", [128, 8, 128], BF16) for i in range(2)]
            sgA = sbt("sgA", [128, 1024])
            sgB = sbt("sgB", [128, 1024])
            m1 = sbt("m1", [128, 1024])
            m2 = sbt("m2", [128, 1024])
            yT = sbt("yT", [128, 8, 128], BF16)
            yb = sbt("yb", [128, 1024], BF16)
            x1r = [sbt(f"x1r{i}", [128, 1024]) for i in range(2)]
            wiv = kparts(I['w_in'])
            dma(wg[:, :, 0:1024], wiv[:, :, C_GA:C_GA + 1024], w=['wg'], q='pool')
            dma(wg[:, :, 1024:2048], wiv[:, :, C_GB:C_GB + 1024], w=['wg'], q='pool')
            dma(won[:], kparts(I['w_o_nsa']), w=['won'], q='pool')
            dma(wom[:], kparts(I['w_o_mla']), w=['wom'], q='pool')
            dma(wout[:], kparts(I['w_out']), w=['wout'], q='pool')
            bcast_vec(lambda j: modT[:, 16 + j:17 + j])
            m3 = sbt("m3", [128, 1024])
            yT2 = sbt("yT2", [128, 8, 128], BF16)
            xr3 = sbt("txr2", [128, 1024])
            xr = xr + [xr3]
            yTs = [yT, yT2]

            xr4 = sbt("txr3", [128, 1024])
            xr = xr + [xr4]
            xn2 = sbt("txn2", [128, 1024])
            xnr = [xn, xn2]

            def t1_norm(i2):
                s4 = i2 % 4
                dma(xr[s4][:], I['xo'][i2 * 128:(i2 + 1) * 128, :], w=[f"txr{s4}"])
                hT_norm(xr[s4], f"txr{s4}", xnr[i2 % 2], f"txn{i2 % 2}", i2 % 4)

            def t1_tr_post(i2):
                hT_tr(xnr[i2 % 2], f"txn{i2 % 2}")
                hT_post(hTr[i2 % 2], f"thT{i2 % 2}", gsA, shA)

            def t1_A(i):
                isl = slice(i * 128, (i + 1) * 128)
                hT = hTr[i % 2]
                hres = f"thT{i % 2}"
                for n in range(2):
                    pz, pzn = nextZ()
                    for k in range(8):
                        mm(pz[:, :], hT[:, k, :], wg[:, k, n * 512:(n + 1) * 512], k == 0, k == 7, r=['wg', hres], w=[pzn])
                    act(sgA[:, n * 512:(n + 1) * 512], pz[:, :], AF.Sigmoid, r=[pzn], w=['sgA'])
                for n in range(2):
                    pz, pzn = nextZ()
                    for k in range(4):
                        mm(pz[:, :], oTn[:, k, isl], won[:, k, n * 512:(n + 1) * 512], k == 0, k == 3, r=['won', 'oTn'], w=[pzn])
                    tt('dve', m1[:, n * 512:(n + 1) * 512], pz[:, :], sgA[:, n * 512:(n + 1) * 512], ALU.mult, r=[pzn, 'sgA'], w=['m1'])
                for n in range(2):
                    pz, pzn = nextZ()
                    for k in range(8):
                        mm(pz[:, :], hT[:, k, :], wg[:, k, 1024 + n * 512:1024 + (n + 1) * 512], k == 0, k == 7, r=['wg', hres], w=[pzn])
                    act(sgB[:, n * 512:(n + 1) * 512], pz[:, :], AF.Sigmoid, r=[pzn], w=['sgB'])
                if i + 1 < NO:
                    t1_tr_post(i + 1)
                if i + 2 < NO:
                    t1_norm(i + 2)
                for n in range(2):
                    pz, pzn = nextZ()
                    for k in range(4):
                        mm(pz[:, :], oTm[:, k, isl], wom[:, k, n * 512:(n + 1) * 512], k == 0, k == 3, r=['wom', 'oTm'], w=[pzn])
                    tt('dve', m2[:, n * 512:(n + 1) * 512], pz[:, :], sgB[:, n * 512:(n + 1) * 512], ALU.mult, r=[pzn, 'sgB'], w=['m2'])
                tt('dve', yb[:], m1[:], m2[:], ALU.add, r=['m1', 'm2'], w=['yb'])
                for k in range(8):
                    P.op('pe', E('transpose', out=psB[:, k * 128:(k + 1) * 128], in_=yb[:, k * 128:(k + 1) * 128],
                                 identity=identb[:]), r=['yb', 'identb'], w=['psB'])
                yTc = yTs[i % 2]
                yv = yTc[:].rearrange("p k t -> p (k t)")
                act(yv[:, 0:512], psB[:, 0:512], AF.Copy, r=['psB'], w=[f'yT{i % 2}'])
                ts('dve', yv[:, 512:1024], psB[:, 512:1024], 1.0, None, ALU.mult, r=['psB'], w=[f'yT{i % 2}'])

            def t1_B(i):
                isl = slice(i * 128, (i + 1) * 128)
                yTc = yTs[i % 2]
                xt = xr[i % 4]
                x1 = x1r[i % 2]
                x1n = f"x1r{i % 2}"
                for n in range(2):
                    pz, pzn = nextZ()
                    for k in range(8):
                        mm(pz[:, :], yTc[:, k, :], wout[:, k, n * 512:(n + 1) * 512], k == 0, k == 7, r=['wout', f'yT{i % 2}'], w=[pzn])
                    tt('dve', m3[:, n * 512:(n + 1) * 512], pz[:, :], gbc[:, n * 512:(n + 1) * 512], ALU.mult, r=[pzn, 'gbc'], w=['m3'])
                tt('dve', x1[:], m3[:], xt[:], ALU.add, r=['m3', f"txr{i % 4}"], w=[x1n])
                dma(x1s[isl, :], x1[:], r=[x1n], w=['x1s'])
                if i == 0:
                    dump('x1', x1[:], [x1n])

            t1_norm(0)
            t1_tr_post(0)
            t1_norm(1)
            for i in range(NO):
                t1_A(i)
                if i >= 1:
                    t1_B(i - 1)
            t1_B(NO - 1)
        P.barrier()
        OS.close()
        checkpoint('t1')

        with ExitStack() as TS:
            def sbt(name, shape, dt=F32):
                return TS.enter_context(SB(name, shape, dt))
            wf1 = sbt("wf1", [128, 8, 4096], BF16)
            wf2 = sbt("wf2", [128, 32, 1024], BF16)
            gfb = sbt("gfb", [128, 1024])
            xr = [sbt(f"uxr{i}", [128, 1024]) for i in range(2)]
            xn = sbt("uxn", [128, 1024])
            hTr = [sbt(f"uhT{i}", [128, 8, 128], BF16) for i in range(2)]
            aT = [sbt(f"aT{i}", [128, 32, 128], BF16) for i in range(2)]
            rl = [sbt(f"rl{i}", [128, 512]) for i in range(2)]
            o2 = sbt("o2", [128, 1024])
            tmp = sbt("utmp", [128, 1024])
            res = [sbt(f"ures{i}", [128, 1024]) for i in range(2)]
            fs = sbt("fs", [128, 4])
            wf1v = kparts(I['w_fc1'])
            wf2v = kparts(I['w_fc2'])
            for q4 in range(4):
                dma(wf1[:, :, q4 * 1024:(q4 + 1) * 1024], wf1v[:, :, q4 * 1024:(q4 + 1) * 1024], w=[f'wf1_{q4}'], q='pool')
            for q4 in range(4):
                dma(wf2[:, q4 * 8:(q4 + 1) * 8, :], wf2v[:, q4 * 8:(q4 + 1) * 8, :], w=[f'wf2_{q4}'], q='pool')
            bcast_vec(lambda j: gl[:, 21 + j:22 + j])
            P.op('pool', E('tensor_copy', out=gfb[:], in_=gbc[:]), r=['gbc'], w=['gfb'])
            bcast_vec(lambda j: modT[:, 40 + j:41 + j])
            xr = xr + [sbt("uxr2", [128, 1024])]
            xnr = [xn, sbt("uxn2", [128, 1024])]

            def t2_norm(i2):
                s3 = i2 % 3
                dma(xr[s3][:], x1s[i2 * 128:(i2 + 1) * 128, :], r=['x1s'], w=[f"uxr{s3}"])
                hT_norm(xr[s3], f"uxr{s3}", xnr[i2 % 2], f"uxn{i2 % 2}", i2 % 4)

            t2_norm(0)
            hT_tr(xnr[0], "uxn0")
            hT_post(hTr[0], "uhT0", gsM, shM)
            t2_norm(1)
            for i in range(NO):
                sl = i % 2
                xt = xr[i % 3]
                xres = f"uxr{i % 3}"
                isl = slice(i * 128, (i + 1) * 128)
                hT = hTr[sl]
                hres = f"uhT{sl}"
                a = aT[sl]
                an = f"aT{sl}"
                for jg in range(8):
                    pz, pzn = (psZ[jg % 2], f"psZ{jg % 2}")
                    for jj in range(4):
                        j = jg * 4 + jj
                        for k in range(8):
                            mm(pz[:, jj * 128:(jj + 1) * 128], wf1[:, k, j * 128:(j + 1) * 128], hT[:, k, :], k == 0, k == 7,
                               r=[f'wf1_{jg // 2}', hres], w=[pzn])
                    rb = rl[jg % 2]
                    rbn = f"rl{jg % 2}"
                    act(rb[:], pz[:, :], AF.Relu, r=[pzn], w=[rbn])
                    av = a[:, jg * 4:(jg + 1) * 4, :].rearrange("p a b -> p (a b)")
                    tt('pool', av, rb[:], rb[:], ALU.mult, r=[rbn], w=[an])
                if i + 1 < NO:
                    hT_tr(xnr[(i + 1) % 2], f"uxn{(i + 1) % 2}")
                for n in range(2):
                    pz, pzn = (psZ[2 + n], f"psZ{2 + n}")
                    for j in range(32):
                        mm(pz[:, :], a[:, j, :], wf2[:, j, n * 512:(n + 1) * 512], j == 0, j == 31, r=[f'wf2_{j // 8}', an], w=[pzn])
                if i + 1 < NO:
                    hT_post(hTr[(i + 1) % 2], f"uhT{(i + 1) % 2}", gsM, shM)
                if i + 2 < NO:
                    t2_norm(i + 2)
                for n in range(2):
                    pz, pzn = (psZ[2 + n], f"psZ{2 + n}")
                    tt('dve', tmp[:, n * 512:(n + 1) * 512], pz[:, :], gbc[:, n * 512:(n + 1) * 512], ALU.mult, r=[pzn, 'gbc'], w=['utmp'])
                tt('dve', o2[:], tmp[:], xt[:], ALU.add, r=['utmp', xres], w=['o2'])
                act(junk[:], o2[:], AF.Square, r=['o2'], w=['junk', 'fs0'], accum_out=fs[:, 0:1])
                act(fs[:, 1:2], fs[:, 0:1], AF.Sqrt, r=['fs0', 'epsb'], w=['fs1'], scale=1.0 / D, bias=epsb[:, 0:1])
                P.op('dve', E('reciprocal', out=fs[:, 2:3], in_=fs[:, 1:2]), r=['fs1'], w=['fs2'])
                rs_ = res[sl]
                rn = f"ures{sl}"
                stt('dve', rs_[:], o2[:], fs[:, 2:3], gfb[:], ALU.mult, ALU.mult, r=['o2', 'fs2', 'gfb'], w=[rn])
                dma(out[isl, :], rs_[:], r=[rn], w=['out'])
    return nc


_CACHE = {}


def _lay(v, k):
    return np.ascontiguousarray(np.asarray(v, np.float32).reshape(k, 128).T)


def make_in_maps(inputs, ncores=8):
    x = np.asarray(inputs['x'], np.float32)
    shared = {
        'w_ada': np.ascontiguousarray(inputs['w_ada'][0], dtype=np.float32),
        'bada_l': _lay(inputs['b_ada'][0], 48),
        'gmix_l': _lay(inputs['g_mix'][0], 8), 'gmlp_l': _lay(inputs['g_mlp'][0], 8),
        'gcq_l': _lay(inputs['g_cq'][0], 3), 'gckv_l': _lay(inputs['g_ckv'][0], 2),
        'gfin_l': _lay(inputs['g_final'], 8),
        'w_in': np.ascontiguousarray(inputs['w_in'][0], dtype=np.float32),
        'peck_t': np.ascontiguousarray(np.asarray(inputs['pe_ck'][0], np.float32).T),
        'w_ck1': np.ascontiguousarray(inputs['w_ck1'][0], dtype=np.float32),
        'w_ck2': np.ascontiguousarray(inputs['w_ck2'][0], dtype=np.float32),
        'pecv_t': np.ascontiguousarray(np.asarray(inputs['pe_cv'][0], np.float32).T),
        'w_cv1': np.ascontiguousarray(inputs['w_cv1'][0], dtype=np.float32),
        'w_cv2': np.ascontiguousarray(inputs['w_cv2'][0], dtype=np.float32),
        'w_uq': np.ascontiguousarray(inputs['w_uq'][0], dtype=np.float32),
        'w_uk': np.ascontiguousarray(inputs['w_uk'][0], dtype=np.float32),
        'w_uv': np.ascontiguousarray(inputs['w_uv'][0], dtype=np.float32),
        'w_o_nsa': np.ascontiguousarray(inputs['w_o_nsa'][0], dtype=np.float32),
        'w_o_mla': np.ascontiguousarray(inputs['w_o_mla'][0], dtype=np.float32),
        'w_out': np.ascontiguousarray(inputs['w_out'][0], dtype=np.float32),
        'w_fc1': np.ascontiguousarray(inputs['w_fc1'][0], dtype=np.float32),
        'w_fc2': np.ascontiguousarray(inputs['w_fc2'][0], dtype=np.float32),
    }
    consts = [host_consts(0), host_consts(1)]
    in_maps = []
    for core in range(ncores):
        b, p = core // 2, core % 2
        if p == 1:
            xl = np.ascontiguousarray(x[b])
        else:
            xl = np.concatenate([np.zeros((128, D), np.float32), x[b, :S - 128]], axis=0)
        xo = np.ascontiguousarray(xl.reshape(NT, 128, D)[1::2].reshape(NO * 128, D))
        m = dict(shared)
        m['xl'] = xl
        m['xo'] = xo
        m['c_l'] = _lay(np.asarray(inputs['c'], np.float32)[b], 8)
        m.update(consts[p])
        in_maps.append(m)
    return in_maps


def assemble(results, ncores=8):
    outp = np.zeros((4, S, D), np.float32)
    for core in range(ncores):
        b, p = core // 2, core % 2
        o = np.asarray(results[core]['out'], np.float32).reshape(NO, 128, D)
        outp[b].reshape(NT, 128, D)[p::2] = o
    return outp


def kernel(**inputs):
    if 'nc' not in _CACHE:
        _CACHE['nc'] = build_program()
    nc = _CACHE['nc']
    in_maps = make_in_maps(inputs)
    res = run_bass_kernel_spmd(nc, in_maps, core_ids=list(range(8)))
    return assemble(res.results)
```
